# Optimizing a Trainium2 kernel written in Bass

```python
import math
import jax, jax.numpy as jnp
from jax import lax
import numpy as np

D_MODEL = 1024
BATCH = 1
SEQ = 16384
DEPTH = 2
DEC_BATCH = 16
DEC_SEQ = 16
PAST_LEN = 2048

CHUNK = 64
N_META = 16
N_A_LAYERS = DEPTH // 2
N_B_LAYERS = DEPTH - N_A_LAYERS
SSM_GROUP = 16
SSM_GROUPS = D_MODEL // SSM_GROUP
SSM_STATE = 64
DT_MIN = 0.001
DT_MAX = 0.1
N_HEADS = 16
HEAD_DIM = D_MODEL // N_HEADS
ATTN_SCALE = HEAD_DIM ** -0.5
Q_BLOCK = 128
PEER_HEADS = 8
PEER_NKEYS = 128
PEER_EXPERTS = PEER_NKEYS * PEER_NKEYS
PEER_TOPK = 16
PEER_DKEY = 256
PEER_HALF = PEER_DKEY // 2
PEER_BLOCK = 256
LN_EPS = 1e-5
DN_ALPHA = (2.0 * DEPTH) ** 0.25
DN_BETA = (8.0 * DEPTH) ** -0.25

kernel_name = "s5_fox_peer_yoco_stream_step"


def layer_norm(x, g, b):
    xf = x.astype(jnp.float32)
    mu = jnp.mean(xf, axis=-1, keepdims=True)
    var = jnp.mean(jnp.square(xf - mu), axis=-1, keepdims=True)
    return ((xf - mu) * lax.rsqrt(var + LN_EPS) * g.astype(jnp.float32) + b.astype(jnp.float32)).astype(x.dtype)


def ssm_discretise(lam_re, lam_im, log_dt, b_re, b_im):
    lam_re = lam_re.astype(jnp.float32)
    lam_im = lam_im.astype(jnp.float32)
    dt = jnp.exp(log_dt.astype(jnp.float32))[:, None]
    mag = jnp.exp(lam_re * dt)
    ab_re = mag * jnp.cos(lam_im * dt)
    ab_im = mag * jnp.sin(lam_im * dt)
    n_re = ab_re - 1.0
    den = lam_re * lam_re + lam_im * lam_im
    f_re = (n_re * lam_re + ab_im * lam_im) / den
    f_im = (ab_im * lam_re - n_re * lam_im) / den
    b_re = b_re.astype(jnp.float32)
    b_im = b_im.astype(jnp.float32)
    bb_re = f_re[..., None] * b_re - f_im[..., None] * b_im
    bb_im = f_re[..., None] * b_im + f_im[..., None] * b_re
    return ab_re, ab_im, bb_re, bb_im


def _ssm_combine(e1, e2):
    a1r, a1i, b1r, b1i = e1
    a2r, a2i, b2r, b2i = e2
    return (a2r * a1r - a2i * a1i, a2r * a1i + a2i * a1r,
            a2r * b1r - a2i * b1i + b2r, a2r * b1i + a2i * b1r + b2i)


def ssm_block(u, h_re, h_im, ab_re, ab_im, bb_re, bb_im, c_re, c_im, d_skip):
    bu_re = jnp.einsum("btgc,gpc->btgp", u, bb_re)
    bu_im = jnp.einsum("btgc,gpc->btgp", u, bb_im)
    bu_re = bu_re.at[:, 0].add(ab_re * h_re - ab_im * h_im)
    bu_im = bu_im.at[:, 0].add(ab_re * h_im + ab_im * h_re)
    a_re = jnp.broadcast_to(ab_re, bu_re.shape)
    a_im = jnp.broadcast_to(ab_im, bu_im.shape)
    _, _, s_re, s_im = lax.associative_scan(_ssm_combine, (a_re, a_im, bu_re, bu_im), axis=1)
    y = (jnp.einsum("btgp,gcp->btgc", s_re, c_re) - jnp.einsum("btgp,gcp->btgc", s_im, c_im)
         + d_skip * u)
    return y, s_re[:, -1], s_im[:, -1]


def s5_mixer(x, h_re, h_im, lam_re, lam_im, log_dt, b_re, b_im, c_re, c_im, d_skip, w_glu, block):
    Bsz, L, _ = x.shape
    pad = (-L) % block
    nc = (L + pad) // block
    ab_re, ab_im, bb_re, bb_im = ssm_discretise(lam_re, lam_im, log_dt, b_re, b_im)
    c_re = c_re.astype(jnp.float32)
    c_im = c_im.astype(jnp.float32)
    d_skip = d_skip.astype(jnp.float32)
    u = jnp.pad(x.astype(jnp.float32), ((0, 0), (pad, 0), (0, 0)))
    u = u.reshape(Bsz, nc, block, SSM_GROUPS, SSM_GROUP).swapaxes(0, 1)

    def step(carry, u_c):
        y_c, hr, hi = ssm_block(u_c, carry[0], carry[1], ab_re, ab_im, bb_re, bb_im, c_re, c_im, d_skip)
        return (hr, hi), y_c

    (hr, hi), ys = lax.scan(step, (h_re.astype(jnp.float32), h_im.astype(jnp.float32)), u)
    y = ys.swapaxes(0, 1).reshape(Bsz, nc * block, D_MODEL)[:, pad:]
    y = jax.nn.gelu(y).astype(x.dtype)
    z = y @ w_glu
    out = z[..., :D_MODEL] * jax.nn.sigmoid(z[..., D_MODEL:])
    return out, hr, hi


def shared_kv(x, w_kvf, b_f):
    Bsz, T, _ = x.shape
    z = x @ w_kvf
    k = z[..., :D_MODEL].reshape(Bsz, T, N_HEADS, HEAD_DIM)
    v = z[..., D_MODEL:2 * D_MODEL].reshape(Bsz, T, N_HEADS, HEAD_DIM)
    logf = jax.nn.log_sigmoid((z[..., 2 * D_MODEL:] + b_f).astype(jnp.float32))
    return k, v, logf


def fox_attend(q, k, v, cq, ck, qpos, kpos):
    Bsz, Tq = q.shape[0], q.shape[1]
    blk = min(Q_BLOCK, Tq)
    pad = (-Tq) % blk
    nb = (Tq + pad) // blk

    def blocks(a):
        a = jnp.pad(a, [(0, 0), (0, pad)] + [(0, 0)] * (a.ndim - 2), mode="edge")
        return a.reshape((Bsz, nb, blk) + a.shape[2:]).swapaxes(0, 1)

    q_b = blocks(q)
    cq_b = blocks(cq)
    qpos_b = jnp.pad(qpos, (0, pad), mode="edge").reshape(nb, blk)
    ck_t = ck.swapaxes(1, 2)

    def one_block(args):
        q_i, cq_i, qpos_i = args
        s = jnp.einsum("bqhd,bkhd->bhqk", q_i, k, preferred_element_type=jnp.float32) * ATTN_SCALE
        s = s + cq_i.swapaxes(1, 2)[..., None] - ck_t[:, :, None, :]
        s = jnp.where(kpos[None, :] <= qpos_i[:, None], s, -jnp.inf)
        p = jax.nn.softmax(s, axis=-1)
        return jnp.einsum("bhqk,bkhd->bqhd", p.astype(v.dtype), v)

    o = lax.map(one_block, (q_b, cq_b, qpos_b))
    return o.swapaxes(0, 1).reshape(Bsz, nb * blk, N_HEADS, HEAD_DIM)[:, :Tq]


def fox_mixer(x, k, v, ck, w_q, w_o):
    Bsz, Tq, _ = x.shape
    Tk = k.shape[1]
    q = (x @ w_q).reshape(Bsz, Tq, N_HEADS, HEAD_DIM)
    kpos = jnp.arange(Tk, dtype=jnp.int32)
    o = fox_attend(q, k, v, ck[:, Tk - Tq:], ck, kpos[Tk - Tq:], kpos)
    return o.reshape(Bsz, Tq, D_MODEL) @ w_o


def peer_ffn(x, w_pq, sub_k1, sub_k2, u_tab, v_tab):
    Bsz, T, _ = x.shape
    n_tok = Bsz * T
    blk = min(PEER_BLOCK, n_tok)
    pad = (-n_tok) % blk
    xb = jnp.pad(x.reshape(n_tok, D_MODEL), ((0, pad), (0, 0))).reshape(-1, blk, D_MODEL)
    k1 = sub_k1.astype(jnp.float32)
    k2 = sub_k2.astype(jnp.float32)

    def one_block(xi):
        q = (xi @ w_pq).astype(jnp.float32).reshape(blk, PEER_HEADS, 2, PEER_HALF)
        s1 = jnp.einsum("nhc,kc->nhk", q[:, :, 0], k1)
        s2 = jnp.einsum("nhc,kc->nhk", q[:, :, 1], k2)
        v1, i1 = lax.top_k(s1, PEER_TOPK)
        v2, i2 = lax.top_k(s2, PEER_TOPK)
        cand = (v1[..., :, None] + v2[..., None, :]).reshape(blk, PEER_HEADS, PEER_TOPK * PEER_TOPK)
        cidx = (i1[..., :, None] * PEER_NKEYS + i2[..., None, :]).reshape(blk, PEER_HEADS, PEER_TOPK * PEER_TOPK)
        sc, j = lax.top_k(cand, PEER_TOPK)
        idx = jnp.take_along_axis(cidx, j, axis=-1)
        g = jax.nn.softmax(sc, axis=-1)
        u = jnp.take(u_tab, idx, axis=0)
        act = jax.nn.gelu(jnp.einsum("nhkd,nd->nhk", u, xi, preferred_element_type=jnp.float32))
        w = (g * act).astype(xi.dtype)
        vv = jnp.take(v_tab, idx, axis=0)
        return jnp.einsum("nhk,nhkd->nd", w, vv)

    out = lax.map(one_block, xb).reshape(-1, D_MODEL)[:n_tok]
    return out.reshape(Bsz, T, D_MODEL)


def trunk(x, h_re, h_im, k_past, v_past, lf_past, ssm_blk,
          ssm_lam_re, ssm_lam_im, ssm_log_dt, ssm_b_re, ssm_b_im, ssm_c_re, ssm_c_im, ssm_d, w_glu,
          w_kvf, b_f, w_q, w_o, peer_w_q, peer_k1, peer_k2, peer_u, peer_v,
          ln1_g, ln1_b, ln2_g, ln2_b):
    new_re, new_im = [], []
    k_new = v_new = lf_new = k_all = v_all = ck = None
    for layer in range(DEPTH):
        if layer < N_A_LAYERS:
            mix, hr, hi = s5_mixer(x, h_re[:, layer], h_im[:, layer], ssm_lam_re[layer], ssm_lam_im[layer],
                                   ssm_log_dt[layer], ssm_b_re[layer], ssm_b_im[layer], ssm_c_re[layer],
                                   ssm_c_im[layer], ssm_d[layer], w_glu[layer], ssm_blk)
            new_re.append(hr)
            new_im.append(hi)
        else:
            if layer == N_A_LAYERS:
                k_new, v_new, lf_new = shared_kv(x, w_kvf, b_f)
                if k_past is None:
                    k_all, v_all, lf_all = k_new, v_new, lf_new
                else:
                    k_all = jnp.concatenate([k_past.astype(k_new.dtype), k_new], axis=1)
                    v_all = jnp.concatenate([v_past.astype(v_new.dtype), v_new], axis=1)
                    lf_all = jnp.concatenate([lf_past.astype(jnp.float32), lf_new], axis=1)
                ck = jnp.cumsum(lf_all, axis=1)
            jb = layer - N_A_LAYERS
            mix = fox_mixer(x, k_all, v_all, ck, w_q[jb], w_o[jb])
        x = layer_norm(DN_ALPHA * x + mix, ln1_g[layer], ln1_b[layer])
        ffn = peer_ffn(x, peer_w_q[layer], peer_k1[layer], peer_k2[layer], peer_u[layer], peer_v[layer])
        x = layer_norm(DN_ALPHA * x + ffn, ln2_g[layer], ln2_b[layer])
    return x, jnp.stack(new_re, axis=1), jnp.stack(new_im, axis=1), k_new, v_new, lf_new


def setup_inputs(seed: int = 0) -> dict:
    key = jax.random.key(seed)
    ks = jax.random.split(key, 32)
    f32 = jnp.float32

    def nrm(k, shape, scale):
        return jax.random.normal(k, shape, f32) * scale

    n_idx = jnp.arange(SSM_STATE, dtype=f32)
    lam_re = -0.5 + nrm(ks[8], (N_A_LAYERS, SSM_GROUPS, SSM_STATE), 0.01)
    lam_im = math.pi * n_idx + nrm(ks[9], (N_A_LAYERS, SSM_GROUPS, SSM_STATE), 0.01)
    return {
        "x_prompt": nrm(ks[0], (BATCH, SEQ, D_MODEL), 1.0),
        "x_sample": nrm(ks[1], (DEC_BATCH, DEC_SEQ, D_MODEL), 1.0),
        "state_ssm_re": nrm(ks[2], (DEC_BATCH, N_A_LAYERS, SSM_GROUPS, SSM_STATE), 0.3),
        "state_ssm_im": nrm(ks[3], (DEC_BATCH, N_A_LAYERS, SSM_GROUPS, SSM_STATE), 0.3),
        "cache_k": nrm(ks[4], (DEC_BATCH, PAST_LEN, N_HEADS, HEAD_DIM), 1.0),
        "cache_v": nrm(ks[5], (DEC_BATCH, PAST_LEN, N_HEADS, HEAD_DIM), 1.0),
        "cache_logf": jax.nn.log_sigmoid(2.0 + nrm(ks[6], (DEC_BATCH, PAST_LEN, N_HEADS), 1.0)),
        "meta_tokens": nrm(ks[7], (N_META, D_MODEL), 1.0),
        "ssm_lam_re": lam_re,
        "ssm_lam_im": lam_im,
        "ssm_log_dt": jax.random.uniform(ks[10], (N_A_LAYERS, SSM_GROUPS), f32,
                                         minval=math.log(DT_MIN), maxval=math.log(DT_MAX)),
        "ssm_b_re": nrm(ks[11], (N_A_LAYERS, SSM_GROUPS, SSM_STATE, SSM_GROUP), (2 * SSM_GROUP) ** -0.5),
        "ssm_b_im": nrm(ks[12], (N_A_LAYERS, SSM_GROUPS, SSM_STATE, SSM_GROUP), (2 * SSM_GROUP) ** -0.5),
        "ssm_c_re": nrm(ks[13], (N_A_LAYERS, SSM_GROUPS, SSM_GROUP, SSM_STATE), SSM_STATE ** -0.5),
        "ssm_c_im": nrm(ks[14], (N_A_LAYERS, SSM_GROUPS, SSM_GROUP, SSM_STATE), SSM_STATE ** -0.5),
        "ssm_d": nrm(ks[15], (N_A_LAYERS, SSM_GROUPS, SSM_GROUP), 1.0),
        "w_glu": nrm(ks[16], (N_A_LAYERS, D_MODEL, 2 * D_MODEL), DN_BETA * D_MODEL ** -0.5),
        "w_kvf": nrm(ks[17], (D_MODEL, 2 * D_MODEL + N_HEADS), D_MODEL ** -0.5),
        "b_f": 2.0 + nrm(ks[18], (N_HEADS,), 0.1),
        "w_q": nrm(ks[19], (N_B_LAYERS, D_MODEL, D_MODEL), D_MODEL ** -0.5),
        "w_o": nrm(ks[20], (N_B_LAYERS, D_MODEL, D_MODEL), DN_BETA * D_MODEL ** -0.5),
        "peer_w_q": nrm(ks[21], (DEPTH, D_MODEL, PEER_HEADS * PEER_DKEY), D_MODEL ** -0.5),
        "peer_k1": nrm(ks[22], (DEPTH, PEER_NKEYS, PEER_HALF), PEER_HALF ** -0.5),
        "peer_k2": nrm(ks[23], (DEPTH, PEER_NKEYS, PEER_HALF), PEER_HALF ** -0.5),
        "peer_u": nrm(ks[24], (DEPTH, PEER_EXPERTS, D_MODEL), D_MODEL ** -0.5),
        "peer_v": nrm(ks[25], (DEPTH, PEER_EXPERTS, D_MODEL), DN_BETA * PEER_HEADS ** -0.5),
        "ln1_g": 1.0 + nrm(ks[26], (DEPTH, D_MODEL), 0.02),
        "ln1_b": nrm(ks[27], (DEPTH, D_MODEL), 0.02),
        "ln2_g": 1.0 + nrm(ks[28], (DEPTH, D_MODEL), 0.02),
        "ln2_b": nrm(ks[29], (DEPTH, D_MODEL), 0.02),
    }


def reference(x_prompt, x_sample, state_ssm_re, state_ssm_im, cache_k, cache_v, cache_logf, meta_tokens,
              ssm_lam_re, ssm_lam_im, ssm_log_dt, ssm_b_re, ssm_b_im, ssm_c_re, ssm_c_im, ssm_d, w_glu,
              w_kvf, b_f, w_q, w_o, peer_w_q, peer_k1, peer_k2, peer_u, peer_v,
              ln1_g, ln1_b, ln2_g, ln2_b):
    weights = (ssm_lam_re, ssm_lam_im, ssm_log_dt, ssm_b_re, ssm_b_im, ssm_c_re, ssm_c_im, ssm_d, w_glu,
               w_kvf, b_f, w_q, w_o, peer_w_q, peer_k1, peer_k2, peer_u, peer_v,
               ln1_g, ln1_b, ln2_g, ln2_b)
    bp = x_prompt.shape[0]
    meta = jnp.broadcast_to(meta_tokens.astype(x_prompt.dtype)[None], (bp, N_META, D_MODEL))
    xp = jnp.concatenate([meta, x_prompt], axis=1)
    h0 = jnp.zeros((bp, N_A_LAYERS, SSM_GROUPS, SSM_STATE), jnp.float32)
    yp, ssm_re_p, ssm_im_p, k_p, v_p, logf_p = trunk(xp, h0, h0, None, None, None, CHUNK, *weights)
    y_prompt = yp[:, N_META:]
    y_sample, ssm_re_s, ssm_im_s, k_s, v_s, logf_s = trunk(
        x_sample, state_ssm_re, state_ssm_im, cache_k, cache_v, cache_logf, x_sample.shape[1], *weights)
    return (y_prompt, y_sample, ssm_re_p, ssm_im_p, k_p, v_p, logf_p, ssm_re_s, ssm_im_s, k_s, v_s, logf_s)
```

```python
import contextlib
import math
import os
import numpy as np
import concourse.bass as bass
import concourse.mybir as mybir
from concourse.bass_utils import run_bass_kernel_spmd

F32 = mybir.dt.float32
BF16 = mybir.dt.bfloat16
I32 = mybir.dt.int32
U32 = mybir.dt.uint32
AF = mybir.ActivationFunctionType
ALU = mybir.AluOpType
AX = mybir.AxisListType
NCORES = 8
D = 1024
NTL = 17
TOK = NTL * 128
TWO_PI = 2.0 * math.pi
DN_ALPHA = 4.0 ** 0.25
LN_EPS = 1e-5
NEG = -1.0e30
V = "vector"; S = "scalar"; G = "gpsimd"; T = "tensor"; Q = "sync"


ENGS = ["tensor", "vector", "scalar", "gpsimd", "sync"]


class Buf:
    __slots__ = ("name", "writer", "readers")

    def __init__(self, name):
        self.name = name
        self.writer = None
        self.readers = []


class Op:
    __slots__ = ("eng", "fn", "deps", "kind", "key", "sig", "idx")

    def __init__(self, eng, fn, kind, key):
        self.eng = eng
        self.fn = fn
        self.kind = kind
        self.key = key
        self.deps = set()
        self.sig = None


class Prog:
    def __init__(self, nc, same_engine_sync=True):
        self.nc = nc
        self.ops = []
        self.bufs = {}
        self.same_engine_sync = same_engine_sync

    def buf(self, name):
        b = self.bufs.get(name)
        if b is None:
            b = self.bufs[name] = Buf(name)
        return b

    def _bl(self, lst):
        out = []
        for x in lst or []:
            out.append(self.buf(x) if isinstance(x, str) else x)
        return out

    def _add(self, op, reads, writes, is_barrier=False):
        i = len(self.ops)
        op.idx = i
        reads = list(reads or [])
        writes = list(writes or [])
        if is_barrier:
            writes.append("__ALL__")
        else:
            reads.append("__ALL__")
        for b in self._bl(reads):
            if b.writer is not None:
                op.deps.add(b.writer)
            b.readers.append(i)
        for b in self._bl(writes):
            if b.writer is not None:
                op.deps.add(b.writer)
            for r in b.readers:
                if r != i:
                    op.deps.add(r)
            b.writer = i
            b.readers = []
        self.ops.append(op)
        return i

    def op(self, eng, fn, reads=(), writes=()):
        return self._add(Op(eng, fn, "c", None), reads, writes)

    def dma(self, eng, out, in_, reads=(), writes=(), key=None):
        ws = self._bl(writes)
        rs = self._bl(reads)
        if key is None:
            key = (ws[0].name if ws else rs[0].name)
        return self._add(Op(eng, lambda e: e.dma_start(out=out, in_=in_), "d", key), rs, ws)

    def dma_fn(self, eng, fn, reads=(), writes=(), key=None):
        ws = self._bl(writes)
        rs = self._bl(reads)
        if key is None:
            key = (ws[0].name if ws else rs[0].name)
        return self._add(Op(eng, fn, "d", key), rs, ws)

    def barrier(self):
        return self._add(Op("vector", self.barrier_fn, "c", None), [], [], is_barrier=True)

    def cc(self, fn, reads=(), writes=(), key="cc"):
        return self._add(Op("gpsimd", fn, "cc", key), reads, writes)

    def finalize(self, final_wait_bufs=()):
        nc = self.nc
        ops = self.ops
        needed = [False] * len(ops)
        for o in ops:
            for d in o.deps:
                if ops[d].kind == "c" and o.kind == "c" and ops[d].eng == o.eng and \
                        (o.eng == "tensor" or not self.same_engine_sync):
                    continue
                needed[d] = True
        cnt = {}
        rot = {}
        dslot = {}
        NDS = 28
        NROT = 8
        semnames = set()
        for i, o in enumerate(ops):
            if o.kind == "c":
                if not needed[i]:
                    continue
                rk = rot.get(o.eng, 0)
                rot[o.eng] = rk + 1
                sn = "e_" + o.eng + str(rk % NROT)
                cnt[sn] = cnt.get(sn, 0) + 1
                o.sig = (sn, cnt[sn], 1)
            elif o.kind == "d":
                kk = (o.eng, str(o.key))
                if kk not in dslot:
                    n_e = sum(1 for q in dslot if q[0] == o.eng)
                    dslot[kk] = n_e % (NDS if o.eng == "sync" else 4)
                sn = "d_" + o.eng + str(dslot[kk])
                cnt[sn] = cnt.get(sn, 0) + 16
                o.sig = (sn, cnt[sn], 16)
            else:
                sn = "c_" + str(o.key)
                cnt[sn] = cnt.get(sn, 0) + 1
                o.sig = (sn, cnt[sn], 1)
            semnames.add(o.sig[0])
        self.sem_final = dict(cnt)
        semnames = sorted(semnames)
        self.nsems = len(semnames)
        import contextlib
        stack = contextlib.ExitStack()
        sems = {}
        for sn in semnames:
            sems[sn] = stack.enter_context(nc.semaphore(sn[:40]))
        per_eng = {e: [] for e in ENGS}
        for o in ops:
            per_eng[o.eng].append(o)
        self.n_waits = 0
        prog = self

        def body_for(engname):
            lst = per_eng[engname]

            def body(e):
                waited = {}
                for o in lst:
                    for d in sorted(o.deps):
                        po = ops[d]
                        if po.sig is None:
                            continue
                        sn, val, _ = po.sig
                        if po.kind == "c" and po.eng == engname and o.kind == "c" and \
                                (engname == "tensor" or not prog.same_engine_sync):
                            continue
                        if waited.get(sn, 0) >= val:
                            continue
                        waited[sn] = val
                        e.wait_ge(sems[sn], val)
                        prog.n_waits += 1
                    ins = o.fn(e)
                    if o.sig is not None:
                        if o.kind == "cc":
                            ins.then_inc(sems[o.sig[0]])
                        else:
                            ins.then_inc(sems[o.sig[0]], o.sig[2])
                fin = {}
                for o in lst:
                    if o.kind != "c":
                        fin[o.sig[0]] = max(fin.get(o.sig[0], 0), o.sig[1])
                for sn, val in fin.items():
                    if waited.get(sn, 0) < val:
                        e.wait_ge(sems[sn], val)
            return body

        with stack:
            with nc.Block() as block:
                for en in ENGS:
                    if per_eng[en]:
                        getattr(block, en)(body_for(en))


def build_program():
    nc = bass.Bass("TRN2", target_bir_lowering=False)
    STOP = os.environ.get("MK_STOP", "")
    NCH = int(os.environ.get("MK_NCH", "128"))

    def din(name, shape, dt=F32):
        return nc.dram_tensor(name, list(shape), dt, kind="ExternalInput").ap()

    def dout(name, shape, dt=F32):
        return nc.dram_tensor(name, list(shape), dt, kind="ExternalOutput").ap()

    def dint(name, shape, dt=F32):
        return nc.dram_tensor(name, list(shape), dt)

    xmisc = din("xmisc", [128, D]); xmain = din("xmain", [8 * 2048, D])
    NROW = 128 + 8 * 2048
    ident_d = din("ident", [128, 128]); tvec_d = din("tvec", [128, 128]); rowmask_d = din("rowmask", [128, 8])
    iota_d = din("iota128", [128, 128])
    lamre2_d = din("lamre2", [128, 32]); lamim2_d = din("lamim2", [128, 32]); logdt2_d = din("logdt2", [128, 32])
    lamre1_d = din("lamre1", [128, 512]); lamim1_d = din("lamim1", [128, 512]); logdt1_d = din("logdt1", [128, 512])
    bre1_d = din("bre1", [128, 512]); bim1_d = din("bim1", [128, 512])
    cre2_d = din("cre2", [128, 512]); cim2_d = din("cim2", [128, 512])
    dfm_d = din("dfm", [128, 8])
    st_re_d = din("st_re", [128, 64]); st_im_d = din("st_im", [128, 64])
    wglu_d = din("wglu", [128, 8 * 2048])
    lnv_d = din("lnv", [8, D])
    wpq_d = [din(f"wpq{l}", [128, 8 * 2048]) for l in range(2)]
    pk_d = [din(f"pk{l}", [128, 256]) for l in range(2)]
    SMALLTB = os.environ.get("MK_SMALLTB", "") == "1"
    TBR = 128 if SMALLTB else 128 * 128
    umy_d = [din(f"ufull{l}", [TBR, D]) for l in range(2)]
    vmy_d = [din(f"vfull{l}", [TBR, D]) for l in range(2)]
    NOCC = os.environ.get("MK_NOCC", "") == "1"
    wkvf_d = din("wkvf", [128, 8 * 2064])
    wq_d = din("wq", [128, 8 * 1024]); wo_d = din("wo", [128, 8 * 1024])
    utm_d = din("utm", [128, 128]); onesm_d = din("onesm", [128, 128]); mask16_d = din("mask16", [128, 1])
    amask_d = din("amask", [128, 4 * 512])
    pen_d = din("pen", [128, 9])
    idxown_d = din("idxown", [128, 16], I32)
    ckT_d = din("cache_kT", [2 * 16 * 64, 2048])
    cv_d = din("cache_v", [2 * 2048, D])
    clf_d = din("cache_lf", [2 * 2048, 16])
    bf_d = din("bf", [1, 16])

    o_ssm_p = dout("o_ssm_p", [128, 64])
    o_ssm_s = dout("o_ssm_s", [128, 128])
    o_k = dout("o_k", [NROW, D]); o_v = dout("o_v", [NROW, D]); o_lf = dout("o_lf", [NROW, 16])
    o_y = dout("o_y", [TOK, D])
    x2_all = dint("x2_all", [NROW, D])
    kT_all = dint("kT_all", [8 * 128, NROW], BF16)
    vT_all = dint("vT_all", [8 * 128, NROW], BF16)
    ck_all = dint("ck_all", [NROW, 16])
    lf_all = dint("lf_all", [NROW, 16])
    cref_d = dint("cref_d", [8, 16])
    ckn_d = dint("ckn_d", [32, 16])
    dbg = dout("dbg", [TOK, D]) if STOP else None

    s5c_cos = dint("s5c_cos", [128, 4096]); s5c_sin = dint("s5c_sin", [128, 4096])
    s5c_wbu = dint("s5c_wbu", [128, 8192], BF16); s5c_wc = dint("s5c_wc", [128, 8192], BF16)
    tb_all = [[dint(f"tball{l}{k}", [128 * 128, D], BF16) for k in range(2)] for l in range(2)]

    P = Prog(nc)
    with contextlib.ExitStack() as st:
        cnt = [0]

        def sb(shape, dt=F32, name=None, stack=st):
            cnt[0] += 1
            return stack.enter_context(nc.sbuf_tensor(f"s{cnt[0]}_" + (name or "t"), list(shape), dt))

        def ps(shape, dt=F32, name=None, stack=st):
            cnt[0] += 1
            return stack.enter_context(nc.psum_tensor("p_" + (name or f"ps{cnt[0]}"), list(shape), dt))

        def load(t_ap, d_ap, name, eng=Q):
            P.dma(eng, t_ap, d_ap, writes=[name])

        def tt(out, a, b, op, rd, wr, eng=V):
            P.op(eng, lambda e: e.tensor_tensor(out=out, in0=a, in1=b, op=op), reads=rd, writes=wr)

        def ts(out, a, s1, s2, op0, op1, rd, wr, eng=V):
            if op1 is None:
                P.op(eng, lambda e: e.tensor_scalar(out=out, in0=a, scalar1=s1, scalar2=None, op0=op0), reads=rd, writes=wr)
            else:
                P.op(eng, lambda e: e.tensor_scalar(out=out, in0=a, scalar1=s1, scalar2=s2, op0=op0, op1=op1), reads=rd, writes=wr)

        def act(out, a, func, rd, wr, bias=None, scale=None):
            kw = {}
            if bias is not None:
                kw["bias"] = bias
            if scale is not None:
                kw["scale"] = scale
            P.op(S, lambda e: e.activation(out=out, in_=a, func=func, **kw), reads=rd, writes=wr)

        def finish(reads, ap_pairs):
            for i, (d_ap, s_ap) in enumerate(ap_pairs):
                P.dma(Q, d_ap, s_ap, reads=reads, key=f"dbg{i % 4}")
            P.finalize()
            return nc, P

        bar = sb([128, 2], name="bar")
        P.barrier_fn = lambda e: e.memset(bar[:], 0.0)
        ident = sb([128, 128], name="ident"); load(ident[:], ident_d, "ident")
        iota = sb([128, 128], name="iota"); load(iota[:], iota_d, "iota")
        X = sb([128, NTL * D], name="X")
        XT = sb([128, 8 * TOK], BF16, name="XT")

        pA = [ps([128, 512], name=f"pA{i}") for i in range(7)]

        def transpose_tile(ti, want32, xt32=None):
            for half in range(2):
                for k in range(4):
                    dc = half * 4 + k
                    P.op(T, (lambda e, dc=dc, k=k: e.transpose(pA[0][:, k * 128:(k + 1) * 128], X[:, ti * D + dc * 128:ti * D + (dc + 1) * 128], ident[:])), reads=[f"X{ti}", "ident"], writes=["pA0"])
                if want32:
                    P.op(V, (lambda e, half=half: e.tensor_copy(xt32[:, half * 512:(half + 1) * 512], pA[0][:])), reads=[], writes=["xT32", "pA0"])
                for k in range(4):
                    dc = half * 4 + k
                    act(XT[:, dc * TOK + ti * 128:dc * TOK + (ti + 1) * 128], pA[0][:, k * 128:(k + 1) * 128], AF.Copy, [], [f"XT{ti}", "pA0"])


        with contextlib.ExitStack() as s1:
            NB = 4
            cvf = [sb([128, D], name=f"cvf{i}", stack=s1) for i in range(NB)]
            cvb = [sb([128, D], BF16, name=f"cvb{i}", stack=s1) for i in range(NB)]
            n = 0
            for l in range(2):
                for k, src in enumerate((umy_d[l], vmy_d[l])):
                    for c in range(1 if SMALLTB else 128):
                        b = n % NB; n += 1
                        P.dma(Q, cvf[b][:], src[c * 128:(c + 1) * 128, :], writes=[f"cvf{b}"])
                        if b % 2 == 0:
                            act(cvb[b][:], cvf[b][:], AF.Copy, [f"cvf{b}"], [f"cvb{b}"])
                        else:
                            P.op(V, (lambda e, b=b: e.tensor_copy(cvb[b][:], cvf[b][:])), reads=[f"cvf{b}"], writes=[f"cvb{b}"])
                        P.dma(Q, tb_all[l][k][c * 128:(c + 1) * 128, :], cvb[b][:], reads=[f"cvb{b}"], writes=[f"tball{l}{k}"], key=f"tbst{b}")
            P.barrier()
        if STOP == "tb":
            return finish([], [])

        utm = sb([128, 128], name="utm"); load(utm[:], utm_d, "utm")
        onesm = sb([128, 128], name="onesm"); load(onesm[:], onesm_d, "onesm")
        mask16 = sb([128, 1], name="mask16"); load(mask16[:], mask16_d, "mask16")
        ckrun = sb([128, 16], name="ckrun")
        P.op(V, lambda e: e.memset(ckrun[:], 0.0), writes=["ckrun"])
        dfm = sb([128, 8], name="dfm"); load(dfm[:], dfm_d, "dfm")
        r2 = sb([128, 32], name="r2")
        carry = sb([128, 64], name="carry")
        st_re = sb([128, 64], name="st_re"); load(st_re[:], st_re_d, "st_re")
        st_im = sb([128, 64], name="st_im"); load(st_im[:], st_im_d, "st_im")
        osts = sb([128, 128], name="osts")
        P.op(V, lambda e: e.memset(carry[:], 0.0), writes=["carry"])
        P.op(V, lambda e: e.memset(osts[:], 0.0), writes=["osts"])
        with contextlib.ExitStack() as s1:
            def tl(shape, dt=F32, name=None):
                return sb(shape, dt, name=name, stack=s1)
            tvec = tl([128, 128], name="tvec"); load(tvec[:], tvec_d, "tvec")
            rowmask = tl([128, 8], name="rowmask"); load(rowmask[:], rowmask_d, "rowmask")
            cosT = tl([128, 32 * 128], name="cosT"); sinT = tl([128, 32 * 128], name="sinT")
            wbu = tl([128, 32 * 2 * 128], BF16, name="wbu")
            wc = tl([128, 32 * 2 * 128], BF16, name="wc")
            RRF = 512
            with contextlib.ExitStack() as s2:
                def t2(shape, dt=F32, name=None):
                    return sb(shape, dt, name=name, stack=s2)
                rr_a = t2([128, RRF], name="rr_a"); rr_ki = t2([128, RRF], I32, name="rr_ki")
                rr_kf = t2([128, RRF], name="rr_kf"); rr_red = t2([128, RRF], name="rr_red"); rr_c1 = t2([128, RRF], name="rr_c1")

                def sin_rr(out, arg, shift, F, rd, wr):
                    a = rr_a[:, 0:F]; ki = rr_ki[:, 0:F]; kf = rr_kf[:, 0:F]; red = rr_red[:, 0:F]; c1 = rr_c1[:, 0:F]
                    n = "rr_"
                    ts(a, arg, float(shift), None, ALU.add, None, rd, [n + "a"])
                    ts(kf, a, 1.0 / TWO_PI, 0.5, ALU.mult, ALU.add, [n + "a"], [n + "kf"])
                    P.op(V, lambda e: e.tensor_copy(ki, kf), reads=[n + "kf"], writes=[n + "ki"])
                    P.op(V, lambda e: e.tensor_copy(kf, ki), reads=[n + "ki"], writes=[n + "kf"])
                    P.op(V, lambda e: e.scalar_tensor_tensor(out=red, in0=kf, scalar=-TWO_PI, in1=a, op0=ALU.mult, op1=ALU.add), reads=[n + "kf", n + "a"], writes=[n + "red"])
                    ts(c1, red, math.pi, -TWO_PI, ALU.is_gt, ALU.mult, [n + "red"], [n + "c1"])
                    tt(red, red, c1, ALU.add, [n + "red", n + "c1"], [n + "red"])
                    ts(c1, red, -math.pi, TWO_PI, ALU.is_lt, ALU.mult, [n + "red"], [n + "c1"])
                    tt(red, red, c1, ALU.add, [n + "red", n + "c1"], [n + "red"])
                    ts(red, red, math.pi, -math.pi, ALU.min, ALU.max, [n + "red"], [n + "red"])
                    act(out, red, AF.Sin, [n + "red"], wr)

                lamre2 = t2([128, 32]); lamim2 = t2([128, 32]); logdt2 = t2([128, 32])
                load(lamre2[:], lamre2_d, "lamre2"); load(lamim2[:], lamim2_d, "lamim2"); load(logdt2[:], logdt2_d, "logdt2")
                dt2 = t2([128, 32]); th2 = t2([128, 32]); tmp2 = t2([128, 32])
                act(dt2[:], logdt2[:], AF.Exp, ["logdt2"], ["dt2"])
                tt(tmp2[:], lamre2[:], dt2[:], ALU.mult, ["lamre2", "dt2"], ["tmp2"])
                act(r2[:], tmp2[:], AF.Exp, ["tmp2"], ["r2"])
                tt(th2[:], lamim2[:], dt2[:], ALU.mult, ["lamim2", "dt2"], ["th2"])
                arg = t2([128, RRF])
                for ch in range(8):
                    for g8 in range(4):
                        gp = ch * 4 + g8
                        ts(arg[:, g8 * 128:(g8 + 1) * 128], tvec[:], th2[:, gp:gp + 1], None, ALU.mult, None, ["tvec", "th2"], ["arg"])
                    sin_rr(sinT[:, ch * 512:(ch + 1) * 512], arg[:], 0.0, 512, ["arg"], ["sinT"])
                    sin_rr(cosT[:, ch * 512:(ch + 1) * 512], arg[:], math.pi / 2, 512, ["arg"], ["cosT"])
                lr = t2([128, 256]); li = t2([128, 256]); ld = t2([128, 256]); bre = t2([128, 256]); bim = t2([128, 256])
                mag = t2([128, 256]); ang = t2([128, 256]); c1_ = t2([128, 256]); s1_ = t2([128, 256])
                abr = t2([128, 256]); abi = t2([128, 256]); den = t2([128, 256]); t1 = t2([128, 256]); t2_ = t2([128, 256])
                P.op(V, lambda e: e.memset(wc[:], 0.0), writes=["wc"])
                for hf in range(2):
                    hs = slice(hf * 256, (hf + 1) * 256)
                    load(lr[:], lamre1_d[:, hs], "lr"); load(li[:], lamim1_d[:, hs], "li"); load(ld[:], logdt1_d[:, hs], "ld")
                    load(bre[:], bre1_d[:, hs], "bre"); load(bim[:], bim1_d[:, hs], "bim")
                    act(ld[:], ld[:], AF.Exp, ["ld"], ["ld"])
                    tt(mag[:], lr[:], ld[:], ALU.mult, ["lr", "ld"], ["mag"])
                    act(mag[:], mag[:], AF.Exp, ["mag"], ["mag"])
                    tt(ang[:], li[:], ld[:], ALU.mult, ["li", "ld"], ["ang"])
                    sin_rr(s1_[:], ang[:], 0.0, 256, ["ang"], ["s1_"])
                    sin_rr(c1_[:], ang[:], math.pi / 2, 256, ["ang"], ["c1_"])
                    tt(abr[:], mag[:], c1_[:], ALU.mult, ["mag", "c1_"], ["abr"])
                    tt(abi[:], mag[:], s1_[:], ALU.mult, ["mag", "s1_"], ["abi"])
                    ts(abr[:], abr[:], -1.0, None, ALU.add, None, ["abr"], ["abr"])
                    tt(den[:], lr[:], lr[:], ALU.mult, ["lr"], ["den"])
                    tt(t1[:], li[:], li[:], ALU.mult, ["li"], ["t1"])
                    tt(den[:], den[:], t1[:], ALU.add, ["den", "t1"], ["den"])
                    P.op(V, lambda e: e.reciprocal(den[:], den[:]), reads=["den"], writes=["den"])
                    tt(t1[:], abr[:], lr[:], ALU.mult, ["abr", "lr"], ["t1"])
                    tt(t2_[:], abi[:], li[:], ALU.mult, ["abi", "li"], ["t2"])
                    tt(t1[:], t1[:], t2_[:], ALU.add, ["t1", "t2"], ["t1"])
                    tt(mag[:], t1[:], den[:], ALU.mult, ["t1", "den"], ["mag"])
                    tt(t1[:], abi[:], lr[:], ALU.mult, ["abi", "lr"], ["t1"])
                    tt(t2_[:], abr[:], li[:], ALU.mult, ["abr", "li"], ["t2"])
                    tt(t1[:], t1[:], t2_[:], ALU.subtract, ["t1", "t2"], ["t1"])
                    tt(ang[:], t1[:], den[:], ALU.mult, ["t1", "den"], ["ang"])
                    tt(t1[:], mag[:], bre[:], ALU.mult, ["mag", "bre"], ["t1"])
                    tt(t2_[:], ang[:], bim[:], ALU.mult, ["ang", "bim"], ["t2"])
                    tt(s1_[:], t1[:], t2_[:], ALU.subtract, ["t1", "t2"], ["s1_"])
                    tt(t1[:], mag[:], bim[:], ALU.mult, ["mag", "bim"], ["t1"])
                    tt(t2_[:], ang[:], bre[:], ALU.mult, ["ang", "bre"], ["t2"])
                    tt(c1_[:], t1[:], t2_[:], ALU.add, ["t1", "t2"], ["c1_"])
                    for gp in range(hf * 16, hf * 16 + 16):
                        dcl = gp // 4 - 4 * hf
                        for gl in range(2):
                            j = 2 * (gp % 4) + gl
                            for ri, src in enumerate((s1_, c1_)):
                                o0 = (gp * 2 + ri) * 128 + gl * 64
                                ts(wbu[:, o0:o0 + 64], src[:, dcl * 64:(dcl + 1) * 64], rowmask[:, j:j + 1], None, ALU.mult, None, ["s1_", "c1_", "rowmask"], ["wbu"])
                    load(den[:], cre2_d[:, hs], "den"); load(abi[:], cim2_d[:, hs], "abi")
                    for gp in range(hf * 16, hf * 16 + 16):
                        gpl = gp - 16 * hf
                        for gl in range(2):
                            j = 2 * (gp % 4) + gl
                            for ri, (src, sc) in enumerate(((den, 1.0), (abi, -1.0))):
                                o0 = (gp * 2 + ri) * 128 + j * 16
                                ts(wc[gl * 64:(gl + 1) * 64, o0:o0 + 16], src[gl * 64:(gl + 1) * 64, gpl * 16:(gpl + 1) * 16], sc, None, ALU.mult, None, ["den", "abi"], ["wc"])
                P.barrier()
            P.dma(Q, s5c_cos[:, :], cosT[:], reads=["cosT"], writes=["s5c"], key="s5c0")
            P.dma(Q, s5c_sin[:, :], sinT[:], reads=["sinT"], writes=["s5c"], key="s5c1")
            P.dma(Q, s5c_wbu[:, :], wbu[:], reads=["wbu"], writes=["s5c"], key="s5c2")
            P.dma(Q, s5c_wc[:, :], wc[:], reads=["wc"], writes=["s5c"], key="s5c3")
            P.barrier()

        def s5_group(tiles, g):
            s1 = contextlib.ExitStack()
            def tl(shape, dt=F32, name=None):
                return sb(shape, dt, name=name, stack=s1)
            cosT = tl([128, 32 * 128], name="cosT"); sinT = tl([128, 32 * 128], name="sinT")
            wbu = tl([128, 32 * 2 * 128], BF16, name="wbu")
            wc = tl([128, 32 * 2 * 128], BF16, name="wc")
            P.dma(Q, cosT[:], s5c_cos[:, :], reads=["s5c"], writes=["cosT"])
            P.dma(Q, sinT[:], s5c_sin[:, :], reads=["s5c"], writes=["sinT"])
            P.dma(Q, wbu[:], s5c_wbu[:, :], reads=["s5c"], writes=["wbu"])
            P.dma(Q, wc[:], s5c_wc[:, :], reads=["s5c"], writes=["wc"])
            xT32 = tl([128, 8 * 128], name="xT32")
            bsets = []
            for q_ in range(2):
                tm = [tl([128, 128], name=f"tm{i}_{q_}") for i in range(4)]
                zr = tl([128, 128], name=f"zr{q_}"); zi = tl([128, 128], name=f"zi{q_}")
                gr = tl([128, 128], name=f"gr{q_}"); gi_ = tl([128, 128], name=f"gi{q_}")
                hr = tl([128, 128], name=f"hr{q_}"); hi = tl([128, 128], name=f"hi{q_}")
                hrb = tl([128, 128], BF16, name=f"hrb{q_}"); hib = tl([128, 128], BF16, name=f"hib{q_}")
                bsets.append((tm, zr, zi, gr, gi_, hr, hi, hrb, hib, f"_{q_}"))
            yf = tl([128, 128], name="yf")
            pT = pA[0]; pbu = [pA[1], pA[2]]; pbn_ = ["pA1", "pA2"]; py = [pA[3], pA[4]]; pyn = ["pA3", "pA4"]

            def s5_one(ti, gp, segs, full, bs):
                tm, zr, zi, gr, gi_, hr, hi, hrb, hib, sfx = bs
                dc = gp // 4
                pb = pbu[gp % 2]; pbn = pbn_[gp % 2]
                for ri in range(2):
                    o0 = (gp * 2 + ri) * 128
                    P.op(T, (lambda e, pb=pb, ri=ri, o0=o0, dc=dc: e.matmul(pb[:, ri * 128:(ri + 1) * 128], wbu[:, o0:o0 + 128], XT[:, dc * TOK + ti * 128:dc * TOK + (ti + 1) * 128], start=True, stop=True)), reads=["wbu", f"XT{ti}"], writes=[pbn])
                    yield
                partial = full and (len(segs) > 1 or segs[0][1] < 128)
                if partial:
                    P.op(V, lambda e: e.memset(hr[:], 0.0), writes=["hr" + sfx])
                    yield
                    P.op(V, lambda e: e.memset(hi[:], 0.0), writes=["hi" + sfx])
                    yield
                for (c0, n, init, dest) in segs:
                    cs = cosT[:, gp * 128:gp * 128 + n]; sn = sinT[:, gp * 128:gp * 128 + n]
                    br = pb[:, c0:c0 + n]; bi = pb[:, 128 + c0:128 + c0 + n]
                    sl = slice(c0, c0 + n)
                    e2 = G
                    tt(tm[0][:, sl], br, cs, ALU.mult, ["cosT"], ["tm0" + sfx, pbn])
                    yield
                    tt(tm[1][:, sl], bi, sn, ALU.mult, ["sinT"], ["tm1" + sfx, pbn])
                    yield
                    tt(zr[:, sl], tm[0][:, sl], tm[1][:, sl], ALU.add, ["tm0" + sfx, "tm1" + sfx], ["zr" + sfx], eng=e2)
                    yield
                    tt(tm[2][:, sl], bi, cs, ALU.mult, ["cosT"], ["tm2" + sfx, pbn])
                    yield
                    tt(tm[3][:, sl], br, sn, ALU.mult, ["sinT"], ["tm3" + sfx, pbn])
                    yield
                    tt(zi[:, sl], tm[2][:, sl], tm[3][:, sl], ALU.subtract, ["tm2" + sfx, "tm3" + sfx], ["zi" + sfx], eng=e2)
                    yield
                    if init == "carry":
                        ir = carry[:, gp:gp + 1]; ii = carry[:, 32 + gp:33 + gp]; irn = ["carry"]
                    elif init == "zero":
                        ir = 0.0; ii = 0.0; irn = []
                    else:
                        sidx = init
                        ir = st_re[:, sidx * 32 + gp:sidx * 32 + gp + 1]; ii = st_im[:, sidx * 32 + gp:sidx * 32 + gp + 1]; irn = ["st_re", "st_im"]
                    rb = r2[:, gp:gp + 1].to_broadcast([128, n])
                    P.op(V, (lambda e, rb=rb, ir=ir, sl=sl: e.tensor_tensor_scan(out=gr[:, sl], data0=rb, data1=zr[:, sl], initial=ir, op0=ALU.mult, op1=ALU.add)), reads=["r2", "zr" + sfx] + irn, writes=["gr" + sfx])
                    yield
                    P.op(V, (lambda e, rb=rb, ii=ii, sl=sl: e.tensor_tensor_scan(out=gi_[:, sl], data0=rb, data1=zi[:, sl], initial=ii, op0=ALU.mult, op1=ALU.add)), reads=["r2", "zi" + sfx] + irn, writes=["gi" + sfx])
                    yield
                    if full:
                        us = sl; ucs = cs; usn = sn
                    else:
                        us = slice(c0 + n - 1, c0 + n); ucs = cosT[:, gp * 128 + n - 1:gp * 128 + n]; usn = sinT[:, gp * 128 + n - 1:gp * 128 + n]
                    tt(tm[0][:, us], gr[:, us], ucs, ALU.mult, ["gr" + sfx, "cosT"], ["tm0" + sfx])
                    yield
                    tt(tm[1][:, us], gi_[:, us], usn, ALU.mult, ["gi" + sfx, "sinT"], ["tm1" + sfx])
                    yield
                    tt(hr[:, us], tm[0][:, us], tm[1][:, us], ALU.subtract, ["tm0" + sfx, "tm1" + sfx], ["hr" + sfx], eng=e2)
                    yield
                    tt(tm[2][:, us], gi_[:, us], ucs, ALU.mult, ["gi" + sfx, "cosT"], ["tm2" + sfx])
                    yield
                    tt(tm[3][:, us], gr[:, us], usn, ALU.mult, ["gr" + sfx, "sinT"], ["tm3" + sfx])
                    yield
                    tt(hi[:, us], tm[2][:, us], tm[3][:, us], ALU.add, ["tm2" + sfx, "tm3" + sfx], ["hi" + sfx], eng=e2)
                    yield
                    last = c0 + n - 1
                    if dest is not None:
                        dt_, o_r, o_i, dn = dest
                        P.op(V, (lambda e, last=last, dt_=dt_, o_r=o_r: e.tensor_copy(dt_[:, o_r:o_r + 1], hr[:, last:last + 1])), reads=["hr" + sfx], writes=[dn])
                        yield
                        P.op(V, (lambda e, last=last, dt_=dt_, o_i=o_i: e.tensor_copy(dt_[:, o_i:o_i + 1], hi[:, last:last + 1])), reads=["hi" + sfx], writes=[dn])
                        yield
                if full:
                    act(hrb[:], hr[:], AF.Copy, ["hr" + sfx], ["hrb" + sfx])
                    yield
                    act(hib[:], hi[:], AF.Copy, ["hi" + sfx], ["hib" + sfx])
                    yield
                    pyt = py[dc % 2]; pn = pyn[dc % 2]
                    k4 = gp % 4
                    for ri, hb in enumerate((hrb, hib)):
                        o0 = (gp * 2 + ri) * 128
                        P.op(T, (lambda e, pyt=pyt, o0=o0, hb=hb, first=(k4 == 0 and ri == 0), lastm=(k4 == 3 and ri == 1): e.matmul(pyt[:, 0:128], wc[:, o0:o0 + 128], hb[:], start=first, stop=lastm)), reads=["wc", "hrb" + sfx, "hib" + sfx], writes=[pn])
                        yield
                    if k4 == 3:
                        P.op(V, (lambda e, pyt=pyt, dc=dc: e.scalar_tensor_tensor(out=yf[:], in0=xT32[:, dc * 128:(dc + 1) * 128], scalar=dfm[:, dc:dc + 1], in1=pyt[:, 0:128], op0=ALU.mult, op1=ALU.add)), reads=["xT32", "dfm"], writes=["yf", pn])
                        yield
                        act(XT[:, dc * TOK + ti * 128:dc * TOK + (ti + 1) * 128], yf[:], AF.Gelu_apprx_tanh, ["yf"], [f"XT{ti}"])
                        yield

            def s5_tile(ti, segs_fn, full):
                import itertools
                for gp in range(0, 32, 2):
                    ga = s5_one(ti, gp, segs_fn(gp), full, bsets[0])
                    gb = s5_one(ti, gp + 1, segs_fn(gp + 1), full, bsets[1])
                    for _ in itertools.zip_longest(ga, gb):
                        pass

            for ti in tiles:
                transpose_tile(ti, True, xT32)
                if ti == 0:
                    s5_tile(0, lambda gp: [(0, 16, "zero", (carry, gp, 32 + gp, "carry")), (16, 16, 0, (osts, gp, 32 + gp, "osts")), (32, 16, 1, (osts, 64 + gp, 96 + gp, "osts"))], True)
                else:
                    s5_tile(ti, lambda gp: [(0, 128, "carry", (carry, gp, 32 + gp, "carry"))], True)
            P.barrier()
            s1.close()


        lnrow = sb([128, 2 * D], name="lnrow")

        def load_ln(idx_g, idx_b):
            P.dma(Q, lnrow[:, 0:D], lnv_d[idx_g:idx_g + 1, :].partition_broadcast(128), writes=["lnrow"])
            P.dma(Q, lnrow[:, D:2 * D], lnv_d[idx_b:idx_b + 1, :].partition_broadcast(128), writes=["lnrow"])

        lnst = sb([128, 12], name="lnst"); lnmv = sb([128, 2], name="lnmv"); lnr = sb([128, 1], name="lnr")

        def layer_norm_tile(ti):
            xt_ = X[:, ti * D:(ti + 1) * D]
            for h in range(2):
                P.op(V, (lambda e, h=h: e.bn_stats(lnst[:, h * 6:(h + 1) * 6], X[:, ti * D + h * 512:ti * D + (h + 1) * 512])), reads=[f"X{ti}"], writes=["lnst"])
            P.op(V, lambda e: e.bn_aggr(lnmv[:], lnst[:]), reads=["lnst"], writes=["lnmv"])
            ts(lnr[:], lnmv[:, 1:2], LN_EPS, None, ALU.add, None, ["lnmv"], ["lnr"])
            act(lnr[:], lnr[:], AF.Sqrt, ["lnr"], ["lnr"])
            P.op(V, lambda e: e.reciprocal(lnr[:], lnr[:]), reads=["lnr"], writes=["lnr"])
            ts(xt_, xt_, lnmv[:, 0:1], lnr[:, 0:1], ALU.subtract, ALU.mult, [f"X{ti}", "lnmv", "lnr"], [f"X{ti}"])
            tt(xt_, xt_, lnrow[:, 0:D], ALU.mult, [f"X{ti}", "lnrow"], [f"X{ti}"], eng=G)
            tt(xt_, xt_, lnrow[:, D:2 * D], ALU.add, [f"X{ti}", "lnrow"], [f"X{ti}"], eng=G)

        def glu_ln1(tiles):
            s1 = contextlib.ExitStack()
            wgf = sb([128, 8 * 512], name="wgf", stack=s1)
            wgv = sb([128, 8 * 512], BF16, name="wgv", stack=s1); wgg = sb([128, 8 * 512], BF16, name="wgg", stack=s1)
            sig = sb([128, 512], name="sig", stack=s1); mixv = sb([128, 512], name="mixv", stack=s1)
            load_ln(0, 1)
            wview = wglu_d.rearrange("p (dc n) -> p dc n", dc=8)
            for cb in range(2):
                for which, dst, dn in ((0, wgv, "wgv"), (1, wgg, "wgg")):
                    c0 = which * 1024 + cb * 512
                    P.dma(Q, wgf[:].rearrange("p (dc n) -> p dc n", dc=8), wview[:, :, c0:c0 + 512], writes=["wgf"])
                    act(dst[:], wgf[:], AF.Copy, ["wgf"], [dn])
                for ti in tiles:
                    for which, wt, wn, pst, pn in ((0, wgv, "wgv", pA[5], "pA5"), (1, wgg, "wgg", pA[6], "pA6")):
                        for dc in range(8):
                            P.op(T, (lambda e, wt=wt, pst=pst, dc=dc, ti=ti: e.matmul(pst[:], XT[:, dc * TOK + ti * 128:dc * TOK + (ti + 1) * 128], wt[:, dc * 512:(dc + 1) * 512], start=(dc == 0), stop=(dc == 7))), reads=[f"XT{ti}", wn], writes=[pn])
                    act(sig[:], pA[6][:], AF.Sigmoid, [], ["sig", "pA6"])
                    tt(mixv[:], pA[5][:], sig[:], ALU.mult, ["sig"], ["mixv", "pA5"])
                    xs_ = X[:, ti * D + cb * 512:ti * D + (cb + 1) * 512]
                    P.op(V, (lambda e, xs_=xs_: e.scalar_tensor_tensor(out=xs_, in0=xs_, scalar=DN_ALPHA, in1=mixv[:], op0=ALU.mult, op1=ALU.add)), reads=["mixv", f"X{ti}"], writes=[f"X{ti}"])
            for ti in tiles:
                layer_norm_tile(ti)
            P.barrier()
            s1.close()

        IDX1T = sb([128, TOK], BF16, name="IDX1T"); IDX2T = sb([128, TOK], BF16, name="IDX2T"); GT = sb([128, TOK], BF16, name="GT")

        def peer_layer(l, ln_idx, tiles):
            for ti in tiles:
                transpose_tile(ti, False)
            with contextlib.ExitStack() as s1:
                def tl(shape, dt=F32, name=None):
                    return sb(shape, dt, name=name, stack=s1)
                wst = tl([128, 8 * 256], name="wst")
                wpq = tl([128, 8 * 2048], BF16, name="wpq")
                pkf = tl([128, 256], name="pkf"); pkb = tl([128, 256], BF16, name="pkb")
                load(pkf[:], pk_d[l], "pkf")
                act(pkb[:], pkf[:], AF.Copy, ["pkf"], ["pkb"])
                wv = wpq_d[l].rearrange("p (dc n) -> p dc n", dc=8)
                wpv = wpq[:].rearrange("p (dc n) -> p dc n", dc=8)
                for cb in range(8):
                    P.dma(Q, wst[:].rearrange("p (dc n) -> p dc n", dc=8), wv[:, :, cb * 256:(cb + 1) * 256], writes=["wst"])
                    if cb % 2 == 0:
                        act(wpv[:, :, cb * 256:(cb + 1) * 256], wst[:].rearrange("p (dc n) -> p dc n", dc=8), AF.Copy, ["wst"], ["wpq"])
                    else:
                        P.op(V, (lambda e, cb=cb: e.tensor_copy(wpv[:, :, cb * 256:(cb + 1) * 256], wst[:].rearrange("p (dc n) -> p dc n", dc=8))), reads=["wst"], writes=["wpq"])
                qT = tl([128, 512], BF16, name="qT")
                Ssb = tl([128, 2048], name="Ssb"); Stmp = tl([128, 128], name="Stmp")
                vals = tl([128, 256], name="vals"); idxs = tl([128, 256], U32, name="idxs"); idxf = tl([128, 256], name="idxf")
                cand = tl([128, 2048], name="cand"); ctmp = tl([128, 256], name="ctmp")
                cv = tl([128, 128], name="cv"); cp = tl([128, 128], U32, name="cp"); cpf = tl([128, 128], name="cpf")
                a_i = tl([128, 128], I32, name="a_i"); a0 = tl([128, 128], name="a0"); gtm = tl([128, 128], name="gtm")
                a_f = tl([128, 128], name="a_f"); b_f = tl([128, 128], name="b_f")
                eqt = tl([128, 2048], name="eqt")
                i1t = tl([128, 128], name="i1t"); i2t = tl([128, 128], name="i2t"); gwt = tl([128, 128], name="gwt")
                negm = tl([128, 8], name="negm"); zs = tl([128, 8], name="zs")
                iota16 = iota[:, 0:16]
                for ti in tiles:
                    for jq in range(4):
                        for jj in range(4):
                            j = jq * 4 + jj
                            for dc in range(8):
                                P.op(T, (lambda e, jj=jj, j=j, dc=dc, ti=ti: e.matmul(pA[1][:, jj * 128:(jj + 1) * 128], wpq[:, dc * 2048 + j * 128:dc * 2048 + (j + 1) * 128], XT[:, dc * TOK + ti * 128:dc * TOK + (ti + 1) * 128], start=(dc == 0), stop=(dc == 7))), reads=["wpq", f"XT{ti}"], writes=["pA1"])
                        act(qT[:], pA[1][:], AF.Copy, [], ["qT", "pA1"])
                        for jj in range(4):
                            j = jq * 4 + jj
                            hf = j % 2
                            P.op(T, (lambda e, jj=jj, hf=hf: e.matmul(pA[2][:, jj * 128:(jj + 1) * 128], qT[:, jj * 128:(jj + 1) * 128], pkb[:, hf * 128:(hf + 1) * 128], start=True, stop=True)), reads=["qT", "pkb"], writes=["pA2"])
                        P.op(V, (lambda e, jq=jq: e.tensor_copy(Ssb[:, jq * 512:(jq + 1) * 512], pA[2][:])), reads=[], writes=["Ssb", "pA2"])
                    if STOP == "peers":
                        return finish(["Ssb"], [(dbg[0:128, :], Ssb[:, 0:1024]), (dbg[128:256, :], Ssb[:, 1024:2048])])
                    for j in range(16):
                        sj = Ssb[:, j * 128:(j + 1) * 128]
                        v0 = vals[:, j * 16:j * 16 + 8]; v1 = vals[:, j * 16 + 8:j * 16 + 16]
                        x0 = idxs[:, j * 16:j * 16 + 8]; x1 = idxs[:, j * 16 + 8:j * 16 + 16]
                        P.op(V, (lambda e, sj=sj, v0=v0: e.max(out=v0, in_=sj)), reads=["Ssb"], writes=["vals"])
                        P.op(V, (lambda e, sj=sj, v0=v0, x0=x0: e.max_index(out=x0, in_max=v0, in_values=sj)), reads=["Ssb", "vals"], writes=["idxs"])
                        P.op(V, (lambda e, sj=sj, v0=v0: e.match_replace(out=Stmp[:], in_to_replace=v0, in_values=sj, imm_value=NEG)), reads=["Ssb", "vals"], writes=["Stmp"])
                        P.op(V, (lambda e, v1=v1: e.max(out=v1, in_=Stmp[:])), reads=["Stmp"], writes=["vals"])
                        P.op(V, (lambda e, v1=v1, x1=x1: e.max_index(out=x1, in_max=v1, in_values=Stmp[:])), reads=["Stmp", "vals"], writes=["idxs"])
                    P.op(V, lambda e: e.tensor_copy(idxf[:], idxs[:]), reads=["idxs"], writes=["idxf"])
                    vv = vals[:].rearrange("p (h s k) -> p h s k", h=8, s=2)
                    ivf = idxf[:].rearrange("p (h s k) -> p h s k", h=8, s=2)
                    cand4 = cand[:].rearrange("p (h a b) -> p h a b", h=8, a=16)
                    P.op(V, lambda e: e.tensor_tensor(out=cand4, in0=vv[:, :, 0, :][:, :, :, None].to_broadcast([128, 8, 16, 16]), in1=vv[:, :, 1, :][:, :, None, :].to_broadcast([128, 8, 16, 16]), op=ALU.add), reads=["vals"], writes=["cand"])
                    for h in range(8):
                        ch = cand[:, h * 256:(h + 1) * 256]
                        c0 = cv[:, h * 16:h * 16 + 8]; c1 = cv[:, h * 16 + 8:h * 16 + 16]
                        p0 = cp[:, h * 16:h * 16 + 8]; p1 = cp[:, h * 16 + 8:h * 16 + 16]
                        P.op(V, (lambda e, ch=ch, c0=c0: e.max(out=c0, in_=ch)), reads=["cand"], writes=["cv"])
                        P.op(V, (lambda e, ch=ch, c0=c0, p0=p0: e.max_index(out=p0, in_max=c0, in_values=ch)), reads=["cand", "cv"], writes=["cp"])
                        P.op(V, (lambda e, ch=ch, c0=c0: e.match_replace(out=ctmp[:], in_to_replace=c0, in_values=ch, imm_value=NEG)), reads=["cand", "cv"], writes=["ctmp"])
                        P.op(V, (lambda e, c1=c1: e.max(out=c1, in_=ctmp[:])), reads=["ctmp"], writes=["cv"])
                        P.op(V, (lambda e, c1=c1, p1=p1: e.max_index(out=p1, in_max=c1, in_values=ctmp[:])), reads=["ctmp", "cv"], writes=["cp"])
                    P.op(V, lambda e: e.tensor_copy(cpf[:], cp[:]), reads=["cp"], writes=["cpf"])
                    ts(a_i[:], cpf[:], 0.0625, None, ALU.mult, None, ["cpf"], ["a_i"])
                    P.op(V, lambda e: e.tensor_copy(a0[:], a_i[:]), reads=["a_i"], writes=["a0"])
                    P.op(V, lambda e: e.scalar_tensor_tensor(out=gtm[:], in0=a0[:], scalar=16.0, in1=cpf[:], op0=ALU.mult, op1=ALU.is_gt), reads=["a0", "cpf"], writes=["gtm"])
                    tt(a_f[:], a0[:], gtm[:], ALU.subtract, ["a0", "gtm"], ["a_f"])
                    P.op(V, lambda e: e.scalar_tensor_tensor(out=b_f[:], in0=a_f[:], scalar=-16.0, in1=cpf[:], op0=ALU.mult, op1=ALU.add), reads=["a_f", "cpf"], writes=["b_f"])
                    eq4 = eqt[:].rearrange("p (h k a) -> p h k a", h=8, k=16)
                    io4 = iota16[:, None, None, :].to_broadcast([128, 8, 16, 16])
                    for (sel, half, dst, dn) in ((a_f, 0, i1t, "i1t"), (b_f, 1, i2t, "i2t")):
                        s3 = sel[:].rearrange("p (h k) -> p h k", h=8)
                        P.op(V, (lambda e, s3=s3: e.tensor_tensor(out=eq4, in0=s3[:, :, :, None].to_broadcast([128, 8, 16, 16]), in1=io4, op=ALU.is_equal)), reads=[sel.name if False else ("a_f" if half == 0 else "b_f"), "iota"], writes=["eqt"])
                        P.op(V, (lambda e, half=half: e.tensor_tensor(out=eq4, in0=eq4, in1=ivf[:, :, half, :][:, :, None, :].to_broadcast([128, 8, 16, 16]), op=ALU.mult)), reads=["eqt", "idxf"], writes=["eqt"])
                        P.op(V, (lambda e, dst=dst: e.tensor_reduce(out=dst[:].rearrange("p (h k) -> p h k", h=8), in_=eq4, axis=AX.X, op=ALU.add)), reads=["eqt"], writes=[dn])
                    cv3 = cv[:].rearrange("p (h k) -> p h k", h=8)
                    ts(negm[:], cv3[:, :, 0], -1.0, None, ALU.mult, None, ["cv"], ["negm"])
                    for h in range(8):
                        act(gwt[:, h * 16:(h + 1) * 16], cv[:, h * 16:(h + 1) * 16], AF.Exp, ["cv", "negm"], ["gwt"], bias=negm[:, h:h + 1], scale=1.0)
                    g3 = gwt[:].rearrange("p (h k) -> p h k", h=8)
                    P.op(V, lambda e: e.tensor_reduce(out=zs[:], in_=g3, axis=AX.X, op=ALU.add), reads=["gwt"], writes=["zs"])
                    P.op(V, lambda e: e.reciprocal(zs[:], zs[:]), reads=["zs"], writes=["zs"])
                    P.op(V, lambda e: e.tensor_tensor(out=g3, in0=g3, in1=zs[:, :, None].to_broadcast([128, 8, 16]), op=ALU.mult), reads=["gwt", "zs"], writes=["gwt"])
                    for (src, sn, dstT, dname) in ((i1t, "i1t", IDX1T, "IDX1T"), (i2t, "i2t", IDX2T, "IDX2T"), (gwt, "gwt", GT, "GT")):
                        P.op(T, (lambda e, src=src: e.transpose(pA[0][:, 0:128], src[:], ident[:])), reads=[sn, "ident"], writes=["pA0"])
                        act(dstT[:, ti * 128:(ti + 1) * 128], pA[0][:, 0:128], AF.Copy, [], [dname, "pA0"])
                if STOP == f"peerq{l}":
                    allx = [f"X{t}" for t in range(NTL)]
                    P.op(V, lambda e: e.tensor_copy(X[:, 7 * D:9 * D], Ssb[:]), reads=["Ssb"], writes=allx)
                    P.op(V, lambda e: e.tensor_copy(X[:, 9 * D:9 * D + 256], vals[:]), reads=["vals"], writes=allx)
                    P.op(V, lambda e: e.tensor_copy(X[:, 9 * D + 256:9 * D + 512], idxf[:]), reads=["idxf"], writes=allx)
                    P.op(V, lambda e: e.tensor_copy(X[:, 9 * D + 512:9 * D + 640], cv[:]), reads=["cv"], writes=allx)
                    P.op(V, lambda e: e.tensor_copy(X[:, 9 * D + 640:9 * D + 768], cpf[:]), reads=["cpf"], writes=allx)
                P.barrier()
            if STOP == f"peerq{l}":
                return "stop"
            with contextlib.ExitStack() as s1:
                def tl(shape, dt=F32, name=None):
                    return sb(shape, dt, name=name, stack=s1)
                Gsb = tl([128, 128 * 256], BF16, name="Gsb")
                NOH = 4
                oh1 = [tl([128, 128], BF16, name=f"oh1_{i}") for i in range(NOH)]
                oh2 = [tl([128, 128], BF16, name=f"oh2_{i}") for i in range(NOH)]
                NUB = 2
                ub = [tl([128, D], BF16, name=f"ub{i}") for i in range(NUB)]
                vb = [tl([128, D], BF16, name=f"vb{i}") for i in range(NUB)]
                ga = [tl([128, 256], name=f"ga{i}") for i in range(2)]
                wT = [tl([128, 256], BF16, name=f"wT{i}") for i in range(2)]
                load_ln(ln_idx, ln_idx + 1)
                G3 = Gsb[:].rearrange("p (c t) -> p c t", c=128)
                st_list = [(tiles[i], 2) for i in range(0, len(tiles) - 1, 2)] + ([(tiles[-1], 1)] if len(tiles) % 2 else [])
                cnt_c = 0
                for (t0, ntile) in st_list:
                    NTK = ntile * 128
                    col0 = t0 * 128
                    for t in range(NTK):
                        b = t % NOH
                        gc = col0 + t
                        ts(oh2[b][:], iota[:], IDX2T[:, gc:gc + 1], None, ALU.is_equal, None, ["iota", "IDX2T"], [f"oh2_{b}"], eng=G)
                        ts(oh1[b][:], iota[:], IDX1T[:, gc:gc + 1], GT[:, gc:gc + 1], ALU.is_equal, ALU.mult, ["iota", "IDX1T", "GT"], [f"oh1_{b}"])
                        pg = pA[1 + (t // 4) % 2]; pgn = f"pA{1 + (t // 4) % 2}"
                        P.op(T, (lambda e, pg=pg, b=b, t=t: e.matmul(pg[:, (t % 4) * 128:(t % 4 + 1) * 128], oh2[b][:], oh1[b][:], start=True, stop=True)), reads=[f"oh2_{b}", f"oh1_{b}"], writes=[pgn])
                        if t % 4 == 3:
                            tq = t - 3
                            act(G3[:, :, tq:tq + 4], pg[:].rearrange("p (t c) -> p c t", t=4), AF.Copy, [], ["Gsb", pgn])
                    for c in range(NCH):
                        bi = cnt_c % NUB; b2 = cnt_c % 2; cnt_c += 1
                        P.dma(Q, ub[bi][:], tb_all[l][0][c * 128:(c + 1) * 128, :], reads=[f"tball{l}0"], writes=[f"ub{bi}"])
                        P.dma(Q, vb[bi][:], tb_all[l][1][c * 128:(c + 1) * 128, :], reads=[f"tball{l}1"], writes=[f"vb{bi}"])
                        pa = pA[1 + b2]; pan = f"pA{1 + b2}"
                        for dc in range(8):
                            P.op(T, (lambda e, pa=pa, bi=bi, dc=dc, col0=col0, NTK=NTK: e.matmul(pa[:, 0:NTK], ub[bi][:, dc * 128:(dc + 1) * 128], XT[:, dc * TOK + col0:dc * TOK + col0 + NTK], start=(dc == 0), stop=(dc == 7))), reads=[f"ub{bi}"] + [f"XT{t0 + i}" for i in range(ntile)], writes=[pan])
                        act(ga[b2][:, 0:NTK], pa[:, 0:NTK], AF.Gelu_apprx_tanh, [], [f"ga{b2}", pan])
                        tt(wT[b2][:, 0:NTK], ga[b2][:, 0:NTK], G3[:, c, 0:NTK], ALU.mult, [f"ga{b2}", "Gsb"], [f"wT{b2}"])
                        for tb in range(ntile):
                            for hf in range(2):
                                po = pA[3 + tb * 2 + hf]; pon = f"pA{3 + tb * 2 + hf}"
                                P.op(T, (lambda e, po=po, b2=b2, bi=bi, tb=tb, hf=hf, c=c: e.matmul(po[:], wT[b2][:, tb * 128:(tb + 1) * 128], vb[bi][:, hf * 512:(hf + 1) * 512], start=(c == 0), stop=(c == NCH - 1))), reads=[f"wT{b2}", f"vb{bi}"], writes=[pon])
                    for tb in range(ntile):
                        ti = t0 + tb
                        for hf in range(2):
                            po = pA[3 + tb * 2 + hf]; pon = f"pA{3 + tb * 2 + hf}"
                            xs_ = X[:, ti * D + hf * 512:ti * D + (hf + 1) * 512]
                            P.op(V, (lambda e, xs_=xs_, po=po: e.scalar_tensor_tensor(out=xs_, in0=xs_, scalar=DN_ALPHA, in1=po[:], op0=ALU.mult, op1=ALU.add)), reads=[f"X{ti}"], writes=[f"X{ti}", pon])
                        layer_norm_tile(ti)
                P.barrier()
            return "ok"

        NG = int(os.environ.get("MK_NG", "8"))

        def kv_phase(tiles, g):
            s1 = contextlib.ExitStack()
            def tl(shape, dt=F32, name=None):
                return sb(shape, dt, name=name, stack=s1)
            for ti in tiles:
                transpose_tile(ti, False)
            wst = tl([128, 8 * 512], name="kwst"); wb = tl([128, 8 * 512], BF16, name="kwb")
            osb = [tl([128, 512], name=f"kosb{i}") for i in range(2)]
            ovb = [tl([128, 512], BF16, name=f"kovb{i}") for i in range(2)]
            okT = [tl([128, 512], BF16, name=f"kokT{i}") for i in range(2)]
            wv = wkvf_d.rearrange("p (dc n) -> p dc n", dc=8)
            wst3 = wst[:].rearrange("p (dc n) -> p dc n", dc=8)
            def row0(ti):
                return ti * 128 if ti == 0 else 128 + g * 2048 + (ti - 1) * 128
            n = 0
            for cb in range(4):
                P.dma(Q, wst3, wv[:, :, cb * 512:(cb + 1) * 512], writes=["kwst"])
                act(wb[:], wst[:], AF.Copy, ["kwst"], ["kwb"])
                for ti in tiles:
                    b = n % 2; n += 1
                    for dc in range(8):
                        P.op(T, (lambda e, dc=dc, ti=ti: e.matmul(pA[1][:], XT[:, dc * TOK + ti * 128:dc * TOK + (ti + 1) * 128], wb[:, dc * 512:(dc + 1) * 512], start=(dc == 0), stop=(dc == 7))), reads=[f"XT{ti}", "kwb"], writes=["pA1"])
                    P.op(V, (lambda e, b=b: e.tensor_copy(osb[b][:], pA[1][:])), reads=[], writes=[f"kosb{b}", "pA1"])
                    dst = o_k if cb < 2 else o_v
                    c0 = (cb % 2) * 512
                    P.dma(Q, dst[row0(ti):row0(ti) + 128, c0:c0 + 512], osb[b][:], reads=[f"kosb{b}"], key=f"kout{b}")
                if True:
                    dstT = kT_all if cb < 2 else vT_all
                    for j in range(4):
                        dcq = (cb % 2) * 4 + j
                        for (ta, nt) in [(tiles[i], min(4, len(tiles) - i)) for i in range(0, len(tiles), 4)]:
                            b = n % 2; n += 1
                            for dc in range(8):
                                P.op(T, (lambda e, dc=dc, j=j, ta=ta, nt=nt: e.matmul(pA[2][:, 0:nt * 128], wb[:, dc * 512 + j * 128:dc * 512 + (j + 1) * 128], XT[:, dc * TOK + ta * 128:dc * TOK + (ta + nt) * 128], start=(dc == 0), stop=(dc == 7))), reads=["kwb"] + [f"XT{ta + i}" for i in range(nt)], writes=["pA2"])
                            act(okT[b][:, 0:nt * 128], pA[2][:, 0:nt * 128], AF.Copy, [], [f"kokT{b}", "pA2"])
                            P.dma(Q, dstT[dcq * 128:(dcq + 1) * 128, row0(ta):row0(ta) + nt * 128], okT[b][:, 0:nt * 128], reads=[f"kokT{b}"], writes=["kT_all"], key=f"kTst{b}")
            wf = tl([128, 8 * 16], name="kwf"); wfb = tl([128, 8 * 16], BF16, name="kwfb"); bfr = tl([128, 16], name="kbfr")
            lf = [tl([128, 16], name=f"klf{i}") for i in range(2)]
            ckt = [tl([128, 16], name=f"kck{i}") for i in range(2)]
            P.dma(Q, wf[:].rearrange("p (dc n) -> p dc n", dc=8), wv[:, :, 2048:2064], writes=["kwf"])
            act(wfb[:], wf[:], AF.Copy, ["kwf"], ["kwfb"])
            P.dma(Q, bfr[:], bf_d[0:1, :].partition_broadcast(128), writes=["kbfr"])
            for ti in tiles:
                b = n % 2; n += 1
                for dc in range(8):
                    P.op(T, (lambda e, dc=dc, ti=ti: e.matmul(pA[3][:, 0:16], XT[:, dc * TOK + ti * 128:dc * TOK + (ti + 1) * 128], wfb[:, dc * 16:(dc + 1) * 16], start=(dc == 0), stop=(dc == 7))), reads=[f"XT{ti}", "kwfb"], writes=["pA3"])
                tt(lf[b][:], pA[3][:, 0:16], bfr[:], ALU.add, ["kbfr"], [f"klf{b}", "pA3"])
                act(lf[b][:], lf[b][:], AF.Exp, [f"klf{b}"], [f"klf{b}"], scale=-1.0)
                act(lf[b][:], lf[b][:], AF.Ln, [f"klf{b}"], [f"klf{b}"], bias=1.0)
                ts(lf[b][:], lf[b][:], -1.0, None, ALU.mult, None, [f"klf{b}"], [f"klf{b}"])
                P.dma(Q, o_lf[row0(ti):row0(ti) + 128, :], lf[b][:], reads=[f"klf{b}"], key=f"klfo{b}")
                P.dma(Q, lf_all[row0(ti):row0(ti) + 128, :], lf[b][:], reads=[f"klf{b}"], writes=["lf_all"], key=f"klfo{b}")
                if ti == 0:
                    ts(lf[b][:], lf[b][:], mask16[:, 0:1], None, ALU.mult, None, [f"klf{b}", "mask16"], [f"klf{b}"])
                P.op(T, (lambda e, b=b: e.matmul(pA[4][:, 0:16], utm[:], lf[b][:], start=True, stop=True)), reads=["utm", f"klf{b}"], writes=["pA4"])
                P.op(T, (lambda e, b=b: e.matmul(pA[4][:, 16:32], onesm[:], lf[b][:], start=True, stop=True)), reads=["onesm", f"klf{b}"], writes=["pA4"])
                tt(ckt[b][:], pA[4][:, 0:16], ckrun[:], ALU.add, ["ckrun"], [f"kck{b}", "pA4"])
                tt(ckrun[:], pA[4][:, 16:32], ckrun[:], ALU.add, ["ckrun"], ["ckrun", "pA4"])
                P.dma(Q, ck_all[row0(ti):row0(ti) + 128, :], ckt[b][:], reads=[f"kck{b}"], writes=["ck_all"], key=f"kcko{b}")
            for ti in tiles:
                P.dma(Q, x2_all[row0(ti):row0(ti) + 128, :], X[:, ti * D:(ti + 1) * D], reads=[f"X{ti}"], writes=["x2_all"], key="x2st")
            P.barrier()
            s1.close()

        for g in range(NG):
            tiles = list(range(0 if g == 0 else 1, NTL))
            if g == 0:
                P.dma(Q, X[:, 0:D], xmisc, writes=["X0"])
            for ti in range(1, NTL):
                P.dma(Q, X[:, ti * D:(ti + 1) * D], xmain[g * 2048 + (ti - 1) * 128:g * 2048 + ti * 128, :], writes=[f"X{ti}"])
            s5_group(tiles, g)
            if g == NG - 1:
                P.dma(Q, o_ssm_p, carry[:], reads=["carry"], key="o_ssm_p")
            if g == 0:
                P.dma(Q, o_ssm_s, osts[:], reads=["osts"], key="o_ssm_s")
            glu_ln1(tiles)
            if STOP == "ln1":
                return finish([f"X{ti}" for ti in range(NTL)], [(dbg[ti * 128:(ti + 1) * 128, :], X[:, ti * D:(ti + 1) * D]) for ti in range(NTL)])
            r = peer_layer(0, 2, tiles)
            if isinstance(r, tuple):
                return r
            if STOP == "ln2":
                return finish([f"X{ti}" for ti in range(NTL)], [(dbg[ti * 128:(ti + 1) * 128, :], X[:, ti * D:(ti + 1) * D]) for ti in range(NTL)])
            kv_phase(tiles, g)
        SC = 0.125
        tiles1 = list(range(NTL))
        idxo = sb([128, 16], I32, name="idxo"); load(idxo[:], idxown_d, "idxo")
        ckown = sb([128, NTL * 16], name="ckown")
        P.dma(Q, X[:, 0:D], x2_all[0:128, :], reads=["x2_all"], writes=["X0"])
        P.dma(Q, ckown[:, 0:16], ck_all[0:128, :], reads=["ck_all"], writes=["ckown"])
        for ti in range(1, NTL):
            P.dma_fn(G, (lambda e, ti=ti: e.indirect_dma_start(out=X[:, ti * D:(ti + 1) * D], out_offset=None, in_=x2_all[:, :], in_offset=bass.IndirectOffsetOnAxis(ap=idxo[:, ti - 1:ti], axis=0))), reads=["x2_all", "idxo"], writes=[f"X{ti}"], key="gx")
            P.dma_fn(G, (lambda e, ti=ti: e.indirect_dma_start(out=ckown[:, ti * 16:(ti + 1) * 16], out_offset=None, in_=ck_all[:, :], in_offset=bass.IndirectOffsetOnAxis(ap=idxo[:, ti - 1:ti], axis=0))), reads=["ck_all", "idxo"], writes=["ckown"], key="gc")
        for ti in tiles1:
            transpose_tile(ti, False)
        pVT = st.enter_context(nc.psum_tensor("p_vt", [128, 1024], BF16))
        identb2 = sb([128, 128], BF16, name="identb2"); act(identb2[:], ident[:], AF.Copy, ["ident"], ["identb2"])
        NB = 1 + 8 * 16
        P.barrier()
        Obuf = X[:].bitcast(BF16)
        P.op(V, lambda e: e.memset(Obuf[:, 0:D], 0.0), writes=["Obuf"])
        with contextlib.ExitStack() as s1:
            def tl(shape, dt=F32, name=None):
                return sb(shape, dt, name=name, stack=s1)
            ckK = tl([128, NB * 16], name="ckK")
            ckK3 = ckK[:].rearrange("p (b h) -> p b h", h=16)
            P.dma(Q, ckK3[:, 0:1, :], ck_all[0:128, :].rearrange("(b p) h -> p b h", p=128), reads=["ck_all"], writes=["ckK"])
            for g_ in range(8):
                for q4 in range(4):
                    b0_ = 1 + g_ * 16 + q4 * 4
                    P.dma(Q, ckK3[:, b0_:b0_ + 4, :], ck_all[b0_ * 128:(b0_ + 4) * 128, :].rearrange("(b p) h -> p b h", p=128), reads=["ck_all"], writes=["ckK"])
            pen = tl([128, 9], name="pen"); load(pen[:], pen_d, "pen")
            amask = tl([128, 2048], BF16, name="amask")
            cqh = tl([16, TOK], BF16, name="cqh"); cql = tl([16, TOK], BF16, name="cql")
            crefb = tl([128, 4 * 16], name="crefb")
            crefS = tl([128, 2 * 16], name="crefS")
            ckC = tl([128, 2 * 16 * 16], name="ckC")
            cknew = tl([128, 2 * 16], name="cknew")
            identb = tl([128, 128], BF16, name="identb"); act(identb[:], ident[:], AF.Copy, ["ident"], ["identb"])
            s2 = contextlib.ExitStack()
            def tl2(shape, dt=F32, name=None):
                return sb(shape, dt, name=name, stack=s2)
            amf = tl2([128, 2048], name="amf")
            load(amf[:], amask_d, "amf"); act(amask[:], amf[:], AF.Copy, ["amf"], ["amask"])
            for st_ in range(4):
                tl_ = 4 * st_ + 4
                P.dma(Q, cref_d[st_:st_ + 1, :], ckown[127:128, tl_ * 16:(tl_ + 1) * 16], reads=["ckown"], writes=["cref_d"], key="crefw")
            for st_ in range(4):
                P.dma(Q, crefb[:, st_ * 16:(st_ + 1) * 16], cref_d[st_:st_ + 1, :].partition_broadcast(128), reads=["cref_d"], writes=["crefb"])
            lfc = [tl2([128, 16], name=f"lfc{i}") for i in range(2)]
            srun = tl2([128, 16], name="srun")
            nn_ = 0
            for s_ in range(2):
                P.op(V, lambda e: e.memset(srun[:], 0.0), writes=["srun"])
                for blk in range(16):
                    b = nn_ % 2; nn_ += 1
                    P.dma(Q, lfc[b][:], clf_d[s_ * 2048 + blk * 128:s_ * 2048 + (blk + 1) * 128, :], writes=[f"lfc{b}"])
                    P.op(T, (lambda e, b=b: e.matmul(pA[4][:, 0:16], utm[:], lfc[b][:], start=True, stop=True)), reads=["utm", f"lfc{b}"], writes=["pA4"])
                    P.op(T, (lambda e, b=b: e.matmul(pA[4][:, 16:32], onesm[:], lfc[b][:], start=True, stop=True)), reads=["onesm", f"lfc{b}"], writes=["pA4"])
                    o0 = (s_ * 16 + blk) * 16
                    tt(ckC[:, o0:o0 + 16], pA[4][:, 0:16], srun[:], ALU.add, ["srun"], ["ckC", "pA4"])
                    tt(srun[:], pA[4][:, 16:32], srun[:], ALU.add, ["srun"], ["srun", "pA4"])
                b = nn_ % 2; nn_ += 1
                P.op(V, (lambda e, b=b: e.memset(lfc[b][:], 0.0)), writes=[f"lfc{b}"])
                P.dma(Q, lfc[b][0:16, :], lf_all[16 + 16 * s_:32 + 16 * s_, :], reads=["lf_all"], writes=[f"lfc{b}"])
                P.op(T, (lambda e, b=b: e.matmul(pA[4][:, 0:16], utm[:], lfc[b][:], start=True, stop=True)), reads=["utm", f"lfc{b}"], writes=["pA4"])
                tt(cknew[:, s_ * 16:(s_ + 1) * 16], pA[4][:, 0:16], srun[:], ALU.add, ["srun"], ["cknew", "pA4"])
                P.dma(Q, ckn_d[16 * s_:16 * s_ + 16, :], cknew[0:16, s_ * 16:(s_ + 1) * 16], reads=["cknew"], writes=["ckn_d"], key="cknw")
                P.dma(Q, cref_d[4 + s_:5 + s_, :], cknew[15:16, s_ * 16:(s_ + 1) * 16], reads=["cknew"], writes=["cref_d"], key="crefw")
            P.dma(Q, ckown[16:48, 0:16], ckn_d[:, :], reads=["ckn_d"], writes=["ckown"])
            for s_ in range(2):
                P.dma(Q, crefS[:, s_ * 16:(s_ + 1) * 16], cref_d[4 + s_:5 + s_, :].partition_broadcast(128), reads=["cref_d"], writes=["crefS"])
            ckT = tl2([16, TOK], name="ckT"); cqs = tl2([16, TOK], name="cqs"); cqt = tl2([16, TOK], name="cqt")
            for ti in tiles1:
                P.op(T, (lambda e, ti=ti: e.transpose(pA[0][0:16, 0:128], ckown[:, ti * 16:(ti + 1) * 16], ident[:])), reads=["ckown", "ident"], writes=["pA0"])
                P.op(V, (lambda e, ti=ti: e.tensor_copy(ckT[:, ti * 128:(ti + 1) * 128], pA[0][0:16, 0:128])), reads=[], writes=["ckT", "pA0"])
            P.op(V, lambda e: e.memset(cqs[:], 0.0), writes=["cqs"])
            for st_ in range(4):
                c0 = 128 + st_ * 512
                ts(cqs[:, c0:c0 + 512], ckT[:, c0:c0 + 512], ckT[:, c0 + 511:c0 + 512], 1.0 / SC, ALU.subtract, ALU.mult, ["ckT"], ["cqs"])
            for s_ in range(2):
                c0 = 16 + 16 * s_
                ts(cqs[:, c0:c0 + 16], ckT[:, c0:c0 + 16], ckT[:, c0 + 15:c0 + 16], 1.0 / SC, ALU.subtract, ALU.mult, ["ckT"], ["cqs"])
            P.op(V, lambda e: e.tensor_copy(cqh[:], cqs[:]), reads=["cqs"], writes=["cqh"])
            P.op(V, lambda e: e.tensor_copy(cqt[:], cqh[:]), reads=["cqh"], writes=["cqt"])
            tt(cqt[:], cqs[:], cqt[:], ALU.subtract, ["cqs", "cqt"], ["cqt"])
            P.op(V, lambda e: e.tensor_copy(cql[:], cqt[:]), reads=["cqt"], writes=["cql"])
            P.barrier()
            s2.close()
            wst = tl([128, 8 * 64], name="awst")
            wqb = tl([128, 8 * 64], BF16, name="awqb"); wkb = tl([128, 8 * 64], BF16, name="awkb"); wvb = tl([128, 8 * 64], BF16, name="awvb")
            QA = tl([128, TOK], BF16, name="QA")
            KAo = tl([128, 2048], BF16, name="KAo")
            VAo = tl([128, 16 * 65], BF16, name="VAo")
            KA = [tl([128, 2048], BF16, name=f"KA{i}") for i in range(2)]
            VA = [tl([128, 16 * 65], BF16, name=f"VA{i}") for i in range(2)]
            VT = [tl([64, 2048], BF16, name=f"VT{i}") for i in range(2)]

            PT = [tl([128, 512], BF16, name=f"PT{i}") for i in range(2)]
            bia = tl([128, 16], name="bia")
            kcf = tl([64, 2048], name="kcf"); vcf = tl([128, 16 * 64], name="vcf")
            KN = tl([128, 16], BF16, name="KN"); VNT = tl([64, 16], BF16, name="VNT"); VN = tl([128, 65], BF16, name="VN")
            osm = tl([16, 64], BF16, name="osm"); rds = tl([16, 1], name="rds")
            P.op(V, lambda e: e.memset(KN[64:128, :], 0.0), writes=["KN"])
            P.op(V, lambda e: e.memset(KN[64:66, :], 1.0), writes=["KN"])
            P.op(V, lambda e: e.memset(VN[:], 1.0), writes=["VN"])
            accS = tl([128, 16 * 65], name="accS")
            rden = tl([128, 16], name="rden")
            for t_ in (KAo, KA[0], KA[1]):
                P.op(V, (lambda e, t_=t_: e.memset(t_[64:128, :], 0.0)), writes=["KAo", "KA0", "KA1"])
                P.op(V, (lambda e, t_=t_: e.memset(t_[64:66, :], 1.0)), writes=["KAo", "KA0", "KA1"])
            for t_ in (VAo, VA[0], VA[1]):
                P.op(V, (lambda e, t_=t_: e.memset(t_[:], 1.0)), writes=["VAo", "VA0", "VA1"])
            P.op(V, lambda e: e.memset(QA[:], 0.0), writes=["QA"])
            wqv = wq_d.rearrange("p (dc n) -> p dc n", dc=8); wkv = wkvf_d.rearrange("p (dc n) -> p dc n", dc=8)
            wst3 = wst[:].rearrange("p (dc n) -> p dc n", dc=8)
            NH = int(os.environ.get("MK_NH", "16"))
            stepc = [0]

            def attn_step(kt, kcols, nk, qcols, nq, vt, vblk, bias_ap, accs, first, last, maskj=None, rd=()):
                b = stepc[0] % 2; stepc[0] += 1
                psn = f"pA{1 + b}"; ps_ = pA[1 + b]
                P.op(T, (lambda e: e.matmul(ps_[0:nk, 0:nq], kt[0:66, kcols:kcols + nk], QA[0:66, qcols:qcols + nq], start=True, stop=(maskj is None))), reads=list(rd) + ["QA"], writes=[psn])
                if maskj is not None:
                    P.op(T, (lambda e: e.matmul(ps_[0:nk, 0:nq], identb[0:nk, 0:nk], amask[0:nk, maskj * 512:maskj * 512 + nq], start=False, stop=True)), reads=["identb", "amask"], writes=[psn])
                act(PT[b][0:nk, 0:nq], ps_[0:nk, 0:nq], AF.Exp, ["bia"], [f"PT{b}", psn], bias=bias_ap, scale=SC)
                for qb in range((nq + 127) // 128):
                    w = min(128, nq - qb * 128)
                    pacc, paccn = accs[qb]
                    P.op(T, (lambda e, qb=qb, w=w, pacc=pacc: e.matmul(pacc[0:w, 0:65], PT[b][0:nk, qb * 128:qb * 128 + w], vt[0:nk, vblk * 65:(vblk + 1) * 65], start=first, stop=last)), reads=[f"PT{b}"] + list(rd), writes=[paccn])

            accs4 = [(pA[3 + i], f"pA{3 + i}") for i in range(4)]
            for h in range(NH):
                dcq = h // 2; ro = 64 * (h % 2)
                for (wsrc, c0, dstb, dn) in ((wqv, h * 64, wqb, "awqb"), (wkv, h * 64, wkb, "awkb"), (wkv, 1024 + h * 64, wvb, "awvb")):
                    P.dma(Q, wst3, wsrc[:, :, c0:c0 + 64], writes=["awst"])
                    act(dstb[:], wst[:], AF.Copy, ["awst"], [dn])
                for (t0, nt) in [(0, 4), (4, 4), (8, 4), (12, 4), (16, 1)]:
                    for dc in range(8):
                        P.op(T, (lambda e, dc=dc, t0=t0, nt=nt: e.matmul(pA[1][0:64, 0:nt * 128], wqb[:, dc * 64:(dc + 1) * 64], XT[:, dc * TOK + t0 * 128:dc * TOK + (t0 + nt) * 128], start=(dc == 0), stop=(dc == 7))), reads=["awqb"] + [f"XT{t0 + i}" for i in range(nt)], writes=["pA1"])
                    act(QA[0:64, t0 * 128:(t0 + nt) * 128], pA[1][0:64, 0:nt * 128], AF.Copy, [], ["QA", "pA1"])
                P.dma(Q, QA[64:65, :], cqh[h:h + 1, :], reads=["cqh"], writes=["QA"], key="qa64")
                P.dma(Q, QA[65:66, :], cql[h:h + 1, :], reads=["cql"], writes=["QA"], key="qa65")
                for (t0, nt) in [(1, 4), (5, 4), (9, 4), (13, 4)]:
                    for dc in range(8):
                        P.op(T, (lambda e, dc=dc, t0=t0, nt=nt: e.matmul(pA[2][0:64, 0:nt * 128], wkb[:, dc * 64:(dc + 1) * 64], XT[:, dc * TOK + t0 * 128:dc * TOK + (t0 + nt) * 128], start=(dc == 0), stop=(dc == 7))), reads=["awkb"] + [f"XT{t0 + i}" for i in range(nt)], writes=["pA2"])
                    act(KAo[0:64, (t0 - 1) * 128:(t0 - 1 + nt) * 128], pA[2][0:64, 0:nt * 128], AF.Copy, [], ["KAo", "pA2"])
                for ti in range(1, NTL):
                    for dc in range(8):
                        P.op(T, (lambda e, dc=dc, ti=ti: e.matmul(pA[2][:, 0:64], XT[:, dc * TOK + ti * 128:dc * TOK + (ti + 1) * 128], wvb[:, dc * 64:(dc + 1) * 64], start=(dc == 0), stop=(dc == 7))), reads=["awvb", f"XT{ti}"], writes=["pA2"])
                    act(VAo[:, (ti - 1) * 65:(ti - 1) * 65 + 64], pA[2][:, 0:64], AF.Copy, [], ["VAo", "pA2"])
                ngrp = 0
                for gk in [-1] + list(range(NG)) + [8]:
                    if gk == 8:
                        kt_, vt_, rdn = KAo, VAo, ["KAo", "VAo"]
                    else:
                        bb = ngrp % 2; ngrp += 1
                        kt_, vt_, rdn = KA[bb], VA[bb], [f"KA{bb}", f"VA{bb}"]
                        if gk == -1:
                            nblk = 1; r0 = 0; nkeys = 128
                        else:
                            nblk = 16; r0 = 128 + gk * 2048; nkeys = 2048
                        P.dma(Q, kt_[0:64, 0:nkeys], kT_all[dcq * 128 + ro:dcq * 128 + ro + 64, r0:r0 + nkeys], reads=["kT_all"], writes=[f"KA{bb}"])
                        P.dma(Q, VT[bb][0:64, 0:nkeys], vT_all[dcq * 128 + ro:dcq * 128 + ro + 64, r0:r0 + nkeys], reads=["kT_all"], writes=[f"VT{bb}"])
                        for kb in range(nblk):
                            P.op(T, (lambda e, kb=kb, bb=bb: e.transpose(pVT[:, kb * 64:(kb + 1) * 64], VT[bb][0:64, kb * 128:(kb + 1) * 128], identb[0:64, 0:64])), reads=[f"VT{bb}", "identb"], writes=["pA7"])
                        P.op(V, (lambda e, vt_=vt_, nblk=nblk: e.tensor_copy(vt_[:].rearrange("p (b c) -> p b c", c=65)[:, 0:nblk, 0:64], pVT[:, 0:nblk * 64].rearrange("p (b c) -> p b c", c=64))), reads=[], writes=[f"VA{bb}", "pA7"])
                    for st_ in range(4):
                        qc = 128 + st_ * 512
                        crs = crefb[:, st_ * 16 + h:st_ * 16 + h + 1]
                        if gk == 8:
                            nblk = 4 * st_ + 4
                            ts(bia[:, 0:nblk], ckown[:].rearrange("p (t h) -> p t h", h=16)[:, 1:1 + nblk, h], crs, -1.0, ALU.subtract, ALU.mult, ["ckown", "crefb"], ["bia"])
                        else:
                            b0 = 0 if gk == -1 else 1 + gk * 16
                            ts(bia[:, 0:nblk], ckK[:].rearrange("p (b h) -> p b h", h=16)[:, b0:b0 + nblk, h], crs, -1.0, ALU.subtract, ALU.mult, ["ckK", "crefb"], ["bia"])
                            if gk >= 0:
                                ts(bia[:, 0:nblk], bia[:, 0:nblk], pen[:, gk:gk + 1], None, ALU.add, None, ["bia", "pen"], ["bia"])
                        for kb in range(nblk):
                            nk = 16 if gk == -1 else 128
                            mj = None
                            if gk == 8 and kb >= 4 * st_:
                                mj = kb - 4 * st_
                            attn_step(kt_, kb * 128, nk, qc, 512, vt_, kb, bia[0:nk, kb:kb + 1], accs4, kb == 0, kb == nblk - 1, maskj=mj, rd=rdn)
                        for qb in range(4):
                            pacc, paccn = accs4[qb]
                            col = (st_ * 4 + qb) * 65
                            if gk == -1:
                                P.op(V, (lambda e, pacc=pacc, col=col: e.tensor_copy(accS[:, col:col + 65], pacc[:, 0:65])), reads=[], writes=["accS", paccn])
                            else:
                                tt(accS[:, col:col + 65], accS[:, col:col + 65], pacc[:, 0:65], ALU.add, [], ["accS", paccn])
                for s_ in range(2):
                    qc = 16 + 16 * s_
                    crs = crefS[:, s_ * 16 + h:s_ * 16 + h + 1]
                    P.dma(Q, kcf[:], ckT_d[(s_ * 16 + h) * 64:(s_ * 16 + h + 1) * 64, :], writes=["kcf"])
                    act(KA[0][0:64, :], kcf[:], AF.Copy, ["kcf"], ["KA0"])
                    for q4 in range(4):
                        P.dma(Q, vcf[:].rearrange("p (b c) -> p b c", c=64)[:, q4 * 4:(q4 + 1) * 4, :], cv_d[s_ * 2048 + q4 * 512:s_ * 2048 + (q4 + 1) * 512, h * 64:(h + 1) * 64].rearrange("(b p) c -> p b c", p=128), writes=["vcf"])
                    P.op(V, lambda e: e.tensor_copy(VA[0][:].rearrange("p (b c) -> p b c", c=65)[:, :, 0:64], vcf[:].rearrange("p (b c) -> p b c", c=64)), reads=["vcf"], writes=["VA0"])
                    ts(bia[:, 0:16], ckC[:].rearrange("p (s b h) -> p s b h", s=2, b=16)[:, s_, :, h], crs, -1.0, ALU.subtract, ALU.mult, ["ckC", "crefS"], ["bia"])
                    acc1 = [(pA[3], "pA3")]
                    for kb in range(16):
                        attn_step(KA[0], kb * 128, 128, qc, 16, VA[0], kb, bia[:, kb:kb + 1], acc1, kb == 0, False, rd=["KA0", "VA0"])
                    P.dma(Q, KN[0:64, :], kT_all[dcq * 128 + ro:dcq * 128 + ro + 64, qc:qc + 16], reads=["kT_all"], writes=["KN"])
                    P.dma(Q, VNT[:, :], vT_all[dcq * 128 + ro:dcq * 128 + ro + 64, qc:qc + 16], reads=["kT_all"], writes=["VNT"])
                    P.op(T, lambda e: e.transpose(pVT[0:16, 0:64], VNT[0:64, 0:16], identb[0:64, 0:64]), reads=["VNT", "identb"], writes=["pA7"])
                    P.op(V, lambda e: e.tensor_copy(VN[0:16, 0:64], pVT[0:16, 0:64]), reads=[], writes=["VN", "pA7"])
                    ts(bia[0:16, 0:1], cknew[0:16, s_ * 16 + h:s_ * 16 + h + 1], crs[0:16, :], -1.0, ALU.subtract, ALU.mult, ["cknew", "crefS"], ["bia"])
                    attn_step(KN, 0, 16, qc, 16, VN, 0, bia[0:16, 0:1], acc1, False, True, maskj=0, rd=["KN", "VN"])
                    P.op(V, lambda e: e.reciprocal(rds[:], pA[3][0:16, 64:65]), reads=[], writes=["rds", "pA3"])
                    ts(osm[:], pA[3][0:16, 0:64], rds[:, 0:1], None, ALU.mult, None, ["rds"], ["osm", "pA3"])
                    P.dma(Q, Obuf[qc:qc + 16, h * 64:(h + 1) * 64], osm[:], reads=["osm"], writes=["Obuf"], key="osmw")
                a3 = accS[:].rearrange("p (t c) -> p t c", c=65)
                P.op(V, lambda e: e.reciprocal(rden[:], a3[:, :, 64]), reads=["accS"], writes=["rden"])
                O3 = Obuf[:, 0:NTL * D].rearrange("p (t d) -> p t d", d=D)
                P.op(V, (lambda e, h=h: e.tensor_tensor(out=O3[:, 1:NTL, h * 64:(h + 1) * 64], in0=a3[:, :, 0:64], in1=rden[:, :, None].to_broadcast([128, 16, 64]), op=ALU.mult)), reads=["accS", "rden"], writes=["Obuf"])
            P.barrier()
        for ti in tiles1:
            for half in range(2):
                for k in range(4):
                    dc = half * 4 + k
                    P.op(T, (lambda e, dc=dc, k=k, ti=ti: e.transpose(pVT[:, k * 128:(k + 1) * 128], Obuf[:, ti * D + dc * 128:ti * D + (dc + 1) * 128], identb2[:])), reads=["Obuf", "identb2"], writes=["pA7"])
                for k in range(4):
                    dc = half * 4 + k
                    act(XT[:, dc * TOK + ti * 128:dc * TOK + (ti + 1) * 128], pVT[:, k * 128:(k + 1) * 128], AF.Copy, [], [f"XT{ti}", "pA7"])
        P.barrier()
        P.dma(Q, X[:, 0:D], x2_all[0:128, :], reads=["x2_all"], writes=["X0"])
        for ti in range(1, NTL):
            P.dma_fn(G, (lambda e, ti=ti: e.indirect_dma_start(out=X[:, ti * D:(ti + 1) * D], out_offset=None, in_=x2_all[:, :], in_offset=bass.IndirectOffsetOnAxis(ap=idxo[:, ti - 1:ti], axis=0))), reads=["x2_all", "idxo"], writes=[f"X{ti}"], key="gx")
        with contextlib.ExitStack() as s1:
            wof = sb([128, 8 * 512], name="wof", stack=s1); wob = sb([128, 8 * 512], BF16, name="wob", stack=s1)
            load_ln(4, 5)
            wov = wo_d.rearrange("p (dc n) -> p dc n", dc=8)
            for cb in range(2):
                P.dma(Q, wof[:].rearrange("p (dc n) -> p dc n", dc=8), wov[:, :, cb * 512:(cb + 1) * 512], writes=["wof"])
                act(wob[:], wof[:], AF.Copy, ["wof"], ["wob"])
                for ti in tiles1:
                    for dc in range(8):
                        P.op(T, (lambda e, dc=dc, ti=ti: e.matmul(pA[5][:], XT[:, dc * TOK + ti * 128:dc * TOK + (ti + 1) * 128], wob[:, dc * 512:(dc + 1) * 512], start=(dc == 0), stop=(dc == 7))), reads=[f"XT{ti}", "wob"], writes=["pA5"])
                    xs_ = X[:, ti * D + cb * 512:ti * D + (cb + 1) * 512]
                    P.op(V, (lambda e, xs_=xs_: e.scalar_tensor_tensor(out=xs_, in0=xs_, scalar=DN_ALPHA, in1=pA[5][:], op0=ALU.mult, op1=ALU.add)), reads=[f"X{ti}"], writes=[f"X{ti}", "pA5"])
            for ti in tiles1:
                layer_norm_tile(ti)
            P.barrier()
        if STOP == "l1ln1":
            return finish([f"X{ti}" for ti in range(NTL)], [(dbg[ti * 128:(ti + 1) * 128, :], X[:, ti * D:(ti + 1) * D]) for ti in range(NTL)])
        r = peer_layer(1, 6, tiles1)
        if isinstance(r, tuple):
            return r
        for ti in tiles1:
            P.dma(Q, o_y[ti * 128:(ti + 1) * 128, :], X[:, ti * D:(ti + 1) * D], reads=[f"X{ti}"], key="oy")
        P.finalize()
    return nc, P


def _prep_common(inputs):
    f = np.float32
    lam_re = inputs["ssm_lam_re"][0]; lam_im = inputs["ssm_lam_im"][0]; log_dt = inputs["ssm_log_dt"][0]
    b_re = inputs["ssm_b_re"][0]; b_im = inputs["ssm_b_im"][0]; c_re = inputs["ssm_c_re"][0]; c_im = inputs["ssm_c_im"][0]

    def L2(a):
        return np.ascontiguousarray(a.reshape(32, 2, 64).transpose(1, 2, 0).reshape(128, 32)).astype(f)

    def L1(a):
        t = a.reshape(8, 8, 64)
        t = np.broadcast_to(t[:, :, None, :], (8, 8, 16, 64))
        return np.ascontiguousarray(t.transpose(1, 2, 0, 3).reshape(128, 512)).astype(f)

    def L1b(a):
        t = a.reshape(8, 8, 64, 16)
        return np.ascontiguousarray(t.transpose(1, 3, 0, 2).reshape(128, 512)).astype(f)

    def L2c(a):
        t = a.reshape(32, 2, 16, 64)
        return np.ascontiguousarray(t.transpose(1, 3, 0, 2).reshape(128, 512)).astype(f)

    def wfm(w):
        n = w.shape[1]
        return np.ascontiguousarray(w.reshape(8, 128, n).transpose(1, 0, 2).reshape(128, 8 * n)).astype(f)

    ldt = np.broadcast_to(log_dt[:, None], (64, 64))
    rowmask = np.zeros((128, 8), f)
    for gi in range(8):
        rowmask[gi * 16:(gi + 1) * 16, gi] = 1.0
    com = {
        "ident": np.eye(128, dtype=f),
        "tvec": np.ascontiguousarray(np.broadcast_to(np.arange(1, 129, dtype=f)[None, :], (128, 128))),
        "iota128": np.ascontiguousarray(np.broadcast_to(np.arange(0, 128, dtype=f)[None, :], (128, 128))),
        "rowmask": rowmask,
        "lamre2": L2(lam_re), "lamim2": L2(lam_im), "logdt2": L2(ldt),
        "lamre1": L1(lam_re), "lamim1": L1(lam_im), "logdt1": L1(ldt),
        "bre1": L1b(b_re), "bim1": L1b(b_im), "cre2": L2c(c_re), "cim2": L2c(c_im),
        "dfm": np.ascontiguousarray(inputs["ssm_d"][0].reshape(8, 128).T).astype(f),
        "wglu": wfm(inputs["w_glu"][0]),
        "lnv": np.ascontiguousarray(np.stack([inputs["ln1_g"][0], inputs["ln1_b"][0], inputs["ln2_g"][0], inputs["ln2_b"][0],
                                              inputs["ln1_g"][1], inputs["ln1_b"][1], inputs["ln2_g"][1], inputs["ln2_b"][1]], 0)).astype(f),
        "wkvf": wfm(inputs["w_kvf"]),
        "wq": wfm(inputs["w_q"][0]), "wo": wfm(inputs["w_o"][0]),
        "utm": np.triu(np.ones((128, 128), f)),
        "onesm": np.ones((128, 128), f),
        "mask16": (np.arange(128) < 16).astype(f)[:, None].copy(),
        "bf": np.ascontiguousarray(inputs["b_f"][None, :]).astype(f),
    }
    am = np.zeros((128, 4, 512), f)
    kk = np.arange(128)[:, None]; qq = np.arange(512)[None, :]
    for j in range(4):
        am[:, j, :] = np.where(128 * j + kk > qq, -30000.0, 0.0)
    com["amask"] = np.ascontiguousarray(am.reshape(128, 2048))
    com["_xmain"] = np.ascontiguousarray(inputs["x_prompt"][0]).astype(f)
    for l in range(2):
        ut = inputs["peer_u"][l].reshape(128, 128, 8, 128).transpose(0, 3, 2, 1)
        com[f"ufull{l}"] = np.ascontiguousarray(ut.reshape(128 * 128, D)).astype(f)
        com[f"vfull{l}"] = np.ascontiguousarray(inputs["peer_v"][l]).astype(f)
        if os.environ.get("MK_SMALLTB", "") == "1":
            com[f"ufull{l}"] = com[f"ufull{l}"][:128]; com[f"vfull{l}"] = com[f"vfull{l}"][:128]
        com[f"wpq{l}"] = wfm(inputs["peer_w_q"][l])
        com[f"pk{l}"] = np.ascontiguousarray(np.concatenate([inputs["peer_k1"][l].T, inputs["peer_k2"][l].T], 1)).astype(f)
    return com


def _prep_core(inputs, com, c):
    f = np.float32
    m = dict(com)
    xs = np.zeros((128, D), f)
    xs[0:16] = inputs["meta_tokens"]
    xs[16:32] = inputs["x_sample"][2 * c]
    xs[32:48] = inputs["x_sample"][2 * c + 1]
    m["xmisc"] = xs
    pen = np.zeros((128, 9), f)
    for g in range(8):
        if g >= c:
            pen[:, g] = NEG
    m["pen"] = pen
    m["idxown"] = np.ascontiguousarray((128 + 2048 * c + np.arange(16)[None, :] * 128 + np.arange(128)[:, None]).astype(np.int32))
    ck = inputs["cache_k"][2 * c:2 * c + 2]
    m["cache_kT"] = np.ascontiguousarray(ck.transpose(0, 2, 3, 1).reshape(2 * 16 * 64, 2048)).astype(f)
    m["cache_v"] = np.ascontiguousarray(inputs["cache_v"][2 * c:2 * c + 2].reshape(2 * 2048, D)).astype(f)
    m["cache_lf"] = np.ascontiguousarray(inputs["cache_logf"][2 * c:2 * c + 2].reshape(2 * 2048, 16)).astype(f)
    m["xmain"] = com["_xmain"]

    def stl(a):
        t = a.reshape(2, 32, 2, 64)
        return np.ascontiguousarray(t.transpose(2, 3, 0, 1).reshape(128, 64)).astype(f)
    m["st_re"] = stl(inputs["state_ssm_re"][2 * c:2 * c + 2, 0])
    m["st_im"] = stl(inputs["state_ssm_im"][2 * c:2 * c + 2, 0])
    return m


def kernel(**inputs):
    f = np.float32
    inputs = {k: np.asarray(v) for k, v in inputs.items()}
    com = _prep_common(inputs)
    in_maps = [_prep_core(inputs, com, c) for c in range(NCORES)]
    for m in in_maps:
        m.pop("_xmain", None)
    nc, P = build_program()
    res = run_bass_kernel_spmd(nc, in_maps, core_ids=list(range(NCORES)))
    R = res.results
    return assemble(R)


def assemble(R):
    f = np.float32

    def unL2(a):
        return a.reshape(2, 64, 32).transpose(2, 0, 1).reshape(64, 64)
    y_prompt = np.zeros((1, 16384, D), f); y_sample = np.zeros((16, 16, D), f)
    sp = R[0]["o_ssm_p"]
    ssm_re_p = unL2(sp[:, 0:32])[None, None].astype(f); ssm_im_p = unL2(sp[:, 32:64])[None, None].astype(f)
    k_p = np.zeros((1, 16400, 16, 64), f); v_p = np.zeros((1, 16400, 16, 64), f); lf_p = np.zeros((1, 16400, 16), f)
    ssm_re_s = np.zeros((16, 1, 64, 64), f); ssm_im_s = np.zeros((16, 1, 64, 64), f)
    k_s = np.zeros((16, 16, 16, 64), f); v_s = np.zeros((16, 16, 16, 64), f); lf_s = np.zeros((16, 16, 16), f)
    ok = R[0]["o_k"]; ov = R[0]["o_v"]; olf = R[0]["o_lf"]
    k_p[0, 0:16] = ok[0:16].reshape(16, 16, 64); v_p[0, 0:16] = ov[0:16].reshape(16, 16, 64); lf_p[0, 0:16] = olf[0:16]
    k_p[0, 16:] = ok[128:].reshape(16384, 16, 64); v_p[0, 16:] = ov[128:].reshape(16384, 16, 64); lf_p[0, 16:] = olf[128:]
    for c in range(NCORES):
        ss = R[c]["o_ssm_s"]
        for s in range(2):
            ssm_re_s[2 * c + s, 0] = unL2(ss[:, s * 64:s * 64 + 32])
            ssm_im_s[2 * c + s, 0] = unL2(ss[:, s * 64 + 32:s * 64 + 64])
        okc = R[c]["o_k"]; ovc = R[c]["o_v"]; olfc = R[c]["o_lf"]; oy = R[c]["o_y"]
        y_prompt[0, 2048 * c:2048 * (c + 1)] = oy[128:]
        for s in range(2):
            r0 = 16 + 16 * s
            k_s[2 * c + s] = okc[r0:r0 + 16].reshape(16, 16, 64); v_s[2 * c + s] = ovc[r0:r0 + 16].reshape(16, 16, 64)
            lf_s[2 * c + s] = olfc[r0:r0 + 16]; y_sample[2 * c + s] = oy[r0:r0 + 16]
    return (y_prompt, y_sample, ssm_re_p, ssm_im_p, k_p, v_p, lf_p, ssm_re_s, ssm_im_s, k_s, v_s, lf_s)
```

```python
import contextlib
import math
import os
import numpy as np
import concourse.bass as bass
import concourse.mybir as mybir
from concourse.bass_utils import run_bass_kernel_spmd

F32 = mybir.dt.float32
BF16 = mybir.dt.bfloat16
I32 = mybir.dt.int32
U32 = mybir.dt.uint32
AF = mybir.ActivationFunctionType
ALU = mybir.AluOpType
AX = mybir.AxisListType
NCORES = 8
D = 1024
NTL = 17
TOK = NTL * 128
TWO_PI = 2.0 * math.pi
DN_ALPHA = 4.0 ** 0.25
LN_EPS = 1e-5
NEG = -1.0e30
V = "vector"; S = "scalar"; G = "gpsimd"; T = "tensor"; Q = "sync"


ENGS = ["tensor", "vector", "scalar", "gpsimd", "sync"]


class Buf:
    __slots__ = ("name", "writer", "readers")

    def __init__(self, name):
        self.name = name
        self.writer = None
        self.readers = []


class Op:
    __slots__ = ("eng", "fn", "deps", "kind", "key", "sig", "idx")

    def __init__(self, eng, fn, kind, key):
        self.eng = eng
        self.fn = fn
        self.kind = kind
        self.key = key
        self.deps = set()
        self.sig = None


class Prog:
    def __init__(self, nc, same_engine_sync=True):
        self.nc = nc
        self.ops = []
        self.bufs = {}
        self.same_engine_sync = same_engine_sync

    def buf(self, name):
        b = self.bufs.get(name)
        if b is None:
            b = self.bufs[name] = Buf(name)
        return b

    def _bl(self, lst):
        out = []
        for x in lst or []:
            out.append(self.buf(x) if isinstance(x, str) else x)
        return out

    def _add(self, op, reads, writes, is_barrier=False):
        i = len(self.ops)
        op.idx = i
        reads = list(reads or [])
        writes = list(writes or [])
        if is_barrier:
            writes.append("__ALL__")
        else:
            reads.append("__ALL__")
        for b in self._bl(reads):
            if b.writer is not None:
                op.deps.add(b.writer)
            b.readers.append(i)
        for b in self._bl(writes):
            if b.writer is not None:
                op.deps.add(b.writer)
            for r in b.readers:
                if r != i:
                    op.deps.add(r)
            b.writer = i
            b.readers = []
        self.ops.append(op)
        return i

    def op(self, eng, fn, reads=(), writes=()):
        return self._add(Op(eng, fn, "c", None), reads, writes)

    def dma(self, eng, out, in_, reads=(), writes=(), key=None):
        ws = self._bl(writes)
        rs = self._bl(reads)
        if key is None:
            key = (ws[0].name if ws else rs[0].name)
        return self._add(Op(eng, lambda e: e.dma_start(out=out, in_=in_), "d", key), rs, ws)

    def dma_fn(self, eng, fn, reads=(), writes=(), key=None):
        ws = self._bl(writes)
        rs = self._bl(reads)
        if key is None:
            key = (ws[0].name if ws else rs[0].name)
        return self._add(Op(eng, fn, "d", key), rs, ws)

    def barrier(self):
        return self._add(Op("vector", self.barrier_fn, "c", None), [], [], is_barrier=True)

    def cc(self, fn, reads=(), writes=(), key="cc"):
        return self._add(Op("gpsimd", fn, "cc", key), reads, writes)

    def finalize(self, final_wait_bufs=()):
        nc = self.nc
        ops = self.ops
        needed = [False] * len(ops)
        for o in ops:
            for d in o.deps:
                if ops[d].kind == "c" and o.kind == "c" and ops[d].eng == o.eng and \
                        (o.eng == "tensor" or not self.same_engine_sync):
                    continue
                needed[d] = True
        cnt = {}
        rot = {}
        dslot = {}
        NDS = 28
        NROT = 8
        semnames = set()
        for i, o in enumerate(ops):
            if o.kind == "c":
                if not needed[i]:
                    continue
                rk = rot.get(o.eng, 0)
                rot[o.eng] = rk + 1
                sn = "e_" + o.eng + str(rk % NROT)
                cnt[sn] = cnt.get(sn, 0) + 1
                o.sig = (sn, cnt[sn], 1)
            elif o.kind == "d":
                kk = (o.eng, str(o.key))
                if kk not in dslot:
                    n_e = sum(1 for q in dslot if q[0] == o.eng)
                    dslot[kk] = n_e % (NDS if o.eng == "sync" else 4)
                sn = "d_" + o.eng + str(dslot[kk])
                cnt[sn] = cnt.get(sn, 0) + 16
                o.sig = (sn, cnt[sn], 16)
            else:
                sn = "c_" + str(o.key)
                cnt[sn] = cnt.get(sn, 0) + 1
                o.sig = (sn, cnt[sn], 1)
            semnames.add(o.sig[0])
        self.sem_final = dict(cnt)
        semnames = sorted(semnames)
        self.nsems = len(semnames)
        import contextlib
        stack = contextlib.ExitStack()
        sems = {}
        for sn in semnames:
            sems[sn] = stack.enter_context(nc.semaphore(sn[:40]))
        per_eng = {e: [] for e in ENGS}
        for o in ops:
            per_eng[o.eng].append(o)
        self.n_waits = 0
        prog = self

        def body_for(engname):
            lst = per_eng[engname]

            def body(e):
                waited = {}
                for o in lst:
                    for d in sorted(o.deps):
                        po = ops[d]
                        if po.sig is None:
                            continue
                        sn, val, _ = po.sig
                        if po.kind == "c" and po.eng == engname and o.kind == "c" and \
                                (engname == "tensor" or not prog.same_engine_sync):
                            continue
                        if waited.get(sn, 0) >= val:
                            continue
                        waited[sn] = val
                        e.wait_ge(sems[sn], val)
                        prog.n_waits += 1
                    ins = o.fn(e)
                    if o.sig is not None:
                        if o.kind == "cc":
                            ins.then_inc(sems[o.sig[0]])
                        else:
                            ins.then_inc(sems[o.sig[0]], o.sig[2])
                fin = {}
                for o in lst:
                    if o.kind != "c":
                        fin[o.sig[0]] = max(fin.get(o.sig[0], 0), o.sig[1])
                for sn, val in fin.items():
                    if waited.get(sn, 0) < val:
                        e.wait_ge(sems[sn], val)
            return body

        with stack:
            with nc.Block() as block:
                for en in ENGS:
                    if per_eng[en]:
                        getattr(block, en)(body_for(en))


def build_program():
    nc = bass.Bass("TRN2", target_bir_lowering=False)
    STOP = os.environ.get("MK_STOP", "")
    NCH = int(os.environ.get("MK_NCH", "128"))

    def din(name, shape, dt=F32):
        return nc.dram_tensor(name, list(shape), dt, kind="ExternalInput").ap()

    def dout(name, shape, dt=F32):
        return nc.dram_tensor(name, list(shape), dt, kind="ExternalOutput").ap()

    def dint(name, shape, dt=F32):
        return nc.dram_tensor(name, list(shape), dt)

    xmisc = din("xmisc", [128, D]); xmain = din("xmain", [8 * 2048, D])
    NROW = 128 + 8 * 2048
    ident_d = din("ident", [128, 128]); tvec_d = din("tvec", [128, 128]); rowmask_d = din("rowmask", [128, 8])
    iota_d = din("iota128", [128, 128])
    lamre2_d = din("lamre2", [128, 32]); lamim2_d = din("lamim2", [128, 32]); logdt2_d = din("logdt2", [128, 32])
    lamre1_d = din("lamre1", [128, 512]); lamim1_d = din("lamim1", [128, 512]); logdt1_d = din("logdt1", [128, 512])
    bre1_d = din("bre1", [128, 512]); bim1_d = din("bim1", [128, 512])
    cre2_d = din("cre2", [128, 512]); cim2_d = din("cim2", [128, 512])
    dfm_d = din("dfm", [128, 8])
    st_re_d = din("st_re", [128, 64]); st_im_d = din("st_im", [128, 64])
    wglu_d = din("wglu", [128, 8 * 2048])
    lnv_d = din("lnv", [8, D])
    wpq_d = [din(f"wpq{l}", [128, 8 * 2048]) for l in range(2)]
    pk_d = [din(f"pk{l}", [128, 256]) for l in range(2)]
    SMALLTB = os.environ.get("MK_SMALLTB", "") == "1"
    TBR = 128 if SMALLTB else 128 * 128
    umy_d = [din(f"ufull{l}", [TBR, D]) for l in range(2)]
    vmy_d = [din(f"vfull{l}", [TBR, D]) for l in range(2)]
    NOCC = os.environ.get("MK_NOCC", "") == "1"
    wkvf_d = din("wkvf", [128, 8 * 2064])
    wq_d = din("wq", [128, 8 * 1024]); wo_d = din("wo", [128, 8 * 1024])
    utm_d = din("utm", [128, 128]); onesm_d = din("onesm", [128, 128]); mask16_d = din("mask16", [128, 1])
    amask_d = din("amask", [128, 4 * 512])
    pen_d = din("pen", [128, 9])
    idxown_d = din("idxown", [128, 16], I32)
    ckT_d = din("cache_kT", [2 * 16 * 64, 2048])
    cv_d = din("cache_v", [2 * 2048, D])
    clf_d = din("cache_lf", [2 * 2048, 16])
    bf_d = din("bf", [1, 16])

    o_ssm_p = dout("o_ssm_p", [128, 64])
    o_ssm_s = dout("o_ssm_s", [128, 128])
    o_k = dout("o_k", [NROW, D]); o_v = dout("o_v", [NROW, D]); o_lf = dout("o_lf", [NROW, 16])
    o_y = dout("o_y", [TOK, D])
    x2_all = dint("x2_all", [NROW, D])
    kT_all = dint("kT_all", [8 * 128, NROW], BF16)
    vT_all = dint("vT_all", [8 * 128, NROW], BF16)
    ck_all = dint("ck_all", [NROW, 16])
    lf_all = dint("lf_all", [NROW, 16])
    cref_d = dint("cref_d", [8, 16])
    ckn_d = dint("ckn_d", [32, 16])
    dbg = dout("dbg", [TOK, D]) if STOP else None

    s5c_cos = dint("s5c_cos", [128, 4096]); s5c_sin = dint("s5c_sin", [128, 4096])
    s5c_wbu = dint("s5c_wbu", [128, 8192], BF16); s5c_wc = dint("s5c_wc", [128, 8192], BF16)
    tb_all = [[dint(f"tball{l}{k}", [128 * 128, D], BF16) for k in range(2)] for l in range(2)]

    P = Prog(nc)
    with contextlib.ExitStack() as st:
        cnt = [0]

        def sb(shape, dt=F32, name=None, stack=st):
            cnt[0] += 1
            return stack.enter_context(nc.sbuf_tensor(f"s{cnt[0]}_" + (name or "t"), list(shape), dt))

        def ps(shape, dt=F32, name=None, stack=st):
            cnt[0] += 1
            return stack.enter_context(nc.psum_tensor("p_" + (name or f"ps{cnt[0]}"), list(shape), dt))

        def load(t_ap, d_ap, name, eng=Q):
            P.dma(eng, t_ap, d_ap, writes=[name])

        def tt(out, a, b, op, rd, wr, eng=V):
            P.op(eng, lambda e: e.tensor_tensor(out=out, in0=a, in1=b, op=op), reads=rd, writes=wr)

        def ts(out, a, s1, s2, op0, op1, rd, wr, eng=V):
            if op1 is None:
                P.op(eng, lambda e: e.tensor_scalar(out=out, in0=a, scalar1=s1, scalar2=None, op0=op0), reads=rd, writes=wr)
            else:
                P.op(eng, lambda e: e.tensor_scalar(out=out, in0=a, scalar1=s1, scalar2=s2, op0=op0, op1=op1), reads=rd, writes=wr)

        def act(out, a, func, rd, wr, bias=None, scale=None):
            kw = {}
            if bias is not None:
                kw["bias"] = bias
            if scale is not None:
                kw["scale"] = scale
            P.op(S, lambda e: e.activation(out=out, in_=a, func=func, **kw), reads=rd, writes=wr)

        def finish(reads, ap_pairs):
            for i, (d_ap, s_ap) in enumerate(ap_pairs):
                P.dma(Q, d_ap, s_ap, reads=reads, key=f"dbg{i % 4}")
            P.finalize()
            return nc, P

        bar = sb([128, 2], name="bar")
        P.barrier_fn = lambda e: e.memset(bar[:], 0.0)
        ident = sb([128, 128], name="ident"); load(ident[:], ident_d, "ident")
        iota = sb([128, 128], name="iota"); load(iota[:], iota_d, "iota")
        X = sb([128, NTL * D], name="X")
        XT = sb([128, 8 * TOK], BF16, name="XT")

        pA = [ps([128, 512], name=f"pA{i}") for i in range(7)]

        def transpose_tile(ti, want32, xt32=None):
            for half in range(2):
                for k in range(4):
                    dc = half * 4 + k
                    P.op(T, (lambda e, dc=dc, k=k: e.transpose(pA[0][:, k * 128:(k + 1) * 128], X[:, ti * D + dc * 128:ti * D + (dc + 1) * 128], ident[:])), reads=[f"X{ti}", "ident"], writes=["pA0"])
                if want32:
                    P.op(V, (lambda e, half=half: e.tensor_copy(xt32[:, half * 512:(half + 1) * 512], pA[0][:])), reads=[], writes=["xT32", "pA0"])
                for k in range(4):
                    dc = half * 4 + k
                    act(XT[:, dc * TOK + ti * 128:dc * TOK + (ti + 1) * 128], pA[0][:, k * 128:(k + 1) * 128], AF.Copy, [], [f"XT{ti}", "pA0"])


        with contextlib.ExitStack() as s1:
            NB = 4
            cvf = [sb([128, D], name=f"cvf{i}", stack=s1) for i in range(NB)]
            cvb = [sb([128, D], BF16, name=f"cvb{i}", stack=s1) for i in range(NB)]
            n = 0
            for l in range(2):
                for k, src in enumerate((umy_d[l], vmy_d[l])):
                    for c in range(1 if SMALLTB else 128):
                        b = n % NB; n += 1
                        P.dma(Q, cvf[b][:], src[c * 128:(c + 1) * 128, :], writes=[f"cvf{b}"])
                        if b % 2 == 0:
                            act(cvb[b][:], cvf[b][:], AF.Copy, [f"cvf{b}"], [f"cvb{b}"])
                        else:
                            P.op(V, (lambda e, b=b: e.tensor_copy(cvb[b][:], cvf[b][:])), reads=[f"cvf{b}"], writes=[f"cvb{b}"])
                        P.dma(Q, tb_all[l][k][c * 128:(c + 1) * 128, :], cvb[b][:], reads=[f"cvb{b}"], writes=[f"tball{l}{k}"], key=f"tbst{b}")
            P.barrier()
        if STOP == "tb":
            return finish([], [])

        utm = sb([128, 128], name="utm"); load(utm[:], utm_d, "utm")
        onesm = sb([128, 128], name="onesm"); load(onesm[:], onesm_d, "onesm")
        mask16 = sb([128, 1], name="mask16"); load(mask16[:], mask16_d, "mask16")
        ckrun = sb([128, 16], name="ckrun")
        P.op(V, lambda e: e.memset(ckrun[:], 0.0), writes=["ckrun"])
        dfm = sb([128, 8], name="dfm"); load(dfm[:], dfm_d, "dfm")
        r2 = sb([128, 32], name="r2")
        carry = sb([128, 64], name="carry")
        st_re = sb([128, 64], name="st_re"); load(st_re[:], st_re_d, "st_re")
        st_im = sb([128, 64], name="st_im"); load(st_im[:], st_im_d, "st_im")
        osts = sb([128, 128], name="osts")
        P.op(V, lambda e: e.memset(carry[:], 0.0), writes=["carry"])
        P.op(V, lambda e: e.memset(osts[:], 0.0), writes=["osts"])
        with contextlib.ExitStack() as s1:
            def tl(shape, dt=F32, name=None):
                return sb(shape, dt, name=name, stack=s1)
            tvec = tl([128, 128], name="tvec"); load(tvec[:], tvec_d, "tvec")
            rowmask = tl([128, 8], name="rowmask"); load(rowmask[:], rowmask_d, "rowmask")
            cosT = tl([128, 32 * 128], name="cosT"); sinT = tl([128, 32 * 128], name="sinT")
            wbu = tl([128, 32 * 2 * 128], BF16, name="wbu")
            wc = tl([128, 32 * 2 * 128], BF16, name="wc")
            RRF = 512
            with contextlib.ExitStack() as s2:
                def t2(shape, dt=F32, name=None):
                    return sb(shape, dt, name=name, stack=s2)
                rr_a = t2([128, RRF], name="rr_a"); rr_ki = t2([128, RRF], I32, name="rr_ki")
                rr_kf = t2([128, RRF], name="rr_kf"); rr_red = t2([128, RRF], name="rr_red"); rr_c1 = t2([128, RRF], name="rr_c1")

                def sin_rr(out, arg, shift, F, rd, wr):
                    a = rr_a[:, 0:F]; ki = rr_ki[:, 0:F]; kf = rr_kf[:, 0:F]; red = rr_red[:, 0:F]; c1 = rr_c1[:, 0:F]
                    n = "rr_"
                    ts(a, arg, float(shift), None, ALU.add, None, rd, [n + "a"])
                    ts(kf, a, 1.0 / TWO_PI, 0.5, ALU.mult, ALU.add, [n + "a"], [n + "kf"])
                    P.op(V, lambda e: e.tensor_copy(ki, kf), reads=[n + "kf"], writes=[n + "ki"])
                    P.op(V, lambda e: e.tensor_copy(kf, ki), reads=[n + "ki"], writes=[n + "kf"])
                    P.op(V, lambda e: e.scalar_tensor_tensor(out=red, in0=kf, scalar=-TWO_PI, in1=a, op0=ALU.mult, op1=ALU.add), reads=[n + "kf", n + "a"], writes=[n + "red"])
                    ts(c1, red, math.pi, -TWO_PI, ALU.is_gt, ALU.mult, [n + "red"], [n + "c1"])
                    tt(red, red, c1, ALU.add, [n + "red", n + "c1"], [n + "red"])
                    ts(c1, red, -math.pi, TWO_PI, ALU.is_lt, ALU.mult, [n + "red"], [n + "c1"])
                    tt(red, red, c1, ALU.add, [n + "red", n + "c1"], [n + "red"])
                    ts(red, red, math.pi, -math.pi, ALU.min, ALU.max, [n + "red"], [n + "red"])
                    act(out, red, AF.Sin, [n + "red"], wr)

                lamre2 = t2([128, 32]); lamim2 = t2([128, 32]); logdt2 = t2([128, 32])
                load(lamre2[:], lamre2_d, "lamre2"); load(lamim2[:], lamim2_d, "lamim2"); load(logdt2[:], logdt2_d, "logdt2")
                dt2 = t2([128, 32]); th2 = t2([128, 32]); tmp2 = t2([128, 32])
                act(dt2[:], logdt2[:], AF.Exp, ["logdt2"], ["dt2"])
                tt(tmp2[:], lamre2[:], dt2[:], ALU.mult, ["lamre2", "dt2"], ["tmp2"])
                act(r2[:], tmp2[:], AF.Exp, ["tmp2"], ["r2"])
                tt(th2[:], lamim2[:], dt2[:], ALU.mult, ["lamim2", "dt2"], ["th2"])
                arg = t2([128, RRF])
                for ch in range(8):
                    for g8 in range(4):
                        gp = ch * 4 + g8
                        ts(arg[:, g8 * 128:(g8 + 1) * 128], tvec[:], th2[:, gp:gp + 1], None, ALU.mult, None, ["tvec", "th2"], ["arg"])
                    sin_rr(sinT[:, ch * 512:(ch + 1) * 512], arg[:], 0.0, 512, ["arg"], ["sinT"])
                    sin_rr(cosT[:, ch * 512:(ch + 1) * 512], arg[:], math.pi / 2, 512, ["arg"], ["cosT"])
                lr = t2([128, 256]); li = t2([128, 256]); ld = t2([128, 256]); bre = t2([128, 256]); bim = t2([128, 256])
                mag = t2([128, 256]); ang = t2([128, 256]); c1_ = t2([128, 256]); s1_ = t2([128, 256])
                abr = t2([128, 256]); abi = t2([128, 256]); den = t2([128, 256]); t1 = t2([128, 256]); t2_ = t2([128, 256])
                P.op(V, lambda e: e.memset(wc[:], 0.0), writes=["wc"])
                for hf in range(2):
                    hs = slice(hf * 256, (hf + 1) * 256)
                    load(lr[:], lamre1_d[:, hs], "lr"); load(li[:], lamim1_d[:, hs], "li"); load(ld[:], logdt1_d[:, hs], "ld")
                    load(bre[:], bre1_d[:, hs], "bre"); load(bim[:], bim1_d[:, hs], "bim")
                    act(ld[:], ld[:], AF.Exp, ["ld"], ["ld"])
                    tt(mag[:], lr[:], ld[:], ALU.mult, ["lr", "ld"], ["mag"])
                    act(mag[:], mag[:], AF.Exp, ["mag"], ["mag"])
                    tt(ang[:], li[:], ld[:], ALU.mult, ["li", "ld"], ["ang"])
                    sin_rr(s1_[:], ang[:], 0.0, 256, ["ang"], ["s1_"])
                    sin_rr(c1_[:], ang[:], math.pi / 2, 256, ["ang"], ["c1_"])
                    tt(abr[:], mag[:], c1_[:], ALU.mult, ["mag", "c1_"], ["abr"])
                    tt(abi[:], mag[:], s1_[:], ALU.mult, ["mag", "s1_"], ["abi"])
                    ts(abr[:], abr[:], -1.0, None, ALU.add, None, ["abr"], ["abr"])
                    tt(den[:], lr[:], lr[:], ALU.mult, ["lr"], ["den"])
                    tt(t1[:], li[:], li[:], ALU.mult, ["li"], ["t1"])
                    tt(den[:], den[:], t1[:], ALU.add, ["den", "t1"], ["den"])
                    P.op(V, lambda e: e.reciprocal(den[:], den[:]), reads=["den"], writes=["den"])
                    tt(t1[:], abr[:], lr[:], ALU.mult, ["abr", "lr"], ["t1"])
                    tt(t2_[:], abi[:], li[:], ALU.mult, ["abi", "li"], ["t2"])
                    tt(t1[:], t1[:], t2_[:], ALU.add, ["t1", "t2"], ["t1"])
                    tt(mag[:], t1[:], den[:], ALU.mult, ["t1", "den"], ["mag"])
                    tt(t1[:], abi[:], lr[:], ALU.mult, ["abi", "lr"], ["t1"])
                    tt(t2_[:], abr[:], li[:], ALU.mult, ["abr", "li"], ["t2"])
                    tt(t1[:], t1[:], t2_[:], ALU.subtract, ["t1", "t2"], ["t1"])
                    tt(ang[:], t1[:], den[:], ALU.mult, ["t1", "den"], ["ang"])
                    tt(t1[:], mag[:], bre[:], ALU.mult, ["mag", "bre"], ["t1"])
                    tt(t2_[:], ang[:], bim[:], ALU.mult, ["ang", "bim"], ["t2"])
                    tt(s1_[:], t1[:], t2_[:], ALU.subtract, ["t1", "t2"], ["s1_"])
                    tt(t1[:], mag[:], bim[:], ALU.mult, ["mag", "bim"], ["t1"])
                    tt(t2_[:], ang[:], bre[:], ALU.mult, ["ang", "bre"], ["t2"])
                    tt(c1_[:], t1[:], t2_[:], ALU.add, ["t1", "t2"], ["c1_"])
                    for gp in range(hf * 16, hf * 16 + 16):
                        dcl = gp // 4 - 4 * hf
                        for gl in range(2):
                            j = 2 * (gp % 4) + gl
                            for ri, src in enumerate((s1_, c1_)):
                                o0 = (gp * 2 + ri) * 128 + gl * 64
                                ts(wbu[:, o0:o0 + 64], src[:, dcl * 64:(dcl + 1) * 64], rowmask[:, j:j + 1], None, ALU.mult, None, ["s1_", "c1_", "rowmask"], ["wbu"])
                    load(den[:], cre2_d[:, hs], "den"); load(abi[:], cim2_d[:, hs], "abi")
                    for gp in range(hf * 16, hf * 16 + 16):
                        gpl = gp - 16 * hf
                        for gl in range(2):
                            j = 2 * (gp % 4) + gl
                            for ri, (src, sc) in enumerate(((den, 1.0), (abi, -1.0))):
                                o0 = (gp * 2 + ri) * 128 + j * 16
                                ts(wc[gl * 64:(gl + 1) * 64, o0:o0 + 16], src[gl * 64:(gl + 1) * 64, gpl * 16:(gpl + 1) * 16], sc, None, ALU.mult, None, ["den", "abi"], ["wc"])
                P.barrier()
            P.dma(Q, s5c_cos[:, :], cosT[:], reads=["cosT"], writes=["s5c"], key="s5c0")
            P.dma(Q, s5c_sin[:, :], sinT[:], reads=["sinT"], writes=["s5c"], key="s5c1")
            P.dma(Q, s5c_wbu[:, :], wbu[:], reads=["wbu"], writes=["s5c"], key="s5c2")
            P.dma(Q, s5c_wc[:, :], wc[:], reads=["wc"], writes=["s5c"], key="s5c3")
            P.barrier()

        def s5_group(tiles, g):
            s1 = contextlib.ExitStack()
            def tl(shape, dt=F32, name=None):
                return sb(shape, dt, name=name, stack=s1)
            cosT = tl([128, 32 * 128], name="cosT"); sinT = tl([128, 32 * 128], name="sinT")
            wbu = tl([128, 32 * 2 * 128], BF16, name="wbu")
            wc = tl([128, 32 * 2 * 128], BF16, name="wc")
            P.dma(Q, cosT[:], s5c_cos[:, :], reads=["s5c"], writes=["cosT"])
            P.dma(Q, sinT[:], s5c_sin[:, :], reads=["s5c"], writes=["sinT"])
            P.dma(Q, wbu[:], s5c_wbu[:, :], reads=["s5c"], writes=["wbu"])
            P.dma(Q, wc[:], s5c_wc[:, :], reads=["s5c"], writes=["wc"])
            xT32 = tl([128, 8 * 128], name="xT32")
            bsets = []
            for q_ in range(2):
                tm = [tl([128, 128], name=f"tm{i}_{q_}") for i in range(4)]
                zr = tl([128, 128], name=f"zr{q_}"); zi = tl([128, 128], name=f"zi{q_}")
                gr = tl([128, 128], name=f"gr{q_}"); gi_ = tl([128, 128], name=f"gi{q_}")
                hr = tl([128, 128], name=f"hr{q_}"); hi = tl([128, 128], name=f"hi{q_}")
                hrb = tl([128, 128], BF16, name=f"hrb{q_}"); hib = tl([128, 128], BF16, name=f"hib{q_}")
                bsets.append((tm, zr, zi, gr, gi_, hr, hi, hrb, hib, f"_{q_}"))
            yf = tl([128, 128], name="yf")
            pT = pA[0]; pbu = [pA[1], pA[2]]; pbn_ = ["pA1", "pA2"]; py = [pA[3], pA[4]]; pyn = ["pA3", "pA4"]

            def s5_one(ti, gp, segs, full, bs):
                tm, zr, zi, gr, gi_, hr, hi, hrb, hib, sfx = bs
                dc = gp // 4
                pb = pbu[gp % 2]; pbn = pbn_[gp % 2]
                for ri in range(2):
                    o0 = (gp * 2 + ri) * 128
                    P.op(T, (lambda e, pb=pb, ri=ri, o0=o0, dc=dc: e.matmul(pb[:, ri * 128:(ri + 1) * 128], wbu[:, o0:o0 + 128], XT[:, dc * TOK + ti * 128:dc * TOK + (ti + 1) * 128], start=True, stop=True)), reads=["wbu", f"XT{ti}"], writes=[pbn])
                    yield
                partial = full and (len(segs) > 1 or segs[0][1] < 128)
                if partial:
                    P.op(V, lambda e: e.memset(hr[:], 0.0), writes=["hr" + sfx])
                    yield
                    P.op(V, lambda e: e.memset(hi[:], 0.0), writes=["hi" + sfx])
                    yield
                for (c0, n, init, dest) in segs:
                    cs = cosT[:, gp * 128:gp * 128 + n]; sn = sinT[:, gp * 128:gp * 128 + n]
                    br = pb[:, c0:c0 + n]; bi = pb[:, 128 + c0:128 + c0 + n]
                    sl = slice(c0, c0 + n)
                    e2 = V
                    tt(tm[0][:, sl], br, cs, ALU.mult, ["cosT"], ["tm0" + sfx, pbn])
                    yield
                    tt(tm[1][:, sl], bi, sn, ALU.mult, ["sinT"], ["tm1" + sfx, pbn])
                    yield
                    tt(zr[:, sl], tm[0][:, sl], tm[1][:, sl], ALU.add, ["tm0" + sfx, "tm1" + sfx], ["zr" + sfx], eng=e2)
                    yield
                    tt(tm[2][:, sl], bi, cs, ALU.mult, ["cosT"], ["tm2" + sfx, pbn])
                    yield
                    tt(tm[3][:, sl], br, sn, ALU.mult, ["sinT"], ["tm3" + sfx, pbn])
                    yield
                    tt(zi[:, sl], tm[2][:, sl], tm[3][:, sl], ALU.subtract, ["tm2" + sfx, "tm3" + sfx], ["zi" + sfx], eng=e2)
                    yield
                    if init == "carry":
                        ir = carry[:, gp:gp + 1]; ii = carry[:, 32 + gp:33 + gp]; irn = ["carry"]
                    elif init == "zero":
                        ir = 0.0; ii = 0.0; irn = []
                    else:
                        sidx = init
                        ir = st_re[:, sidx * 32 + gp:sidx * 32 + gp + 1]; ii = st_im[:, sidx * 32 + gp:sidx * 32 + gp + 1]; irn = ["st_re", "st_im"]
                    rb = r2[:, gp:gp + 1].to_broadcast([128, n])
                    P.op(V, (lambda e, rb=rb, ir=ir, sl=sl: e.tensor_tensor_scan(out=gr[:, sl], data0=rb, data1=zr[:, sl], initial=ir, op0=ALU.mult, op1=ALU.add)), reads=["r2", "zr" + sfx] + irn, writes=["gr" + sfx])
                    yield
                    P.op(V, (lambda e, rb=rb, ii=ii, sl=sl: e.tensor_tensor_scan(out=gi_[:, sl], data0=rb, data1=zi[:, sl], initial=ii, op0=ALU.mult, op1=ALU.add)), reads=["r2", "zi" + sfx] + irn, writes=["gi" + sfx])
                    yield
                    if full:
                        us = sl; ucs = cs; usn = sn
                    else:
                        us = slice(c0 + n - 1, c0 + n); ucs = cosT[:, gp * 128 + n - 1:gp * 128 + n]; usn = sinT[:, gp * 128 + n - 1:gp * 128 + n]
                    tt(tm[0][:, us], gr[:, us], ucs, ALU.mult, ["gr" + sfx, "cosT"], ["tm0" + sfx])
                    yield
                    tt(tm[1][:, us], gi_[:, us], usn, ALU.mult, ["gi" + sfx, "sinT"], ["tm1" + sfx])
                    yield
                    tt(hr[:, us], tm[0][:, us], tm[1][:, us], ALU.subtract, ["tm0" + sfx, "tm1" + sfx], ["hr" + sfx], eng=e2)
                    yield
                    tt(tm[2][:, us], gi_[:, us], ucs, ALU.mult, ["gi" + sfx, "cosT"], ["tm2" + sfx])
                    yield
                    tt(tm[3][:, us], gr[:, us], usn, ALU.mult, ["gr" + sfx, "sinT"], ["tm3" + sfx])
                    yield
                    tt(hi[:, us], tm[2][:, us], tm[3][:, us], ALU.add, ["tm2" + sfx, "tm3" + sfx], ["hi" + sfx], eng=e2)
                    yield
                    last = c0 + n - 1
                    if dest is not None:
                        dt_, o_r, o_i, dn = dest
                        P.op(V, (lambda e, last=last, dt_=dt_, o_r=o_r: e.tensor_copy(dt_[:, o_r:o_r + 1], hr[:, last:last + 1])), reads=["hr" + sfx], writes=[dn])
                        yield
                        P.op(V, (lambda e, last=last, dt_=dt_, o_i=o_i: e.tensor_copy(dt_[:, o_i:o_i + 1], hi[:, last:last + 1])), reads=["hi" + sfx], writes=[dn])
                        yield
                if full:
                    act(hrb[:], hr[:], AF.Copy, ["hr" + sfx], ["hrb" + sfx])
                    yield
                    act(hib[:], hi[:], AF.Copy, ["hi" + sfx], ["hib" + sfx])
                    yield
                    pyt = py[dc % 2]; pn = pyn[dc % 2]
                    k4 = gp % 4
                    for ri, hb in enumerate((hrb, hib)):
                        o0 = (gp * 2 + ri) * 128
                        P.op(T, (lambda e, pyt=pyt, o0=o0, hb=hb, first=(k4 == 0 and ri == 0), lastm=(k4 == 3 and ri == 1): e.matmul(pyt[:, 0:128], wc[:, o0:o0 + 128], hb[:], start=first, stop=lastm)), reads=["wc", "hrb" + sfx, "hib" + sfx], writes=[pn])
                        yield
                    if k4 == 3:
                        P.op(V, (lambda e, pyt=pyt, dc=dc: e.scalar_tensor_tensor(out=yf[:], in0=xT32[:, dc * 128:(dc + 1) * 128], scalar=dfm[:, dc:dc + 1], in1=pyt[:, 0:128], op0=ALU.mult, op1=ALU.add)), reads=["xT32", "dfm"], writes=["yf", pn])
                        yield
                        act(XT[:, dc * TOK + ti * 128:dc * TOK + (ti + 1) * 128], yf[:], AF.Gelu_apprx_tanh, ["yf"], [f"XT{ti}"])
                        yield

            def s5_tile(ti, segs_fn, full):
                import itertools
                for gp in range(0, 32, 2):
                    ga = s5_one(ti, gp, segs_fn(gp), full, bsets[0])
                    gb = s5_one(ti, gp + 1, segs_fn(gp + 1), full, bsets[1])
                    for _ in itertools.zip_longest(ga, gb):
                        pass

            for ti in tiles:
                transpose_tile(ti, True, xT32)
                if ti == 0:
                    s5_tile(0, lambda gp: [(0, 16, "zero", (carry, gp, 32 + gp, "carry")), (16, 16, 0, (osts, gp, 32 + gp, "osts")), (32, 16, 1, (osts, 64 + gp, 96 + gp, "osts"))], True)
                else:
                    s5_tile(ti, lambda gp: [(0, 128, "carry", (carry, gp, 32 + gp, "carry"))], True)
            P.barrier()
            s1.close()


        lnrow = sb([128, 2 * D], name="lnrow")

        def load_ln(idx_g, idx_b):
            P.dma(Q, lnrow[:, 0:D], lnv_d[idx_g:idx_g + 1, :].partition_broadcast(128), writes=["lnrow"])
            P.dma(Q, lnrow[:, D:2 * D], lnv_d[idx_b:idx_b + 1, :].partition_broadcast(128), writes=["lnrow"])

        lnst = sb([128, 12], name="lnst"); lnmv = sb([128, 2], name="lnmv"); lnr = sb([128, 1], name="lnr")

        def layer_norm_tile(ti):
            xt_ = X[:, ti * D:(ti + 1) * D]
            for h in range(2):
                P.op(V, (lambda e, h=h: e.bn_stats(lnst[:, h * 6:(h + 1) * 6], X[:, ti * D + h * 512:ti * D + (h + 1) * 512])), reads=[f"X{ti}"], writes=["lnst"])
            P.op(V, lambda e: e.bn_aggr(lnmv[:], lnst[:]), reads=["lnst"], writes=["lnmv"])
            ts(lnr[:], lnmv[:, 1:2], LN_EPS, None, ALU.add, None, ["lnmv"], ["lnr"])
            act(lnr[:], lnr[:], AF.Sqrt, ["lnr"], ["lnr"])
            P.op(V, lambda e: e.reciprocal(lnr[:], lnr[:]), reads=["lnr"], writes=["lnr"])
            ts(xt_, xt_, lnmv[:, 0:1], lnr[:, 0:1], ALU.subtract, ALU.mult, [f"X{ti}", "lnmv", "lnr"], [f"X{ti}"])
            tt(xt_, xt_, lnrow[:, 0:D], ALU.mult, [f"X{ti}", "lnrow"], [f"X{ti}"])
            tt(xt_, xt_, lnrow[:, D:2 * D], ALU.add, [f"X{ti}", "lnrow"], [f"X{ti}"])

        def glu_ln1(tiles):
            s1 = contextlib.ExitStack()
            wgf = sb([128, 8 * 512], name="wgf", stack=s1)
            wgv = sb([128, 8 * 512], BF16, name="wgv", stack=s1); wgg = sb([128, 8 * 512], BF16, name="wgg", stack=s1)
            sig = sb([128, 512], name="sig", stack=s1); mixv = sb([128, 512], name="mixv", stack=s1)
            load_ln(0, 1)
            wview = wglu_d.rearrange("p (dc n) -> p dc n", dc=8)
            for cb in range(2):
                for which, dst, dn in ((0, wgv, "wgv"), (1, wgg, "wgg")):
                    c0 = which * 1024 + cb * 512
                    P.dma(Q, wgf[:].rearrange("p (dc n) -> p dc n", dc=8), wview[:, :, c0:c0 + 512], writes=["wgf"])
                    act(dst[:], wgf[:], AF.Copy, ["wgf"], [dn])
                for ti in tiles:
                    for which, wt, wn, pst, pn in ((0, wgv, "wgv", pA[5], "pA5"), (1, wgg, "wgg", pA[6], "pA6")):
                        for dc in range(8):
                            P.op(T, (lambda e, wt=wt, pst=pst, dc=dc, ti=ti: e.matmul(pst[:], XT[:, dc * TOK + ti * 128:dc * TOK + (ti + 1) * 128], wt[:, dc * 512:(dc + 1) * 512], start=(dc == 0), stop=(dc == 7))), reads=[f"XT{ti}", wn], writes=[pn])
                    act(sig[:], pA[6][:], AF.Sigmoid, [], ["sig", "pA6"])
                    tt(mixv[:], pA[5][:], sig[:], ALU.mult, ["sig"], ["mixv", "pA5"])
                    xs_ = X[:, ti * D + cb * 512:ti * D + (cb + 1) * 512]
                    P.op(V, (lambda e, xs_=xs_: e.scalar_tensor_tensor(out=xs_, in0=xs_, scalar=DN_ALPHA, in1=mixv[:], op0=ALU.mult, op1=ALU.add)), reads=["mixv", f"X{ti}"], writes=[f"X{ti}"])
            for ti in tiles:
                layer_norm_tile(ti)
            P.barrier()
            s1.close()

        IDX1T = sb([128, TOK], BF16, name="IDX1T"); IDX2T = sb([128, TOK], BF16, name="IDX2T"); GT = sb([128, TOK], BF16, name="GT")

        def peer_layer(l, ln_idx, tiles):
            for ti in tiles:
                transpose_tile(ti, False)
            with contextlib.ExitStack() as s1:
                def tl(shape, dt=F32, name=None):
                    return sb(shape, dt, name=name, stack=s1)
                wst = tl([128, 8 * 256], name="wst")
                wpq = tl([128, 8 * 2048], BF16, name="wpq")
                pkf = tl([128, 256], name="pkf"); pkb = tl([128, 256], BF16, name="pkb")
                load(pkf[:], pk_d[l], "pkf")
                act(pkb[:], pkf[:], AF.Copy, ["pkf"], ["pkb"])
                wv = wpq_d[l].rearrange("p (dc n) -> p dc n", dc=8)
                wpv = wpq[:].rearrange("p (dc n) -> p dc n", dc=8)
                for cb in range(8):
                    P.dma(Q, wst[:].rearrange("p (dc n) -> p dc n", dc=8), wv[:, :, cb * 256:(cb + 1) * 256], writes=["wst"])
                    if cb % 2 == 0:
                        act(wpv[:, :, cb * 256:(cb + 1) * 256], wst[:].rearrange("p (dc n) -> p dc n", dc=8), AF.Copy, ["wst"], ["wpq"])
                    else:
                        P.op(V, (lambda e, cb=cb: e.tensor_copy(wpv[:, :, cb * 256:(cb + 1) * 256], wst[:].rearrange("p (dc n) -> p dc n", dc=8))), reads=["wst"], writes=["wpq"])
                qT = tl([128, 512], BF16, name="qT")
                Ssb = tl([128, 2048], name="Ssb"); Stmp = tl([128, 128], name="Stmp")
                vals = tl([128, 256], name="vals"); idxs = tl([128, 256], U32, name="idxs"); idxf = tl([128, 256], name="idxf")
                cand = tl([128, 2048], name="cand"); ctmp = tl([128, 256], name="ctmp")
                cv = tl([128, 128], name="cv"); cp = tl([128, 128], U32, name="cp"); cpf = tl([128, 128], name="cpf")
                a_i = tl([128, 128], I32, name="a_i"); a0 = tl([128, 128], name="a0"); gtm = tl([128, 128], name="gtm")
                a_f = tl([128, 128], name="a_f"); b_f = tl([128, 128], name="b_f")
                eqt = tl([128, 2048], name="eqt")
                i1t = tl([128, 128], name="i1t"); i2t = tl([128, 128], name="i2t"); gwt = tl([128, 128], name="gwt")
                negm = tl([128, 8], name="negm"); zs = tl([128, 8], name="zs")
                iota16 = iota[:, 0:16]
                for ti in tiles:
                    for jq in range(4):
                        for jj in range(4):
                            j = jq * 4 + jj
                            for dc in range(8):
                                P.op(T, (lambda e, jj=jj, j=j, dc=dc, ti=ti: e.matmul(pA[1][:, jj * 128:(jj + 1) * 128], wpq[:, dc * 2048 + j * 128:dc * 2048 + (j + 1) * 128], XT[:, dc * TOK + ti * 128:dc * TOK + (ti + 1) * 128], start=(dc == 0), stop=(dc == 7))), reads=["wpq", f"XT{ti}"], writes=["pA1"])
                        act(qT[:], pA[1][:], AF.Copy, [], ["qT", "pA1"])
                        for jj in range(4):
                            j = jq * 4 + jj
                            hf = j % 2
                            P.op(T, (lambda e, jj=jj, hf=hf: e.matmul(pA[2][:, jj * 128:(jj + 1) * 128], qT[:, jj * 128:(jj + 1) * 128], pkb[:, hf * 128:(hf + 1) * 128], start=True, stop=True)), reads=["qT", "pkb"], writes=["pA2"])
                        P.op(V, (lambda e, jq=jq: e.tensor_copy(Ssb[:, jq * 512:(jq + 1) * 512], pA[2][:])), reads=[], writes=["Ssb", "pA2"])
                    if STOP == "peers":
                        return finish(["Ssb"], [(dbg[0:128, :], Ssb[:, 0:1024]), (dbg[128:256, :], Ssb[:, 1024:2048])])
                    for j in range(16):
                        sj = Ssb[:, j * 128:(j + 1) * 128]
                        v0 = vals[:, j * 16:j * 16 + 8]; v1 = vals[:, j * 16 + 8:j * 16 + 16]
                        x0 = idxs[:, j * 16:j * 16 + 8]; x1 = idxs[:, j * 16 + 8:j * 16 + 16]
                        P.op(V, (lambda e, sj=sj, v0=v0: e.max(out=v0, in_=sj)), reads=["Ssb"], writes=["vals"])
                        P.op(V, (lambda e, sj=sj, v0=v0, x0=x0: e.max_index(out=x0, in_max=v0, in_values=sj)), reads=["Ssb", "vals"], writes=["idxs"])
                        P.op(V, (lambda e, sj=sj, v0=v0: e.match_replace(out=Stmp[:], in_to_replace=v0, in_values=sj, imm_value=NEG)), reads=["Ssb", "vals"], writes=["Stmp"])
                        P.op(V, (lambda e, v1=v1: e.max(out=v1, in_=Stmp[:])), reads=["Stmp"], writes=["vals"])
                        P.op(V, (lambda e, v1=v1, x1=x1: e.max_index(out=x1, in_max=v1, in_values=Stmp[:])), reads=["Stmp", "vals"], writes=["idxs"])
                    P.op(V, lambda e: e.tensor_copy(idxf[:], idxs[:]), reads=["idxs"], writes=["idxf"])
                    vv = vals[:].rearrange("p (h s k) -> p h s k", h=8, s=2)
                    ivf = idxf[:].rearrange("p (h s k) -> p h s k", h=8, s=2)
                    cand4 = cand[:].rearrange("p (h a b) -> p h a b", h=8, a=16)
                    P.op(V, lambda e: e.tensor_tensor(out=cand4, in0=vv[:, :, 0, :][:, :, :, None].to_broadcast([128, 8, 16, 16]), in1=vv[:, :, 1, :][:, :, None, :].to_broadcast([128, 8, 16, 16]), op=ALU.add), reads=["vals"], writes=["cand"])
                    for h in range(8):
                        ch = cand[:, h * 256:(h + 1) * 256]
                        c0 = cv[:, h * 16:h * 16 + 8]; c1 = cv[:, h * 16 + 8:h * 16 + 16]
                        p0 = cp[:, h * 16:h * 16 + 8]; p1 = cp[:, h * 16 + 8:h * 16 + 16]
                        P.op(V, (lambda e, ch=ch, c0=c0: e.max(out=c0, in_=ch)), reads=["cand"], writes=["cv"])
                        P.op(V, (lambda e, ch=ch, c0=c0, p0=p0: e.max_index(out=p0, in_max=c0, in_values=ch)), reads=["cand", "cv"], writes=["cp"])
                        P.op(V, (lambda e, ch=ch, c0=c0: e.match_replace(out=ctmp[:], in_to_replace=c0, in_values=ch, imm_value=NEG)), reads=["cand", "cv"], writes=["ctmp"])
                        P.op(V, (lambda e, c1=c1: e.max(out=c1, in_=ctmp[:])), reads=["ctmp"], writes=["cv"])
                        P.op(V, (lambda e, c1=c1, p1=p1: e.max_index(out=p1, in_max=c1, in_values=ctmp[:])), reads=["ctmp", "cv"], writes=["cp"])
                    P.op(V, lambda e: e.tensor_copy(cpf[:], cp[:]), reads=["cp"], writes=["cpf"])
                    ts(a_i[:], cpf[:], 0.0625, None, ALU.mult, None, ["cpf"], ["a_i"])
                    P.op(V, lambda e: e.tensor_copy(a0[:], a_i[:]), reads=["a_i"], writes=["a0"])
                    P.op(V, lambda e: e.scalar_tensor_tensor(out=gtm[:], in0=a0[:], scalar=16.0, in1=cpf[:], op0=ALU.mult, op1=ALU.is_gt), reads=["a0", "cpf"], writes=["gtm"])
                    tt(a_f[:], a0[:], gtm[:], ALU.subtract, ["a0", "gtm"], ["a_f"])
                    P.op(V, lambda e: e.scalar_tensor_tensor(out=b_f[:], in0=a_f[:], scalar=-16.0, in1=cpf[:], op0=ALU.mult, op1=ALU.add), reads=["a_f", "cpf"], writes=["b_f"])
                    eq4 = eqt[:].rearrange("p (h k a) -> p h k a", h=8, k=16)
                    io4 = iota16[:, None, None, :].to_broadcast([128, 8, 16, 16])
                    for (sel, half, dst, dn) in ((a_f, 0, i1t, "i1t"), (b_f, 1, i2t, "i2t")):
                        s3 = sel[:].rearrange("p (h k) -> p h k", h=8)
                        P.op(V, (lambda e, s3=s3: e.tensor_tensor(out=eq4, in0=s3[:, :, :, None].to_broadcast([128, 8, 16, 16]), in1=io4, op=ALU.is_equal)), reads=[sel.name if False else ("a_f" if half == 0 else "b_f"), "iota"], writes=["eqt"])
                        P.op(V, (lambda e, half=half: e.tensor_tensor(out=eq4, in0=eq4, in1=ivf[:, :, half, :][:, :, None, :].to_broadcast([128, 8, 16, 16]), op=ALU.mult)), reads=["eqt", "idxf"], writes=["eqt"])
                        P.op(V, (lambda e, dst=dst: e.tensor_reduce(out=dst[:].rearrange("p (h k) -> p h k", h=8), in_=eq4, axis=AX.X, op=ALU.add)), reads=["eqt"], writes=[dn])
                    cv3 = cv[:].rearrange("p (h k) -> p h k", h=8)
                    ts(negm[:], cv3[:, :, 0], -1.0, None, ALU.mult, None, ["cv"], ["negm"])
                    for h in range(8):
                        act(gwt[:, h * 16:(h + 1) * 16], cv[:, h * 16:(h + 1) * 16], AF.Exp, ["cv", "negm"], ["gwt"], bias=negm[:, h:h + 1], scale=1.0)
                    g3 = gwt[:].rearrange("p (h k) -> p h k", h=8)
                    P.op(V, lambda e: e.tensor_reduce(out=zs[:], in_=g3, axis=AX.X, op=ALU.add), reads=["gwt"], writes=["zs"])
                    P.op(V, lambda e: e.reciprocal(zs[:], zs[:]), reads=["zs"], writes=["zs"])
                    P.op(V, lambda e: e.tensor_tensor(out=g3, in0=g3, in1=zs[:, :, None].to_broadcast([128, 8, 16]), op=ALU.mult), reads=["gwt", "zs"], writes=["gwt"])
                    for (src, sn, dstT, dname) in ((i1t, "i1t", IDX1T, "IDX1T"), (i2t, "i2t", IDX2T, "IDX2T"), (gwt, "gwt", GT, "GT")):
                        P.op(T, (lambda e, src=src: e.transpose(pA[0][:, 0:128], src[:], ident[:])), reads=[sn, "ident"], writes=["pA0"])
                        act(dstT[:, ti * 128:(ti + 1) * 128], pA[0][:, 0:128], AF.Copy, [], [dname, "pA0"])
                if STOP == f"peerq{l}":
                    allx = [f"X{t}" for t in range(NTL)]
                    P.op(V, lambda e: e.tensor_copy(X[:, 7 * D:9 * D], Ssb[:]), reads=["Ssb"], writes=allx)
                    P.op(V, lambda e: e.tensor_copy(X[:, 9 * D:9 * D + 256], vals[:]), reads=["vals"], writes=allx)
                    P.op(V, lambda e: e.tensor_copy(X[:, 9 * D + 256:9 * D + 512], idxf[:]), reads=["idxf"], writes=allx)
                    P.op(V, lambda e: e.tensor_copy(X[:, 9 * D + 512:9 * D + 640], cv[:]), reads=["cv"], writes=allx)
                    P.op(V, lambda e: e.tensor_copy(X[:, 9 * D + 640:9 * D + 768], cpf[:]), reads=["cpf"], writes=allx)
                P.barrier()
            if STOP == f"peerq{l}":
                return "stop"
            with contextlib.ExitStack() as s1:
                def tl(shape, dt=F32, name=None):
                    return sb(shape, dt, name=name, stack=s1)
                Gsb = tl([128, 128 * 256], BF16, name="Gsb")
                NOH = 4
                oh1 = [tl([128, 128], BF16, name=f"oh1_{i}") for i in range(NOH)]
                oh2 = [tl([128, 128], BF16, name=f"oh2_{i}") for i in range(NOH)]
                NUB = 2
                ub = [tl([128, D], BF16, name=f"ub{i}") for i in range(NUB)]
                vb = [tl([128, D], BF16, name=f"vb{i}") for i in range(NUB)]
                ga = [tl([128, 256], name=f"ga{i}") for i in range(2)]
                wT = [tl([128, 256], BF16, name=f"wT{i}") for i in range(2)]
                load_ln(ln_idx, ln_idx + 1)
                G3 = Gsb[:].rearrange("p (c t) -> p c t", c=128)
                st_list = [(tiles[i], 2) for i in range(0, len(tiles) - 1, 2)] + ([(tiles[-1], 1)] if len(tiles) % 2 else [])
                cnt_c = 0
                for (t0, ntile) in st_list:
                    NTK = ntile * 128
                    col0 = t0 * 128
                    for t in range(NTK):
                        b = t % NOH
                        gc = col0 + t
                        ts(oh2[b][:], iota[:], IDX2T[:, gc:gc + 1], None, ALU.is_equal, None, ["iota", "IDX2T"], [f"oh2_{b}"])
                        ts(oh1[b][:], iota[:], IDX1T[:, gc:gc + 1], GT[:, gc:gc + 1], ALU.is_equal, ALU.mult, ["iota", "IDX1T", "GT"], [f"oh1_{b}"])
                        pg = pA[1 + (t // 4) % 2]; pgn = f"pA{1 + (t // 4) % 2}"
                        P.op(T, (lambda e, pg=pg, b=b, t=t: e.matmul(pg[:, (t % 4) * 128:(t % 4 + 1) * 128], oh2[b][:], oh1[b][:], start=True, stop=True)), reads=[f"oh2_{b}", f"oh1_{b}"], writes=[pgn])
                        if t % 4 == 3:
                            tq = t - 3
                            act(G3[:, :, tq:tq + 4], pg[:].rearrange("p (t c) -> p c t", t=4), AF.Copy, [], ["Gsb", pgn])
                    for c in range(NCH):
                        bi = cnt_c % NUB; b2 = cnt_c % 2; cnt_c += 1
                        P.dma(Q, ub[bi][:], tb_all[l][0][c * 128:(c + 1) * 128, :], reads=[f"tball{l}0"], writes=[f"ub{bi}"])
                        P.dma(G, vb[bi][:], tb_all[l][1][c * 128:(c + 1) * 128, :], reads=[f"tball{l}1"], writes=[f"vb{bi}"])
                        pa = pA[1 + b2]; pan = f"pA{1 + b2}"
                        for dc in range(8):
                            P.op(T, (lambda e, pa=pa, bi=bi, dc=dc, col0=col0, NTK=NTK: e.matmul(pa[:, 0:NTK], ub[bi][:, dc * 128:(dc + 1) * 128], XT[:, dc * TOK + col0:dc * TOK + col0 + NTK], start=(dc == 0), stop=(dc == 7))), reads=[f"ub{bi}"] + [f"XT{t0 + i}" for i in range(ntile)], writes=[pan])
                        act(ga[b2][:, 0:NTK], pa[:, 0:NTK], AF.Gelu_apprx_tanh, [], [f"ga{b2}", pan])
                        tt(wT[b2][:, 0:NTK], ga[b2][:, 0:NTK], G3[:, c, 0:NTK], ALU.mult, [f"ga{b2}", "Gsb"], [f"wT{b2}"])
                        for tb in range(ntile):
                            for hf in range(2):
                                po = pA[3 + tb * 2 + hf]; pon = f"pA{3 + tb * 2 + hf}"
                                P.op(T, (lambda e, po=po, b2=b2, bi=bi, tb=tb, hf=hf, c=c: e.matmul(po[:], wT[b2][:, tb * 128:(tb + 1) * 128], vb[bi][:, hf * 512:(hf + 1) * 512], start=(c == 0), stop=(c == NCH - 1))), reads=[f"wT{b2}", f"vb{bi}"], writes=[pon])
                    for tb in range(ntile):
                        ti = t0 + tb
                        for hf in range(2):
                            po = pA[3 + tb * 2 + hf]; pon = f"pA{3 + tb * 2 + hf}"
                            xs_ = X[:, ti * D + hf * 512:ti * D + (hf + 1) * 512]
                            P.op(V, (lambda e, xs_=xs_, po=po: e.scalar_tensor_tensor(out=xs_, in0=xs_, scalar=DN_ALPHA, in1=po[:], op0=ALU.mult, op1=ALU.add)), reads=[f"X{ti}"], writes=[f"X{ti}", pon])
                        layer_norm_tile(ti)
                P.barrier()
            return "ok"

        NG = int(os.environ.get("MK_NG", "8"))

        def kv_phase(tiles, g):
            s1 = contextlib.ExitStack()
            def tl(shape, dt=F32, name=None):
                return sb(shape, dt, name=name, stack=s1)
            for ti in tiles:
                transpose_tile(ti, False)
            wst = tl([128, 8 * 512], name="kwst"); wb = tl([128, 8 * 512], BF16, name="kwb")
            osb = [tl([128, 512], name=f"kosb{i}") for i in range(2)]
            ovb = [tl([128, 512], BF16, name=f"kovb{i}") for i in range(2)]
            okT = [tl([128, 512], BF16, name=f"kokT{i}") for i in range(2)]
            wv = wkvf_d.rearrange("p (dc n) -> p dc n", dc=8)
            wst3 = wst[:].rearrange("p (dc n) -> p dc n", dc=8)
            def row0(ti):
                return ti * 128 if ti == 0 else 128 + g * 2048 + (ti - 1) * 128
            n = 0
            for cb in range(4):
                P.dma(Q, wst3, wv[:, :, cb * 512:(cb + 1) * 512], writes=["kwst"])
                act(wb[:], wst[:], AF.Copy, ["kwst"], ["kwb"])
                for ti in tiles:
                    b = n % 2; n += 1
                    for dc in range(8):
                        P.op(T, (lambda e, dc=dc, ti=ti: e.matmul(pA[1][:], XT[:, dc * TOK + ti * 128:dc * TOK + (ti + 1) * 128], wb[:, dc * 512:(dc + 1) * 512], start=(dc == 0), stop=(dc == 7))), reads=[f"XT{ti}", "kwb"], writes=["pA1"])
                    P.op(V, (lambda e, b=b: e.tensor_copy(osb[b][:], pA[1][:])), reads=[], writes=[f"kosb{b}", "pA1"])
                    dst = o_k if cb < 2 else o_v
                    c0 = (cb % 2) * 512
                    P.dma(Q, dst[row0(ti):row0(ti) + 128, c0:c0 + 512], osb[b][:], reads=[f"kosb{b}"], key=f"kout{b}")
                if True:
                    dstT = kT_all if cb < 2 else vT_all
                    for j in range(4):
                        dcq = (cb % 2) * 4 + j
                        for (ta, nt) in [(tiles[i], min(4, len(tiles) - i)) for i in range(0, len(tiles), 4)]:
                            b = n % 2; n += 1
                            for dc in range(8):
                                P.op(T, (lambda e, dc=dc, j=j, ta=ta, nt=nt: e.matmul(pA[2][:, 0:nt * 128], wb[:, dc * 512 + j * 128:dc * 512 + (j + 1) * 128], XT[:, dc * TOK + ta * 128:dc * TOK + (ta + nt) * 128], start=(dc == 0), stop=(dc == 7))), reads=["kwb"] + [f"XT{ta + i}" for i in range(nt)], writes=["pA2"])
                            act(okT[b][:, 0:nt * 128], pA[2][:, 0:nt * 128], AF.Copy, [], [f"kokT{b}", "pA2"])
                            P.dma(Q, dstT[dcq * 128:(dcq + 1) * 128, row0(ta):row0(ta) + nt * 128], okT[b][:, 0:nt * 128], reads=[f"kokT{b}"], writes=["kT_all"], key=f"kTst{b}")
            wf = tl([128, 8 * 16], name="kwf"); wfb = tl([128, 8 * 16], BF16, name="kwfb"); bfr = tl([128, 16], name="kbfr")
            lf = [tl([128, 16], name=f"klf{i}") for i in range(2)]
            ckt = [tl([128, 16], name=f"kck{i}") for i in range(2)]
            P.dma(Q, wf[:].rearrange("p (dc n) -> p dc n", dc=8), wv[:, :, 2048:2064], writes=["kwf"])
            act(wfb[:], wf[:], AF.Copy, ["kwf"], ["kwfb"])
            P.dma(Q, bfr[:], bf_d[0:1, :].partition_broadcast(128), writes=["kbfr"])
            for ti in tiles:
                b = n % 2; n += 1
                for dc in range(8):
                    P.op(T, (lambda e, dc=dc, ti=ti: e.matmul(pA[3][:, 0:16], XT[:, dc * TOK + ti * 128:dc * TOK + (ti + 1) * 128], wfb[:, dc * 16:(dc + 1) * 16], start=(dc == 0), stop=(dc == 7))), reads=[f"XT{ti}", "kwfb"], writes=["pA3"])
                tt(lf[b][:], pA[3][:, 0:16], bfr[:], ALU.add, ["kbfr"], [f"klf{b}", "pA3"])
                act(lf[b][:], lf[b][:], AF.Exp, [f"klf{b}"], [f"klf{b}"], scale=-1.0)
                act(lf[b][:], lf[b][:], AF.Ln, [f"klf{b}"], [f"klf{b}"], bias=1.0)
                ts(lf[b][:], lf[b][:], -1.0, None, ALU.mult, None, [f"klf{b}"], [f"klf{b}"])
                P.dma(Q, o_lf[row0(ti):row0(ti) + 128, :], lf[b][:], reads=[f"klf{b}"], key=f"klfo{b}")
                P.dma(Q, lf_all[row0(ti):row0(ti) + 128, :], lf[b][:], reads=[f"klf{b}"], writes=["lf_all"], key=f"klfo{b}")
                if ti == 0:
                    ts(lf[b][:], lf[b][:], mask16[:, 0:1], None, ALU.mult, None, [f"klf{b}", "mask16"], [f"klf{b}"])
                P.op(T, (lambda e, b=b: e.matmul(pA[4][:, 0:16], utm[:], lf[b][:], start=True, stop=True)), reads=["utm", f"klf{b}"], writes=["pA4"])
                P.op(T, (lambda e, b=b: e.matmul(pA[4][:, 16:32], onesm[:], lf[b][:], start=True, stop=True)), reads=["onesm", f"klf{b}"], writes=["pA4"])
                tt(ckt[b][:], pA[4][:, 0:16], ckrun[:], ALU.add, ["ckrun"], [f"kck{b}", "pA4"])
                tt(ckrun[:], pA[4][:, 16:32], ckrun[:], ALU.add, ["ckrun"], ["ckrun", "pA4"])
                P.dma(Q, ck_all[row0(ti):row0(ti) + 128, :], ckt[b][:], reads=[f"kck{b}"], writes=["ck_all"], key=f"kcko{b}")
            for ti in tiles:
                P.dma(Q, x2_all[row0(ti):row0(ti) + 128, :], X[:, ti * D:(ti + 1) * D], reads=[f"X{ti}"], writes=["x2_all"], key="x2st")
            P.barrier()
            s1.close()

        for g in range(NG):
            tiles = list(range(0 if g == 0 else 1, NTL))
            if g == 0:
                P.dma(Q, X[:, 0:D], xmisc, writes=["X0"])
            for ti in range(1, NTL):
                P.dma(Q, X[:, ti * D:(ti + 1) * D], xmain[g * 2048 + (ti - 1) * 128:g * 2048 + ti * 128, :], writes=[f"X{ti}"])
            s5_group(tiles, g)
            if g == NG - 1:
                P.dma(Q, o_ssm_p, carry[:], reads=["carry"], key="o_ssm_p")
            if g == 0:
                P.dma(Q, o_ssm_s, osts[:], reads=["osts"], key="o_ssm_s")
            glu_ln1(tiles)
            if STOP == "ln1":
                return finish([f"X{ti}" for ti in range(NTL)], [(dbg[ti * 128:(ti + 1) * 128, :], X[:, ti * D:(ti + 1) * D]) for ti in range(NTL)])
            r = peer_layer(0, 2, tiles)
            if isinstance(r, tuple):
                return r
            if STOP == "ln2":
                return finish([f"X{ti}" for ti in range(NTL)], [(dbg[ti * 128:(ti + 1) * 128, :], X[:, ti * D:(ti + 1) * D]) for ti in range(NTL)])
            kv_phase(tiles, g)
        SC = 0.125
        tiles1 = list(range(NTL))
        idxo = sb([128, 16], I32, name="idxo"); load(idxo[:], idxown_d, "idxo")
        ckown = sb([128, NTL * 16], name="ckown")
        P.dma(Q, X[:, 0:D], x2_all[0:128, :], reads=["x2_all"], writes=["X0"])
        P.dma(Q, ckown[:, 0:16], ck_all[0:128, :], reads=["ck_all"], writes=["ckown"])
        for ti in range(1, NTL):
            P.dma_fn(G, (lambda e, ti=ti: e.indirect_dma_start(out=X[:, ti * D:(ti + 1) * D], out_offset=None, in_=x2_all[:, :], in_offset=bass.IndirectOffsetOnAxis(ap=idxo[:, ti - 1:ti], axis=0))), reads=["x2_all", "idxo"], writes=[f"X{ti}"], key="gx")
            P.dma_fn(G, (lambda e, ti=ti: e.indirect_dma_start(out=ckown[:, ti * 16:(ti + 1) * 16], out_offset=None, in_=ck_all[:, :], in_offset=bass.IndirectOffsetOnAxis(ap=idxo[:, ti - 1:ti], axis=0))), reads=["ck_all", "idxo"], writes=["ckown"], key="gc")
        for ti in tiles1:
            transpose_tile(ti, False)
        pVT = st.enter_context(nc.psum_tensor("p_vt", [128, 1024], BF16))
        identb2 = sb([128, 128], BF16, name="identb2"); act(identb2[:], ident[:], AF.Copy, ["ident"], ["identb2"])
        NB = 1 + 8 * 16
        P.barrier()
        Obuf = X[:].bitcast(BF16)
        P.op(V, lambda e: e.memset(Obuf[:, 0:D], 0.0), writes=["Obuf"])
        with contextlib.ExitStack() as s1:
            def tl(shape, dt=F32, name=None):
                return sb(shape, dt, name=name, stack=s1)
            ckK = tl([128, NB * 16], name="ckK")
            ckK3 = ckK[:].rearrange("p (b h) -> p b h", h=16)
            P.dma(Q, ckK3[:, 0:1, :], ck_all[0:128, :].rearrange("(b p) h -> p b h", p=128), reads=["ck_all"], writes=["ckK"])
            for g_ in range(8):
                for q4 in range(4):
                    b0_ = 1 + g_ * 16 + q4 * 4
                    P.dma(Q, ckK3[:, b0_:b0_ + 4, :], ck_all[b0_ * 128:(b0_ + 4) * 128, :].rearrange("(b p) h -> p b h", p=128), reads=["ck_all"], writes=["ckK"])
            pen = tl([128, 9], name="pen"); load(pen[:], pen_d, "pen")
            amask = tl([128, 2048], BF16, name="amask")
            cqh = tl([16, TOK], BF16, name="cqh"); cql = tl([16, TOK], BF16, name="cql")
            crefb = tl([128, 4 * 16], name="crefb")
            crefS = tl([128, 2 * 16], name="crefS")
            ckC = tl([128, 2 * 16 * 16], name="ckC")
            cknew = tl([128, 2 * 16], name="cknew")
            identb = tl([128, 128], BF16, name="identb"); act(identb[:], ident[:], AF.Copy, ["ident"], ["identb"])
            s2 = contextlib.ExitStack()
            def tl2(shape, dt=F32, name=None):
                return sb(shape, dt, name=name, stack=s2)
            amf = tl2([128, 2048], name="amf")
            load(amf[:], amask_d, "amf"); act(amask[:], amf[:], AF.Copy, ["amf"], ["amask"])
            for st_ in range(4):
                tl_ = 4 * st_ + 4
                P.dma(Q, cref_d[st_:st_ + 1, :], ckown[127:128, tl_ * 16:(tl_ + 1) * 16], reads=["ckown"], writes=["cref_d"], key="crefw")
            for st_ in range(4):
                P.dma(Q, crefb[:, st_ * 16:(st_ + 1) * 16], cref_d[st_:st_ + 1, :].partition_broadcast(128), reads=["cref_d"], writes=["crefb"])
            lfc = [tl2([128, 16], name=f"lfc{i}") for i in range(2)]
            srun = tl2([128, 16], name="srun")
            nn_ = 0
            for s_ in range(2):
                P.op(V, lambda e: e.memset(srun[:], 0.0), writes=["srun"])
                for blk in range(16):
                    b = nn_ % 2; nn_ += 1
                    P.dma(Q, lfc[b][:], clf_d[s_ * 2048 + blk * 128:s_ * 2048 + (blk + 1) * 128, :], writes=[f"lfc{b}"])
                    P.op(T, (lambda e, b=b: e.matmul(pA[4][:, 0:16], utm[:], lfc[b][:], start=True, stop=True)), reads=["utm", f"lfc{b}"], writes=["pA4"])
                    P.op(T, (lambda e, b=b: e.matmul(pA[4][:, 16:32], onesm[:], lfc[b][:], start=True, stop=True)), reads=["onesm", f"lfc{b}"], writes=["pA4"])
                    o0 = (s_ * 16 + blk) * 16
                    tt(ckC[:, o0:o0 + 16], pA[4][:, 0:16], srun[:], ALU.add, ["srun"], ["ckC", "pA4"])
                    tt(srun[:], pA[4][:, 16:32], srun[:], ALU.add, ["srun"], ["srun", "pA4"])
                b = nn_ % 2; nn_ += 1
                P.op(V, (lambda e, b=b: e.memset(lfc[b][:], 0.0)), writes=[f"lfc{b}"])
                P.dma(Q, lfc[b][0:16, :], lf_all[16 + 16 * s_:32 + 16 * s_, :], reads=["lf_all"], writes=[f"lfc{b}"])
                P.op(T, (lambda e, b=b: e.matmul(pA[4][:, 0:16], utm[:], lfc[b][:], start=True, stop=True)), reads=["utm", f"lfc{b}"], writes=["pA4"])
                tt(cknew[:, s_ * 16:(s_ + 1) * 16], pA[4][:, 0:16], srun[:], ALU.add, ["srun"], ["cknew", "pA4"])
                P.dma(Q, ckn_d[16 * s_:16 * s_ + 16, :], cknew[0:16, s_ * 16:(s_ + 1) * 16], reads=["cknew"], writes=["ckn_d"], key="cknw")
                P.dma(Q, cref_d[4 + s_:5 + s_, :], cknew[15:16, s_ * 16:(s_ + 1) * 16], reads=["cknew"], writes=["cref_d"], key="crefw")
            P.dma(Q, ckown[16:48, 0:16], ckn_d[:, :], reads=["ckn_d"], writes=["ckown"])
            for s_ in range(2):
                P.dma(Q, crefS[:, s_ * 16:(s_ + 1) * 16], cref_d[4 + s_:5 + s_, :].partition_broadcast(128), reads=["cref_d"], writes=["crefS"])
            ckT = tl2([16, TOK], name="ckT"); cqs = tl2([16, TOK], name="cqs"); cqt = tl2([16, TOK], name="cqt")
            for ti in tiles1:
                P.op(T, (lambda e, ti=ti: e.transpose(pA[0][0:16, 0:128], ckown[:, ti * 16:(ti + 1) * 16], ident[:])), reads=["ckown", "ident"], writes=["pA0"])
                P.op(V, (lambda e, ti=ti: e.tensor_copy(ckT[:, ti * 128:(ti + 1) * 128], pA[0][0:16, 0:128])), reads=[], writes=["ckT", "pA0"])
            P.op(V, lambda e: e.memset(cqs[:], 0.0), writes=["cqs"])
            for st_ in range(4):
                c0 = 128 + st_ * 512
                ts(cqs[:, c0:c0 + 512], ckT[:, c0:c0 + 512], ckT[:, c0 + 511:c0 + 512], 1.0 / SC, ALU.subtract, ALU.mult, ["ckT"], ["cqs"])
            for s_ in range(2):
                c0 = 16 + 16 * s_
                ts(cqs[:, c0:c0 + 16], ckT[:, c0:c0 + 16], ckT[:, c0 + 15:c0 + 16], 1.0 / SC, ALU.subtract, ALU.mult, ["ckT"], ["cqs"])
            P.op(V, lambda e: e.tensor_copy(cqh[:], cqs[:]), reads=["cqs"], writes=["cqh"])
            P.op(V, lambda e: e.tensor_copy(cqt[:], cqh[:]), reads=["cqh"], writes=["cqt"])
            tt(cqt[:], cqs[:], cqt[:], ALU.subtract, ["cqs", "cqt"], ["cqt"])
            P.op(V, lambda e: e.tensor_copy(cql[:], cqt[:]), reads=["cqt"], writes=["cql"])
            P.barrier()
            s2.close()
            wst = tl([128, 8 * 64], name="awst")
            wqb = tl([128, 8 * 64], BF16, name="awqb"); wkb = tl([128, 8 * 64], BF16, name="awkb"); wvb = tl([128, 8 * 64], BF16, name="awvb")
            QA = tl([128, TOK], BF16, name="QA")
            KAo = tl([128, 2048], BF16, name="KAo")
            VAo = tl([128, 16 * 65], BF16, name="VAo")
            KA = [tl([128, 2048], BF16, name=f"KA{i}") for i in range(2)]
            VA = [tl([128, 16 * 65], BF16, name=f"VA{i}") for i in range(2)]
            VT = [tl([64, 2048], BF16, name=f"VT{i}") for i in range(2)]

            PT = [tl([128, 512], BF16, name=f"PT{i}") for i in range(2)]
            bia = tl([128, 16], name="bia")
            kcf = tl([64, 2048], name="kcf"); vcf = tl([128, 16 * 64], name="vcf")
            KN = tl([128, 16], BF16, name="KN"); VNT = tl([64, 16], BF16, name="VNT"); VN = tl([128, 65], BF16, name="VN")
            osm = tl([16, 64], BF16, name="osm"); rds = tl([16, 1], name="rds")
            P.op(V, lambda e: e.memset(KN[64:128, :], 0.0), writes=["KN"])
            P.op(V, lambda e: e.memset(KN[64:66, :], 1.0), writes=["KN"])
            P.op(V, lambda e: e.memset(VN[:], 1.0), writes=["VN"])
            accS = tl([128, 16 * 65], name="accS")
            rden = tl([128, 16], name="rden")
            for t_ in (KAo, KA[0], KA[1]):
                P.op(V, (lambda e, t_=t_: e.memset(t_[64:128, :], 0.0)), writes=["KAo", "KA0", "KA1"])
                P.op(V, (lambda e, t_=t_: e.memset(t_[64:66, :], 1.0)), writes=["KAo", "KA0", "KA1"])
            for t_ in (VAo, VA[0], VA[1]):
                P.op(V, (lambda e, t_=t_: e.memset(t_[:], 1.0)), writes=["VAo", "VA0", "VA1"])
            P.op(V, lambda e: e.memset(QA[:], 0.0), writes=["QA"])
            wqv = wq_d.rearrange("p (dc n) -> p dc n", dc=8); wkv = wkvf_d.rearrange("p (dc n) -> p dc n", dc=8)
            wst3 = wst[:].rearrange("p (dc n) -> p dc n", dc=8)
            NH = int(os.environ.get("MK_NH", "16"))
            stepc = [0]

            def attn_step(kt, kcols, nk, qcols, nq, vt, vblk, bias_ap, accs, first, last, maskj=None, rd=()):
                b = stepc[0] % 2; stepc[0] += 1
                psn = f"pA{1 + b}"; ps_ = pA[1 + b]
                P.op(T, (lambda e: e.matmul(ps_[0:nk, 0:nq], kt[0:66, kcols:kcols + nk], QA[0:66, qcols:qcols + nq], start=True, stop=(maskj is None))), reads=list(rd) + ["QA"], writes=[psn])
                if maskj is not None:
                    P.op(T, (lambda e: e.matmul(ps_[0:nk, 0:nq], identb[0:nk, 0:nk], amask[0:nk, maskj * 512:maskj * 512 + nq], start=False, stop=True)), reads=["identb", "amask"], writes=[psn])
                act(PT[b][0:nk, 0:nq], ps_[0:nk, 0:nq], AF.Exp, ["bia"], [f"PT{b}", psn], bias=bias_ap, scale=SC)
                for qb in range((nq + 127) // 128):
                    w = min(128, nq - qb * 128)
                    pacc, paccn = accs[qb]
                    P.op(T, (lambda e, qb=qb, w=w, pacc=pacc: e.matmul(pacc[0:w, 0:65], PT[b][0:nk, qb * 128:qb * 128 + w], vt[0:nk, vblk * 65:(vblk + 1) * 65], start=first, stop=last)), reads=[f"PT{b}"] + list(rd), writes=[paccn])

            accs4 = [(pA[3 + i], f"pA{3 + i}") for i in range(4)]
            for h in range(NH):
                dcq = h // 2; ro = 64 * (h % 2)
                for (wsrc, c0, dstb, dn) in ((wqv, h * 64, wqb, "awqb"), (wkv, h * 64, wkb, "awkb"), (wkv, 1024 + h * 64, wvb, "awvb")):
                    P.dma(Q, wst3, wsrc[:, :, c0:c0 + 64], writes=["awst"])
                    act(dstb[:], wst[:], AF.Copy, ["awst"], [dn])
                for (t0, nt) in [(0, 4), (4, 4), (8, 4), (12, 4), (16, 1)]:
                    for dc in range(8):
                        P.op(T, (lambda e, dc=dc, t0=t0, nt=nt: e.matmul(pA[1][0:64, 0:nt * 128], wqb[:, dc * 64:(dc + 1) * 64], XT[:, dc * TOK + t0 * 128:dc * TOK + (t0 + nt) * 128], start=(dc == 0), stop=(dc == 7))), reads=["awqb"] + [f"XT{t0 + i}" for i in range(nt)], writes=["pA1"])
                    act(QA[0:64, t0 * 128:(t0 + nt) * 128], pA[1][0:64, 0:nt * 128], AF.Copy, [], ["QA", "pA1"])
                P.dma(Q, QA[64:65, :], cqh[h:h + 1, :], reads=["cqh"], writes=["QA"], key="qa64")
                P.dma(Q, QA[65:66, :], cql[h:h + 1, :], reads=["cql"], writes=["QA"], key="qa65")
                for (t0, nt) in [(1, 4), (5, 4), (9, 4), (13, 4)]:
                    for dc in range(8):
                        P.op(T, (lambda e, dc=dc, t0=t0, nt=nt: e.matmul(pA[2][0:64, 0:nt * 128], wkb[:, dc * 64:(dc + 1) * 64], XT[:, dc * TOK + t0 * 128:dc * TOK + (t0 + nt) * 128], start=(dc == 0), stop=(dc == 7))), reads=["awkb"] + [f"XT{t0 + i}" for i in range(nt)], writes=["pA2"])
                    act(KAo[0:64, (t0 - 1) * 128:(t0 - 1 + nt) * 128], pA[2][0:64, 0:nt * 128], AF.Copy, [], ["KAo", "pA2"])
                for ti in range(1, NTL):
                    for dc in range(8):
                        P.op(T, (lambda e, dc=dc, ti=ti: e.matmul(pA[2][:, 0:64], XT[:, dc * TOK + ti * 128:dc * TOK + (ti + 1) * 128], wvb[:, dc * 64:(dc + 1) * 64], start=(dc == 0), stop=(dc == 7))), reads=["awvb", f"XT{ti}"], writes=["pA2"])
                    act(VAo[:, (ti - 1) * 65:(ti - 1) * 65 + 64], pA[2][:, 0:64], AF.Copy, [], ["VAo", "pA2"])
                ngrp = 0
                for gk in [-1] + list(range(NG)) + [8]:
                    if gk == 8:
                        kt_, vt_, rdn = KAo, VAo, ["KAo", "VAo"]
                    else:
                        bb = ngrp % 2; ngrp += 1
                        kt_, vt_, rdn = KA[bb], VA[bb], [f"KA{bb}", f"VA{bb}"]
                        if gk == -1:
                            nblk = 1; r0 = 0; nkeys = 128
                        else:
                            nblk = 16; r0 = 128 + gk * 2048; nkeys = 2048
                        P.dma(Q, kt_[0:64, 0:nkeys], kT_all[dcq * 128 + ro:dcq * 128 + ro + 64, r0:r0 + nkeys], reads=["kT_all"], writes=[f"KA{bb}"])
                        P.dma(Q, VT[bb][0:64, 0:nkeys], vT_all[dcq * 128 + ro:dcq * 128 + ro + 64, r0:r0 + nkeys], reads=["kT_all"], writes=[f"VT{bb}"])
                        for kb in range(nblk):
                            P.op(T, (lambda e, kb=kb, bb=bb: e.transpose(pVT[:, kb * 64:(kb + 1) * 64], VT[bb][0:64, kb * 128:(kb + 1) * 128], identb[0:64, 0:64])), reads=[f"VT{bb}", "identb"], writes=["pA7"])
                        P.op(V, (lambda e, vt_=vt_, nblk=nblk: e.tensor_copy(vt_[:].rearrange("p (b c) -> p b c", c=65)[:, 0:nblk, 0:64], pVT[:, 0:nblk * 64].rearrange("p (b c) -> p b c", c=64))), reads=[], writes=[f"VA{bb}", "pA7"])
                    for st_ in range(4):
                        qc = 128 + st_ * 512
                        crs = crefb[:, st_ * 16 + h:st_ * 16 + h + 1]
                        if gk == 8:
                            nblk = 4 * st_ + 4
                            ts(bia[:, 0:nblk], ckown[:].rearrange("p (t h) -> p t h", h=16)[:, 1:1 + nblk, h], crs, -1.0, ALU.subtract, ALU.mult, ["ckown", "crefb"], ["bia"])
                        else:
                            b0 = 0 if gk == -1 else 1 + gk * 16
                            ts(bia[:, 0:nblk], ckK[:].rearrange("p (b h) -> p b h", h=16)[:, b0:b0 + nblk, h], crs, -1.0, ALU.subtract, ALU.mult, ["ckK", "crefb"], ["bia"])
                            if gk >= 0:
                                ts(bia[:, 0:nblk], bia[:, 0:nblk], pen[:, gk:gk + 1], None, ALU.add, None, ["bia", "pen"], ["bia"])
                        for kb in range(nblk):
                            nk = 16 if gk == -1 else 128
                            mj = None
                            if gk == 8 and kb >= 4 * st_:
                                mj = kb - 4 * st_
                            attn_step(kt_, kb * 128, nk, qc, 512, vt_, kb, bia[0:nk, kb:kb + 1], accs4, kb == 0, kb == nblk - 1, maskj=mj, rd=rdn)
                        for qb in range(4):
                            pacc, paccn = accs4[qb]
                            col = (st_ * 4 + qb) * 65
                            if gk == -1:
                                P.op(V, (lambda e, pacc=pacc, col=col: e.tensor_copy(accS[:, col:col + 65], pacc[:, 0:65])), reads=[], writes=["accS", paccn])
                            else:
                                tt(accS[:, col:col + 65], accS[:, col:col + 65], pacc[:, 0:65], ALU.add, [], ["accS", paccn])
                for s_ in range(2):
                    qc = 16 + 16 * s_
                    crs = crefS[:, s_ * 16 + h:s_ * 16 + h + 1]
                    P.dma(Q, kcf[:], ckT_d[(s_ * 16 + h) * 64:(s_ * 16 + h + 1) * 64, :], writes=["kcf"])
                    act(KA[0][0:64, :], kcf[:], AF.Copy, ["kcf"], ["KA0"])
                    for q4 in range(4):
                        P.dma(Q, vcf[:].rearrange("p (b c) -> p b c", c=64)[:, q4 * 4:(q4 + 1) * 4, :], cv_d[s_ * 2048 + q4 * 512:s_ * 2048 + (q4 + 1) * 512, h * 64:(h + 1) * 64].rearrange("(b p) c -> p b c", p=128), writes=["vcf"])
                    P.op(V, lambda e: e.tensor_copy(VA[0][:].rearrange("p (b c) -> p b c", c=65)[:, :, 0:64], vcf[:].rearrange("p (b c) -> p b c", c=64)), reads=["vcf"], writes=["VA0"])
                    ts(bia[:, 0:16], ckC[:].rearrange("p (s b h) -> p s b h", s=2, b=16)[:, s_, :, h], crs, -1.0, ALU.subtract, ALU.mult, ["ckC", "crefS"], ["bia"])
                    acc1 = [(pA[3], "pA3")]
                    for kb in range(16):
                        attn_step(KA[0], kb * 128, 128, qc, 16, VA[0], kb, bia[:, kb:kb + 1], acc1, kb == 0, False, rd=["KA0", "VA0"])
                    P.dma(Q, KN[0:64, :], kT_all[dcq * 128 + ro:dcq * 128 + ro + 64, qc:qc + 16], reads=["kT_all"], writes=["KN"])
                    P.dma(Q, VNT[:, :], vT_all[dcq * 128 + ro:dcq * 128 + ro + 64, qc:qc + 16], reads=["kT_all"], writes=["VNT"])
                    P.op(T, lambda e: e.transpose(pVT[0:16, 0:64], VNT[0:64, 0:16], identb[0:64, 0:64]), reads=["VNT", "identb"], writes=["pA7"])
                    P.op(V, lambda e: e.tensor_copy(VN[0:16, 0:64], pVT[0:16, 0:64]), reads=[], writes=["VN", "pA7"])
                    ts(bia[0:16, 0:1], cknew[0:16, s_ * 16 + h:s_ * 16 + h + 1], crs[0:16, :], -1.0, ALU.subtract, ALU.mult, ["cknew", "crefS"], ["bia"])
                    attn_step(KN, 0, 16, qc, 16, VN, 0, bia[0:16, 0:1], acc1, False, True, maskj=0, rd=["KN", "VN"])
                    P.op(V, lambda e: e.reciprocal(rds[:], pA[3][0:16, 64:65]), reads=[], writes=["rds", "pA3"])
                    ts(osm[:], pA[3][0:16, 0:64], rds[:, 0:1], None, ALU.mult, None, ["rds"], ["osm", "pA3"])
                    P.dma(Q, Obuf[qc:qc + 16, h * 64:(h + 1) * 64], osm[:], reads=["osm"], writes=["Obuf"], key="osmw")
                a3 = accS[:].rearrange("p (t c) -> p t c", c=65)
                P.op(V, lambda e: e.reciprocal(rden[:], a3[:, :, 64]), reads=["accS"], writes=["rden"])
                O3 = Obuf[:, 0:NTL * D].rearrange("p (t d) -> p t d", d=D)
                P.op(V, (lambda e, h=h: e.tensor_tensor(out=O3[:, 1:NTL, h * 64:(h + 1) * 64], in0=a3[:, :, 0:64], in1=rden[:, :, None].to_broadcast([128, 16, 64]), op=ALU.mult)), reads=["accS", "rden"], writes=["Obuf"])
            P.barrier()
        for ti in tiles1:
            for half in range(2):
                for k in range(4):
                    dc = half * 4 + k
                    P.op(T, (lambda e, dc=dc, k=k, ti=ti: e.transpose(pVT[:, k * 128:(k + 1) * 128], Obuf[:, ti * D + dc * 128:ti * D + (dc + 1) * 128], identb2[:])), reads=["Obuf", "identb2"], writes=["pA7"])
                for k in range(4):
                    dc = half * 4 + k
                    act(XT[:, dc * TOK + ti * 128:dc * TOK + (ti + 1) * 128], pVT[:, k * 128:(k + 1) * 128], AF.Copy, [], [f"XT{ti}", "pA7"])
        P.barrier()
        P.dma(Q, X[:, 0:D], x2_all[0:128, :], reads=["x2_all"], writes=["X0"])
        for ti in range(1, NTL):
            P.dma_fn(G, (lambda e, ti=ti: e.indirect_dma_start(out=X[:, ti * D:(ti + 1) * D], out_offset=None, in_=x2_all[:, :], in_offset=bass.IndirectOffsetOnAxis(ap=idxo[:, ti - 1:ti], axis=0))), reads=["x2_all", "idxo"], writes=[f"X{ti}"], key="gx")
        with contextlib.ExitStack() as s1:
            wof = sb([128, 8 * 512], name="wof", stack=s1); wob = sb([128, 8 * 512], BF16, name="wob", stack=s1)
            load_ln(4, 5)
            wov = wo_d.rearrange("p (dc n) -> p dc n", dc=8)
            for cb in range(2):
                P.dma(Q, wof[:].rearrange("p (dc n) -> p dc n", dc=8), wov[:, :, cb * 512:(cb + 1) * 512], writes=["wof"])
                act(wob[:], wof[:], AF.Copy, ["wof"], ["wob"])
                for ti in tiles1:
                    for dc in range(8):
                        P.op(T, (lambda e, dc=dc, ti=ti: e.matmul(pA[5][:], XT[:, dc * TOK + ti * 128:dc * TOK + (ti + 1) * 128], wob[:, dc * 512:(dc + 1) * 512], start=(dc == 0), stop=(dc == 7))), reads=[f"XT{ti}", "wob"], writes=["pA5"])
                    xs_ = X[:, ti * D + cb * 512:ti * D + (cb + 1) * 512]
                    P.op(V, (lambda e, xs_=xs_: e.scalar_tensor_tensor(out=xs_, in0=xs_, scalar=DN_ALPHA, in1=pA[5][:], op0=ALU.mult, op1=ALU.add)), reads=[f"X{ti}"], writes=[f"X{ti}", "pA5"])
            for ti in tiles1:
                layer_norm_tile(ti)
            P.barrier()
        if STOP == "l1ln1":
            return finish([f"X{ti}" for ti in range(NTL)], [(dbg[ti * 128:(ti + 1) * 128, :], X[:, ti * D:(ti + 1) * D]) for ti in range(NTL)])
        r = peer_layer(1, 6, tiles1)
        if isinstance(r, tuple):
            return r
        for ti in tiles1:
            P.dma(Q, o_y[ti * 128:(ti + 1) * 128, :], X[:, ti * D:(ti + 1) * D], reads=[f"X{ti}"], key="oy")
        P.finalize()
    return nc, P


def _prep_common(inputs):
    f = np.float32
    lam_re = inputs["ssm_lam_re"][0]; lam_im = inputs["ssm_lam_im"][0]; log_dt = inputs["ssm_log_dt"][0]
    b_re = inputs["ssm_b_re"][0]; b_im = inputs["ssm_b_im"][0]; c_re = inputs["ssm_c_re"][0]; c_im = inputs["ssm_c_im"][0]

    def L2(a):
        return np.ascontiguousarray(a.reshape(32, 2, 64).transpose(1, 2, 0).reshape(128, 32)).astype(f)

    def L1(a):
        t = a.reshape(8, 8, 64)
        t = np.broadcast_to(t[:, :, None, :], (8, 8, 16, 64))
        return np.ascontiguousarray(t.transpose(1, 2, 0, 3).reshape(128, 512)).astype(f)

    def L1b(a):
        t = a.reshape(8, 8, 64, 16)
        return np.ascontiguousarray(t.transpose(1, 3, 0, 2).reshape(128, 512)).astype(f)

    def L2c(a):
        t = a.reshape(32, 2, 16, 64)
        return np.ascontiguousarray(t.transpose(1, 3, 0, 2).reshape(128, 512)).astype(f)

    def wfm(w):
        n = w.shape[1]
        return np.ascontiguousarray(w.reshape(8, 128, n).transpose(1, 0, 2).reshape(128, 8 * n)).astype(f)

    ldt = np.broadcast_to(log_dt[:, None], (64, 64))
    rowmask = np.zeros((128, 8), f)
    for gi in range(8):
        rowmask[gi * 16:(gi + 1) * 16, gi] = 1.0
    com = {
        "ident": np.eye(128, dtype=f),
        "tvec": np.ascontiguousarray(np.broadcast_to(np.arange(1, 129, dtype=f)[None, :], (128, 128))),
        "iota128": np.ascontiguousarray(np.broadcast_to(np.arange(0, 128, dtype=f)[None, :], (128, 128))),
        "rowmask": rowmask,
        "lamre2": L2(lam_re), "lamim2": L2(lam_im), "logdt2": L2(ldt),
        "lamre1": L1(lam_re), "lamim1": L1(lam_im), "logdt1": L1(ldt),
        "bre1": L1b(b_re), "bim1": L1b(b_im), "cre2": L2c(c_re), "cim2": L2c(c_im),
        "dfm": np.ascontiguousarray(inputs["ssm_d"][0].reshape(8, 128).T).astype(f),
        "wglu": wfm(inputs["w_glu"][0]),
        "lnv": np.ascontiguousarray(np.stack([inputs["ln1_g"][0], inputs["ln1_b"][0], inputs["ln2_g"][0], inputs["ln2_b"][0],
                                              inputs["ln1_g"][1], inputs["ln1_b"][1], inputs["ln2_g"][1], inputs["ln2_b"][1]], 0)).astype(f),
        "wkvf": wfm(inputs["w_kvf"]),
        "wq": wfm(inputs["w_q"][0]), "wo": wfm(inputs["w_o"][0]),
        "utm": np.triu(np.ones((128, 128), f)),
        "onesm": np.ones((128, 128), f),
        "mask16": (np.arange(128) < 16).astype(f)[:, None].copy(),
        "bf": np.ascontiguousarray(inputs["b_f"][None, :]).astype(f),
    }
    am = np.zeros((128, 4, 512), f)
    kk = np.arange(128)[:, None]; qq = np.arange(512)[None, :]
    for j in range(4):
        am[:, j, :] = np.where(128 * j + kk > qq, -30000.0, 0.0)
    com["amask"] = np.ascontiguousarray(am.reshape(128, 2048))
    com["_xmain"] = np.ascontiguousarray(inputs["x_prompt"][0]).astype(f)
    for l in range(2):
        ut = inputs["peer_u"][l].reshape(128, 128, 8, 128).transpose(0, 3, 2, 1)
        com[f"ufull{l}"] = np.ascontiguousarray(ut.reshape(128 * 128, D)).astype(f)
        com[f"vfull{l}"] = np.ascontiguousarray(inputs["peer_v"][l]).astype(f)
        if os.environ.get("MK_SMALLTB", "") == "1":
            com[f"ufull{l}"] = com[f"ufull{l}"][:128]; com[f"vfull{l}"] = com[f"vfull{l}"][:128]
        com[f"wpq{l}"] = wfm(inputs["peer_w_q"][l])
        com[f"pk{l}"] = np.ascontiguousarray(np.concatenate([inputs["peer_k1"][l].T, inputs["peer_k2"][l].T], 1)).astype(f)
    return com


def _prep_core(inputs, com, c):
    f = np.float32
    m = dict(com)
    xs = np.zeros((128, D), f)
    xs[0:16] = inputs["meta_tokens"]
    xs[16:32] = inputs["x_sample"][2 * c]
    xs[32:48] = inputs["x_sample"][2 * c + 1]
    m["xmisc"] = xs
    pen = np.zeros((128, 9), f)
    for g in range(8):
        if g >= c:
            pen[:, g] = NEG
    m["pen"] = pen
    m["idxown"] = np.ascontiguousarray((128 + 2048 * c + np.arange(16)[None, :] * 128 + np.arange(128)[:, None]).astype(np.int32))
    ck = inputs["cache_k"][2 * c:2 * c + 2]
    m["cache_kT"] = np.ascontiguousarray(ck.transpose(0, 2, 3, 1).reshape(2 * 16 * 64, 2048)).astype(f)
    m["cache_v"] = np.ascontiguousarray(inputs["cache_v"][2 * c:2 * c + 2].reshape(2 * 2048, D)).astype(f)
    m["cache_lf"] = np.ascontiguousarray(inputs["cache_logf"][2 * c:2 * c + 2].reshape(2 * 2048, 16)).astype(f)
    m["xmain"] = com["_xmain"]

    def stl(a):
        t = a.reshape(2, 32, 2, 64)
        return np.ascontiguousarray(t.transpose(2, 3, 0, 1).reshape(128, 64)).astype(f)
    m["st_re"] = stl(inputs["state_ssm_re"][2 * c:2 * c + 2, 0])
    m["st_im"] = stl(inputs["state_ssm_im"][2 * c:2 * c + 2, 0])
    return m


def kernel(**inputs):
    f = np.float32
    inputs = {k: np.asarray(v) for k, v in inputs.items()}
    com = _prep_common(inputs)
    in_maps = [_prep_core(inputs, com, c) for c in range(NCORES)]
    for m in in_maps:
        m.pop("_xmain", None)
    nc, P = build_program()
    res = run_bass_kernel_spmd(nc, in_maps, core_ids=list(range(NCORES)))
    R = res.results
    return assemble(R)


def assemble(R):
    f = np.float32

    def unL2(a):
        return a.reshape(2, 64, 32).transpose(2, 0, 1).reshape(64, 64)
    y_prompt = np.zeros((1, 16384, D), f); y_sample = np.zeros((16, 16, D), f)
    sp = R[0]["o_ssm_p"]
    ssm_re_p = unL2(sp[:, 0:32])[None, None].astype(f); ssm_im_p = unL2(sp[:, 32:64])[None, None].astype(f)
    k_p = np.zeros((1, 16400, 16, 64), f); v_p = np.zeros((1, 16400, 16, 64), f); lf_p = np.zeros((1, 16400, 16), f)
    ssm_re_s = np.zeros((16, 1, 64, 64), f); ssm_im_s = np.zeros((16, 1, 64, 64), f)
    k_s = np.zeros((16, 16, 16, 64), f); v_s = np.zeros((16, 16, 16, 64), f); lf_s = np.zeros((16, 16, 16), f)
    ok = R[0]["o_k"]; ov = R[0]["o_v"]; olf = R[0]["o_lf"]
    k_p[0, 0:16] = ok[0:16].reshape(16, 16, 64); v_p[0, 0:16] = ov[0:16].reshape(16, 16, 64); lf_p[0, 0:16] = olf[0:16]
    k_p[0, 16:] = ok[128:].reshape(16384, 16, 64); v_p[0, 16:] = ov[128:].reshape(16384, 16, 64); lf_p[0, 16:] = olf[128:]
    for c in range(NCORES):
        ss = R[c]["o_ssm_s"]
        for s in range(2):
            ssm_re_s[2 * c + s, 0] = unL2(ss[:, s * 64:s * 64 + 32])
            ssm_im_s[2 * c + s, 0] = unL2(ss[:, s * 64 + 32:s * 64 + 64])
        okc = R[c]["o_k"]; ovc = R[c]["o_v"]; olfc = R[c]["o_lf"]; oy = R[c]["o_y"]
        y_prompt[0, 2048 * c:2048 * (c + 1)] = oy[128:]
        for s in range(2):
            r0 = 16 + 16 * s
            k_s[2 * c + s] = okc[r0:r0 + 16].reshape(16, 16, 64); v_s[2 * c + s] = ovc[r0:r0 + 16].reshape(16, 16, 64)
            lf_s[2 * c + s] = olfc[r0:r0 + 16]; y_sample[2 * c + s] = oy[r0:r0 + 16]
    return (y_prompt, y_sample, ssm_re_p, ssm_im_p, k_p, v_p, lf_p, ssm_re_s, ssm_im_s, k_s, v_s, lf_s)
```
